# Optimizing a Trainium2 kernel written in Bass

```python
import math
import jax, jax.numpy as jnp
from jax import lax
import numpy as np

D_MODEL = 1024
BATCH = 1
SEQ = 16384
DEPTH = 1

FOX_HEAD_DIM = 64
FOX_HEADS = (D_MODEL // 2) // FOX_HEAD_DIM
FOX_WIDTH = FOX_HEADS * FOX_HEAD_DIM
RET_HEAD_DIM = 128
RET_HEADS = (D_MODEL - FOX_WIDTH) // RET_HEAD_DIM
RET_WIDTH = RET_HEADS * RET_HEAD_DIM
MIX_WIDTH = FOX_WIDTH + RET_WIDTH
D_FF = 256 * int(math.ceil(8 * D_MODEL / 3 / 256))
CONV_WIDTH = 3
Q_BLOCK = 128
RET_CHUNK = 128
ROPE_BASE = 10000.0
LN_EPS = 1e-5
GN_EPS = 1e-6
ALPHA = (2 * DEPTH) ** 0.25
BETA = (8 * DEPTH) ** -0.25
IN_SIZES = [FOX_WIDTH, FOX_WIDTH, FOX_WIDTH, FOX_HEADS,
            RET_WIDTH, RET_WIDTH, RET_WIDTH, RET_WIDTH]
IN_COLS = sum(IN_SIZES)
IN_SPLITS = [int(v) for v in np.cumsum(IN_SIZES)[:-1]]

kernel_name = "fox_retnet_hymba_deepnorm_adaln_layer"


def layer_norm(x, g, b):
    xf = x.astype(jnp.float32)
    mu = jnp.mean(xf, axis=-1, keepdims=True)
    var = jnp.mean(jnp.square(xf - mu), axis=-1, keepdims=True)
    y = (xf - mu) * lax.rsqrt(var + LN_EPS)
    return (y * g + b).astype(x.dtype)


def forgetting_attention(q, k, v, log_f):
    B, S, H, Dh = q.shape
    nb = S // Q_BLOCK
    scale = Dh ** -0.5
    cum = jnp.cumsum(log_f.astype(jnp.float32), axis=1).transpose(0, 2, 1)
    kh = k.transpose(0, 2, 1, 3)
    vh = v.transpose(0, 2, 1, 3)
    qb = q.reshape(B, nb, Q_BLOCK, H, Dh).transpose(1, 0, 3, 2, 4)
    cb = cum.reshape(B, H, nb, Q_BLOCK).transpose(2, 0, 1, 3)
    key_pos = jnp.arange(S)

    def block(args):
        qi, ci, start = args
        s = jnp.einsum('bhqd,bhkd->bhqk', qi, kh,
                       preferred_element_type=jnp.float32) * scale
        s = s + ci[..., None] - cum[:, :, None, :]
        q_pos = start + jnp.arange(Q_BLOCK)
        s = jnp.where(key_pos[None, :] <= q_pos[:, None], s, -jnp.inf)
        p = jax.nn.softmax(s, axis=-1)
        return jnp.einsum('bhqk,bhkd->bhqd', p.astype(vh.dtype), vh)

    out = lax.map(block, (qb, cb, jnp.arange(nb) * Q_BLOCK))
    return out.transpose(1, 0, 3, 2, 4).reshape(B, S, H * Dh)


def rotate_half(x):
    x1, x2 = jnp.split(x, 2, axis=-1)
    return jnp.concatenate([-x2, x1], axis=-1)


def retention(q, k, v, g):
    B, S, H, dk = q.shape
    dv = v.shape[-1]
    C = RET_CHUNK
    nc = S // C
    dt = q.dtype
    pos = jnp.arange(S, dtype=jnp.float32)
    inv_freq = ROPE_BASE ** (-jnp.arange(0, dk, 2, dtype=jnp.float32) / dk)
    ang = pos[:, None] * inv_freq[None, :]
    ang = jnp.concatenate([ang, ang], axis=-1)
    cos = jnp.cos(ang)[None, :, None, :].astype(dt)
    sin = jnp.sin(ang)[None, :, None, :].astype(dt)
    q = q * cos + rotate_half(q) * sin
    k = (k * cos + rotate_half(k) * sin) * (dk ** -0.5)
    log_gamma = jnp.log1p(-jnp.exp2(-5.0 - jnp.arange(H, dtype=jnp.float32)))
    idx = jnp.arange(C, dtype=jnp.float32)
    diff = idx[:, None] - idx[None, :]
    inner = jnp.where(diff[None] >= 0,
                      jnp.exp(jnp.maximum(diff, 0.0)[None] * log_gamma[:, None, None]),
                      0.0).astype(dt)
    xi = jnp.exp((idx[None, :] + 1.0) * log_gamma[:, None]).astype(dt)
    zeta = jnp.exp((C - 1.0 - idx[None, :]) * log_gamma[:, None]).astype(dt)
    g_chunk = jnp.exp(C * log_gamma).astype(dt)

    def to_chunks(t):
        return t.reshape(B, nc, C, H, t.shape[-1]).transpose(1, 0, 3, 2, 4)

    qc, kc, vc = to_chunks(q), to_chunks(k), to_chunks(v)

    def step(R, xs):
        qi, ki, vi = xs
        s = jnp.einsum('bhnd,bhmd->bhnm', qi, ki) * inner[None]
        o = (jnp.einsum('bhnm,bhmv->bhnv', s, vi)
             + jnp.einsum('bhnd,bhdv->bhnv', qi, R) * xi[None, :, :, None])
        R = (R * g_chunk[None, :, None, None]
             + jnp.einsum('bhmd,bhmv->bhdv', ki * zeta[None, :, :, None], vi))
        return R, o

    R0 = jnp.zeros((B, H, dk, dv), dt)
    _, o = lax.scan(step, R0, (qc, kc, vc))
    o = o.transpose(1, 0, 3, 2, 4).reshape(B, S, H, dv)
    of = o.astype(jnp.float32)
    mu = jnp.mean(of, axis=-1, keepdims=True)
    var = jnp.mean(jnp.square(of - mu), axis=-1, keepdims=True)
    o = ((of - mu) * lax.rsqrt(var + GN_EPS)).astype(dt).reshape(B, S, H * dv)
    return jax.nn.silu(g) * o


def causal_depthwise_conv(u, w, b):
    S = u.shape[1]
    up = jnp.pad(u, ((0, 0), (CONV_WIDTH - 1, 0), (0, 0)))
    y = b
    for i in range(CONV_WIDTH):
        y = y + up[:, i:i + S, :] * w[i]
    return y


def setup_inputs(seed: int = 0) -> dict:
    key = jax.random.key(seed)
    ks = jax.random.split(key, 16)
    f32 = jnp.float32
    x = jax.random.normal(ks[0], (BATCH, SEQ, D_MODEL), f32)
    c = jax.random.normal(ks[1], (BATCH, D_MODEL), f32)
    w_ada = jax.random.normal(ks[2], (DEPTH, D_MODEL, 6 * D_MODEL), f32) * D_MODEL ** -0.5
    b_ada = 0.01 * jax.random.normal(ks[3], (DEPTH, 6 * D_MODEL), f32)
    col_scale = np.ones((IN_COLS,), np.float32)
    fv0 = 2 * FOX_WIDTH
    col_scale[fv0:fv0 + FOX_WIDTH] = BETA
    rv0 = 3 * FOX_WIDTH + FOX_HEADS + 2 * RET_WIDTH
    col_scale[rv0:rv0 + RET_WIDTH] = BETA
    w_in = (jax.random.normal(ks[4], (DEPTH, D_MODEL, IN_COLS), f32)
            * D_MODEL ** -0.5 * jnp.asarray(col_scale))
    b_f = 2.0 + 0.1 * jax.random.normal(ks[5], (DEPTH, FOX_HEADS), f32)
    w_out = jax.random.normal(ks[6], (DEPTH, MIX_WIDTH, D_MODEL), f32) * MIX_WIDTH ** -0.5 * BETA
    ln1_g = 1.0 + 0.02 * jax.random.normal(ks[7], (DEPTH, D_MODEL), f32)
    ln1_b = 0.02 * jax.random.normal(ks[8], (DEPTH, D_MODEL), f32)
    w_up = jax.random.normal(ks[9], (DEPTH, D_MODEL, 2 * D_FF), f32) * D_MODEL ** -0.5 * BETA
    conv_w = jax.random.normal(ks[10], (DEPTH, CONV_WIDTH, 2 * D_FF), f32) * CONV_WIDTH ** -0.5
    conv_b = 0.02 * jax.random.normal(ks[11], (DEPTH, 2 * D_FF), f32)
    w_down = jax.random.normal(ks[12], (DEPTH, D_FF, D_MODEL), f32) * D_FF ** -0.5 * BETA
    ln2_g = 1.0 + 0.02 * jax.random.normal(ks[13], (DEPTH, D_MODEL), f32)
    ln2_b = 0.02 * jax.random.normal(ks[14], (DEPTH, D_MODEL), f32)
    return {"x": x, "c": c, "w_ada": w_ada, "b_ada": b_ada, "w_in": w_in,
            "b_f": b_f, "w_out": w_out, "ln1_g": ln1_g, "ln1_b": ln1_b,
            "w_up": w_up, "conv_w": conv_w, "conv_b": conv_b, "w_down": w_down,
            "ln2_g": ln2_g, "ln2_b": ln2_b}


def reference(x, c, w_ada, b_ada, w_in, b_f, w_out, ln1_g, ln1_b,
              w_up, conv_w, conv_b, w_down, ln2_g, ln2_b):
    B, S, D = x.shape
    for l in range(DEPTH):
        mod = jax.nn.silu(c) @ w_ada[l] + b_ada[l]
        sh1, sc1, g1, sh2, sc2, g2 = jnp.split(mod[:, None, :], 6, axis=-1)
        h = x * (1.0 + sc1) + sh1
        proj = h @ w_in[l]
        fq, fk, fv, ff, rq, rk, rv, rg = jnp.split(proj, IN_SPLITS, axis=-1)
        log_f = jax.nn.log_sigmoid(ff + b_f[l])
        fox = forgetting_attention(
            fq.reshape(B, S, FOX_HEADS, FOX_HEAD_DIM),
            fk.reshape(B, S, FOX_HEADS, FOX_HEAD_DIM),
            fv.reshape(B, S, FOX_HEADS, FOX_HEAD_DIM), log_f)
        ret = retention(
            rq.reshape(B, S, RET_HEADS, RET_HEAD_DIM),
            rk.reshape(B, S, RET_HEADS, RET_HEAD_DIM),
            rv.reshape(B, S, RET_HEADS, RET_HEAD_DIM), rg)
        mix = jnp.concatenate([fox, ret], axis=-1) @ w_out[l]
        x = layer_norm(ALPHA * x + g1 * mix, ln1_g[l], ln1_b[l])
        h = x * (1.0 + sc2) + sh2
        u = causal_depthwise_conv(h @ w_up[l], conv_w[l], conv_b[l])
        a, bv = jnp.split(u, 2, axis=-1)
        y = (jax.nn.gelu(a, approximate=False) * bv) @ w_down[l]
        x = layer_norm(ALPHA * x + g2 * y, ln2_g[l], ln2_b[l])
    return x
```

```python
from contextlib import ExitStack

import numpy as np
import concourse.bass as bass
import concourse.mybir as mybir
from concourse.bass_utils import run_bass_kernel_spmd

F32 = mybir.dt.float32
BF16 = mybir.dt.bfloat16
ALU = mybir.AluOpType
AF = mybir.ActivationFunctionType
AX = mybir.AxisListType

NCORES = 8
D = 1024
TS = 512
HALO = 32


def _cfg(S_):
    global S, NT, NKB, SLAB, SL
    S = S_
    NT = S // TS
    NKB = S // 128
    SLAB = S // NCORES
    SL = SLAB + HALO


_cfg(16384)
DFF = 2816
NCH = DFF // 128
ALPHA = 2.0 ** 0.25
LN_EPS = 1e-5
GN_EPS = 1e-6
ENGS = ["pe", "act", "dve", "pool", "sp"]


class Buf:
    __slots__ = ("w", "r", "excl")

    def __init__(self, excl=False):
        self.w = None
        self.r = {}
        self.excl = excl


class Prog:
    def __init__(self):
        self.q = {e: [] for e in ENGS}
        self.cnt = {e: 0 for e in ENGS}
        self.dcnt = {}
        self.waited = {e: {} for e in ENGS}
        self.fence = []
        self.nops = 0
        self.limit = None
        self.log = []

    def _skip(self):
        self.nops += 1
        import sys as _s
        self.log.append((self.nops, _s._getframe(2).f_lineno))
        return self.limit is not None and self.nops > self.limit

    def _deps(self, eng, reads, writes, extra):
        deps = list(extra) + list(self.fence)
        for b in reads:
            deps.append(b.w)
            if b.excl:
                deps.extend(b.r.items())
        for b in writes:
            deps.append(b.w)
            deps.extend(b.r.items())
        ws = []
        for d in deps:
            if d is None:
                continue
            s, v = d
            if eng == "pe" and s == "e_pe":
                continue
            if self.waited[eng].get(s, 0) >= v:
                continue
            self.waited[eng][s] = v
            ws.append((s, v))
        return ws

    def _commit(self, h, reads, writes):
        for b in reads:
            if b.excl:
                b.w = h
                b.r = {}
            elif b.r.get(h[0], 0) < h[1]:
                b.r[h[0]] = h[1]
        for b in writes:
            b.w = h
            b.r = {}

    def op(self, eng, fn, reads=(), writes=(), extra=(), signal=True):
        if self._skip():
            return None
        ws = self._deps(eng, reads, writes, extra)
        if signal:
            self.cnt[eng] += 1
            h = ("e_" + eng, self.cnt[eng])
            self.q[eng].append((ws, fn, ("e_" + eng, 1)))
            self._commit(h, reads, writes)
            return h
        self.q[eng].append((ws, fn, None))
        return None

    def dma(self, eng, fn, sem, reads=(), writes=(), extra=()):
        if self._skip():
            return None
        ws = self._deps(eng, reads, writes, extra)
        self.dcnt[sem] = self.dcnt.get(sem, 0) + 16
        h = (sem, self.dcnt[sem])
        self.q[eng].append((ws, fn, (sem, 16)))
        self._commit(h, reads, writes)
        return h

    def coll(self, eng, fn, sem, reads=(), writes=()):
        if self._skip():
            return None
        ws = self._deps(eng, reads, writes, ())
        self.dcnt[sem] = self.dcnt.get(sem, 0) + 1
        h = (sem, self.dcnt[sem])
        self.q[eng].append((ws, fn, (sem, 1)))
        self._commit(h, reads, writes)
        return h

    def batch_done(self, sem, bufs):
        if sem not in self.dcnt:
            return
        h = (sem, self.dcnt[sem])
        for b in bufs:
            b.w = h

    def set_fence(self):
        f = [("e_" + e, self.cnt[e]) for e in ENGS if self.cnt[e] > 0]
        f += [(s, v) for s, v in self.dcnt.items()]
        self.fence = f

    def sem_names(self):
        return ["e_" + e for e in ENGS] + sorted(self.dcnt.keys())

    def replay(self, eng, engine_obj, sems, final_waits=()):
        for ws, fn, inc in self.q[eng]:
            for s, v in ws:
                engine_obj.wait_ge(sems[s], v)
            inst = fn(engine_obj)
            if inc is not None:
                inst.then_inc(sems[inc[0]], inc[1])
        for s, v in final_waits:
            engine_obj.wait_ge(sems[s], v)


def TSC(out, in0, s1, s2, op0, op1=None):
    if op1 is None:
        return lambda e: e.tensor_scalar(out=out, in0=in0, scalar1=s1, scalar2=None, op0=op0)
    return lambda e: e.tensor_scalar(out=out, in0=in0, scalar1=s1, scalar2=s2, op0=op0, op1=op1)


def TT(out, in0, in1, op):
    return lambda e: e.tensor_tensor(out=out, in0=in0, in1=in1, op=op)


def STT(out, in0, scalar, in1, op0, op1):
    return lambda e: e.scalar_tensor_tensor(out=out, in0=in0, scalar=scalar, in1=in1, op0=op0, op1=op1)


def CP(out, in_):
    return lambda e: e.tensor_copy(out=out, in_=in_)


def ACT(out, in_, func, bias=None, scale=None):
    kw = {}
    if bias is not None:
        kw["bias"] = bias
    if scale is not None:
        kw["scale"] = scale
    return lambda e: e.activation(out=out, in_=in_, func=func, **kw)


def MM(out, lhsT, rhs, start, stop):
    return lambda e: e.matmul(out, lhsT=lhsT, rhs=rhs, start=start, stop=stop)


def TR(out, in_, ident):
    return lambda e: e.transpose(out, in_, ident)


def MS(ap, v):
    return lambda e: e.memset(ap, v)


def DMA(out, in_):
    return lambda e: e.dma_start(out=out, in_=in_)


class _Stop(Exception):
    pass


def build_program(stop=None, limit=None):
    nc = bass.Bass("TRN2", target_bir_lowering=False)
    P = Prog()
    P.limit = limit

    def din(name, shape):
        return nc.dram_tensor(name, shape, F32, kind="ExternalInput").ap()

    xT = din("xT", [D, S])
    xs = din("xs", [SL, D])
    c_col = din("c_col", [128, 8])
    w_ada = din("w_ada", [D, 6 * D])
    b_ada = din("b_ada", [1, 6 * D])
    w1f = din("w1f", [D, 512])
    w1t = din("w1t", [D, 193])
    bfh = din("bfh", [128, 1])
    cosT = din("cosT", [128, S])
    sinT = din("sinT", [128, S])
    innerT_d = din("innerT", [128, 128])
    xi512_d = din("xi512", [128, 512])
    zeta_d = din("zeta", [128, 1])
    gch_d = din("gch", [128, 1])
    ident_d = din("ident", [128, 128])
    utri_d = din("utri", [128, 128])
    cmask_d = din("cmask", [128, 128])
    pm_d = din("pm", [128, 128])
    hmask_d = din("hmask", [128, 1])
    w_out = din("w_out", [D, D])
    w_up = din("w_up", [D, 2 * DFF])
    w_dn = din("w_dn", [DFF, D])
    convw = din("convw", [128, 2 * NCH * 3])
    convb = din("convb", [128, 2 * NCH])
    ln1g_d = din("ln1g", [1, D])
    ln1b_d = din("ln1b", [1, D])
    ln2g_d = din("ln2g", [1, D])
    ln2b_d = din("ln2b", [1, D])
    y = nc.dram_tensor("y", [SLAB, D], F32, kind="ExternalOutput").ap()

    w1f16 = nc.dram_tensor("w1f16", [D, 512], BF16).ap()
    w1t16 = nc.dram_tensor("w1t16", [D, 193], BF16).ap()
    wout16 = nc.dram_tensor("wout16", [D, D], BF16).ap()
    wup16 = nc.dram_tensor("wup16", [D, 2 * DFF], BF16).ap()
    wdn16 = nc.dram_tensor("wdn16", [DFF, D], BF16).ap()
    XR = 768
    BLK = NCORES * XR
    xin = [nc.dram_tensor("xin%d" % t, [XR, 128], BF16) for t in range(NT)]
    o1 = [nc.dram_tensor("o1_%d" % t, [4 * XR, 128], BF16) for t in range(NT)]
    O2 = nc.dram_tensor("O2", [(NT + 1) * BLK, 128], BF16)
    O2ap = O2.ap()
    GQ = [[0, 2, 4, 6], [1, 3, 5, 7]]
    GP = [[0, 1], [2, 3], [4, 5], [6, 7]]

    with ExitStack() as top:
        def sb(es, name, shape, dt=F32):
            return es.enter_context(nc.sbuf_tensor("s_" + name, shape, dt))

        def pst(es, name, shape, dt=F32):
            return es.enter_context(nc.psum_tensor("p_" + name, shape, dt))

        g1bc = sb(top, "g1bc", [128, D])
        g2bc = sb(top, "g2bc", [128, D])
        sc1p = sb(top, "sc1p", [128, 8])
        sh1c = sb(top, "sh1c", [128, 8])
        sc2p = sb(top, "sc2p", [128, 8])
        sh2c = sb(top, "sh2c", [128, 8])
        ident = sb(top, "ident", [128, 128])
        ident16 = sb(top, "ident16", [128, 128], BF16)
        ones = sb(top, "ones", [128, 512])
        B_const = Buf()
        B_mod = Buf()

        P.dma("sp", DMA(ident[:], ident_d), "ld_c0", writes=[B_const])
        P.op("pool", MS(ones[:], 1.0), writes=[B_const])
        P.op("dve", CP(ident16[:], ident[:]), reads=[B_const], writes=[B_const])

        B_w16a = Buf()
        B_w16 = Buf()

        def cast_dma(dst, src, rows, cols, sem, Bw, defer=None):
            cw_ = max(c for c in range(1, 2049) if cols % c == 0)
            if cw_ != cols:
                d2 = dst.rearrange("r (k c) -> (r k) c", c=cw_)
                s2 = src.rearrange("r (k c) -> (r k) c", c=cw_)
                n = rows * (cols // cw_)
            else:
                d2, s2, n = dst, src, rows
            step = 2048
            for r0 in range(0, n, step):
                r1 = min(n, r0 + step)
                fn_ = (lambda a, b_: (lambda: P.dma("pool", DMA(d2[a:b_, :], s2[a:b_, :]), sem, writes=[Bw])))(r0, r1)
                if defer is None:
                    fn_()
                else:
                    defer.append(fn_)

        if stop != -2:
            cast_dma(w1f16, w1f, D, 512, "ld_w16a", B_w16a)
            cast_dma(w1t16, w1t, D, 193, "ld_w16a", B_w16a)

        final = []
        try:
            if stop in (-1, -2):
                final.append(P.dma("sp", DMA(y[0:128, 0:128], ident[:]), "st_dbg", reads=[B_const, B_w16a]))
                raise _Stop()
            with ExitStack() as s0:
                ccol = sb(s0, "ccol", [128, 8])
                sil = sb(s0, "sil", [128, 8])
                tmp8 = sb(s0, "tmp8", [128, 8])
                rep = sb(s0, "rep", [128, 8, 128])
                wa = [sb(s0, "wa%d" % i, [128, 8, 512]) for i in range(2)]
                bb = [sb(s0, "bb%d" % i, [128, 512]) for i in range(2)]
                modbc = sb(s0, "modbc", [128, D])
                dtmp = sb(s0, "dtmp", [128, 128])
                psm = [pst(s0, "psm%d" % i, [128, 512]) for i in range(2)]
                B_wa = [Buf(), Buf()]
                B_bb = [Buf(), Buf()]
                B_psm = [Buf(True), Buf(True)]
                B_s0 = Buf()
                B_modbc = Buf()
                B_dtmp = Buf()

                P.dma("sp", DMA(ccol[:], c_col), "ld_c0b", writes=[B_s0])
                P.op("act", ACT(tmp8[:], ccol[:], AF.Exp, scale=-1.0), reads=[B_s0], writes=[B_s0])
                P.op("dve", TSC(tmp8[:], tmp8[:], 1.0, None, ALU.add), reads=[B_s0], writes=[B_s0])
                P.op("dve", lambda e: e.reciprocal(out=tmp8[:], in_=tmp8[:]), reads=[B_s0], writes=[B_s0])
                P.op("dve", TT(sil[:], ccol[:], tmp8[:], ALU.mult), reads=[B_s0], writes=[B_s0])
                for kc in range(8):
                    P.op("dve", TSC(rep[:, kc, :], ones[:, 0:128], sil[:, kc:kc + 1], None, ALU.mult),
                         reads=[B_s0, B_const], writes=[B_s0])
                if stop == -3:
                    final.append(P.dma("sp", DMA(y[0:128, 0:128], rep[:, 3, :]), "st_dbg", reads=[B_s0]))
                    raise _Stop()
                w_ada_v = w_ada.rearrange("(kc p) n -> p kc n", p=128)
                it = 0
                for g in range(6):
                    for n2 in range(2):
                        bi = it % 2
                        it += 1
                        c0 = g * D + n2 * 512
                        P.dma("sp", DMA(wa[bi][:], w_ada_v[:, :, c0:c0 + 512]), "ld_wa%d" % bi, writes=[B_wa[bi]])
                        P.dma("sp", DMA(bb[bi][:], b_ada[0:1, c0:c0 + 512].broadcast_to([128, 512])), "ld_bb%d" % bi,
                              writes=[B_bb[bi]])
                        for kc in range(8):
                            P.op("pe", MM(psm[bi][:], rep[:, kc, :], wa[bi][:, kc, :], kc == 0, kc == 7),
                                 reads=[B_s0, B_wa[bi]], writes=[B_psm[bi]], signal=(kc == 7))
                        P.op("dve", TT(modbc[:, n2 * 512:(n2 + 1) * 512], psm[bi][:], bb[bi][:], ALU.add),
                             reads=[B_psm[bi], B_bb[bi]], writes=[B_modbc])
                    if stop == -4:
                        final.append(P.dma("sp", DMA(y[0:128, :], modbc[:]), "st_dbg", reads=[B_modbc]))
                        raise _Stop()
                    if g in (0, 1, 3, 4):
                        dst = {0: sh1c, 1: sc1p, 3: sh2c, 4: sc2p}[g]
                        for cc in range(8):
                            P.op("dve", TT(dtmp[:], modbc[:, cc * 128:(cc + 1) * 128], ident[:], ALU.mult),
                                 reads=[B_modbc, B_const], writes=[B_dtmp])
                            P.op("dve", lambda e, cc=cc, dst=dst: e.reduce_sum(out=dst[:, cc:cc + 1], in_=dtmp[:], axis=AX.X),
                                 reads=[B_dtmp], writes=[B_mod])
                        if g in (1, 4):
                            P.op("dve", TSC(dst[:], dst[:], 1.0, None, ALU.add), reads=[B_mod], writes=[B_mod])
                        if stop == -5:
                            final.append(P.dma("sp", DMA(y[0:128, 0:8], dst[:]), "st_dbg", reads=[B_mod]))
                            raise _Stop()
                    else:
                        dst = g1bc if g == 2 else g2bc
                        P.op("dve", CP(dst[:], modbc[:]), reads=[B_modbc], writes=[B_mod])
            if stop == 0:
                final.append(P.dma("sp", DMA(y[0:128, :], g1bc[:]), "st_dbg", reads=[B_mod]))
                raise _Stop()
            P.set_fence()

            big_casts = []
            cast_dma(wout16, w_out, D, D, "ld_w16b", B_w16, big_casts)
            cast_dma(wup16, w_up, D, 2 * DFF, "ld_w16b", B_w16, big_casts)
            cast_dma(wdn16, w_dn, DFF, D, "ld_w16b", B_w16, big_casts)

            with ExitStack() as s1:
                xt = [sb(s1, "xt%d" % i, [128, 8, TS]) for i in range(2)]
                hT = [sb(s1, "hT%d" % i, [128, 8, TS], BF16) for i in range(2)]
                wf = sb(s1, "wf", [128, 8, 512], BF16)
                wt = sb(s1, "wt", [128, 8, 193], BF16)
                Kaug = sb(s1, "Kaug", [128, S], BF16)
                Vaug = sb(s1, "Vaug", [128, NKB, 128], BF16)
                Qaug = [sb(s1, "Qaug%d" % i, [128, TS], BF16) for i in range(2)]
                NPT = 4
                PT = [sb(s1, "PT%d" % i, [128, TS], BF16) for i in range(NPT)]
                NFcol = sb(s1, "NFcol", [128, NKB])
                biasT = [sb(s1, "biasT%d" % i, [128, NKB]) for i in range(2)]
                carc = sb(s1, "carc", [128, 1])
                carr = sb(s1, "carr", [128, 1])
                negb = sb(s1, "negb", [128, 1])
                fr1 = sb(s1, "fr1", [128, TS])
                fr2 = sb(s1, "fr2", [128, TS])
                fhi = sb(s1, "fhi", [128, TS], BF16)
                flo = sb(s1, "flo", [128, TS], BF16)
                spc = sb(s1, "spc", [128, 4])
                exc = sb(s1, "exc", [128, 5])
                cs = [sb(s1, "cs%d" % i, [128, 2, TS]) for i in range(2)]
                q16 = sb(s1, "q16", [128, TS], BF16)
                k16 = sb(s1, "k16", [128, TS], BF16)
                rt1 = sb(s1, "rt1", [128, TS])
                rt2 = sb(s1, "rt2", [128, TS])
                qr16 = sb(s1, "qr16", [128, TS], BF16)
                kr16 = sb(s1, "kr16", [128, TS], BF16)
                qxi16 = sb(s1, "qxi16", [128, TS], BF16)
                v16 = sb(s1, "v16", [128, 4, 64], BF16)
                gat = sb(s1, "gat", [128, 4, 64])
                gat2 = sb(s1, "gat2", [128, 4, 64])
                kz16 = sb(s1, "kz16", [128, 128], BF16)
                sTm16 = sb(s1, "sTm16", [128, 128], BF16)
                R32 = sb(s1, "R32", [128, 64])
                R16 = sb(s1, "R16", [128, 64], BF16)
                ro16 = [sb(s1, "ro16_%d" % i, [128, 4, 128], BF16) for i in range(2)]
                OT16 = [sb(s1, "OT16_%d" % i, [64, TS], BF16) for i in range(2)]
                rr = sb(s1, "rr", [128, TS])
                rhi = sb(s1, "rhi", [128, TS], BF16)
                rlo = sb(s1, "rlo", [128, TS], BF16)
                bcs = sb(s1, "bcs", [64, TS])
                sel16 = sb(s1, "sel16", [128, 64], BF16)
                utri = sb(s1, "utri", [128, 128])
                cm32 = sb(s1, "cm32", [128, 128])
                pm32 = sb(s1, "pm32", [128, 128])
                cmask16 = sb(s1, "cmask16", [128, 128], BF16)
                pm16 = sb(s1, "pm16", [128, 128], BF16)
                innerT = sb(s1, "innerT", [128, 128])
                xi512 = sb(s1, "xi512", [128, 512])
                zeta = sb(s1, "zeta", [128, 1])
                gch = sb(s1, "gch", [128, 1])
                zero16 = sb(s1, "zero16", [128, 128], BF16)

                psS = [pst(s1, "psS%d" % i, [128, 512]) for i in range(3)]
                psO = [pst(s1, "psO%d" % i, [128, 512]) for i in range(2)]
                psM = [pst(s1, "psM%d" % i, [128, 512]) for i in range(2)]
                psT = pst(s1, "psT", [128, 1024], BF16)
                B_psS = [Buf(True) for _ in range(3)]
                B_psO = [Buf(True) for _ in range(2)]
                B_psM = [Buf(True) for _ in range(2)]
                B_psT = Buf(True)
                mctr = [0]

                def misc_bank():
                    i = mctr[0] % 2
                    mctr[0] += 1
                    return psM[i], B_psM[i]

                B_c1 = Buf()
                B_xt = [[Buf() for _ in range(8)] for _ in range(2)]
                B_hT = [[Buf() for _ in range(8)] for _ in range(2)]
                B_K = [Buf() for _ in range(NT)]
                B_V = [Buf() for _ in range(NT)]
                B_Q = [Buf(), Buf()]
                B_PT = [Buf() for _ in range(NPT)]
                B_NF = Buf()
                B_bias = [Buf(), Buf()]
                B_carc = Buf()
                B_carr = Buf()
                B_fr = Buf()
                B_spc = Buf()
                B_cs = [Buf(), Buf()]
                B_q16, B_k16, B_rt1, B_rt2 = Buf(), Buf(), Buf(), Buf()
                B_qr, B_kr, B_qxi = Buf(), Buf(), Buf()
                B_v16, B_gat, B_kz, B_sTm, B_R32, B_R16 = Buf(), Buf(), Buf(), Buf(), Buf(), Buf()
                B_ro = [Buf(), Buf()]
                B_OT = [Buf(), Buf()]
                B_rr = Buf()
                B_bcs = Buf()
                B_xin = [Buf() for _ in range(NT)]
                B_o1 = [Buf() for _ in range(NT)]
                B_O2 = Buf()

                P.dma("sp", DMA(utri[:], utri_d), "ld_c", writes=[B_c1])
                P.dma("sp", DMA(cm32[:], cmask_d), "ld_c", writes=[B_c1])
                P.dma("sp", DMA(pm32[:], pm_d), "ld_c", writes=[B_c1])
                P.dma("sp", DMA(innerT[:], innerT_d), "ld_c", writes=[B_c1])
                P.dma("sp", DMA(xi512[:], xi512_d), "ld_c", writes=[B_c1])
                P.dma("sp", DMA(zeta[:], zeta_d), "ld_c", writes=[B_c1])
                P.dma("sp", DMA(gch[:], gch_d), "ld_c", writes=[B_c1])
                P.dma("sp", DMA(negb[:], bfh), "ld_c", writes=[B_c1])
                P.dma("sp", DMA(wf[:], w1f16.rearrange("(kc p) n -> p kc n", p=128)), "ld_c", reads=[B_w16a], writes=[B_c1])
                P.dma("sp", DMA(wt[:], w1t16.rearrange("(kc p) n -> p kc n", p=128)), "ld_c", reads=[B_w16a], writes=[B_c1])
                P.batch_done("ld_c", [B_c1])
                P.op("dve", TSC(negb[:], negb[:], -1.0, None, ALU.mult), reads=[B_c1], writes=[B_c1])
                P.op("dve", CP(cmask16[:], cm32[:]), reads=[B_c1], writes=[B_c1])
                P.op("dve", CP(pm16[:], pm32[:]), reads=[B_c1], writes=[B_c1])
                P.op("pool", MS(sel16[:], 0.0), writes=[B_c1])
                P.op("pool", MS(sel16[64:65, :], 1.0), writes=[B_c1])
                P.op("pool", MS(sel16[96:97, :], 1.0), writes=[B_c1])
                P.op("pool", MS(zero16[:], 0.0), writes=[B_c1])
                P.op("pool", MS(Kaug[64:97, :], 0.0), writes=[B_c1])
                P.op("pool", MS(Kaug[64:65, :], 1.0), writes=[B_c1])
                P.op("pool", MS(Kaug[96:97, :], 1.0), writes=[B_c1])
                P.op("pool", MS(Vaug[:, :, 64:128], 1.0), writes=[B_c1])
                P.op("pool", MS(Qaug[0][:], 0.0), writes=[B_Q[0]])
                P.op("pool", MS(Qaug[1][:], 0.0), writes=[B_Q[1]])
                P.op("dve", MS(carc[:], 0.0), writes=[B_carc])
                P.op("dve", MS(carr[:], 0.0), writes=[B_carr])
                P.op("dve", MS(R32[:], 0.0), writes=[B_R32])
                P.op("dve", MS(R16[:], 0.0), writes=[B_R16])
                P.op("dve", MS(exc[:], 0.0), writes=[B_spc])
                for k in range(NCORES):
                    P.dma("sp", DMA(O2ap[k * XR:k * XR + 128, :], zero16[:, :]), "st_z", reads=[B_c1], writes=[B_O2])
                    P.dma("sp", DMA(O2ap[k * XR + 128:k * XR + 256, :], zero16[:, :]), "st_z", reads=[B_c1], writes=[B_O2])
                    P.dma("sp", DMA(O2ap[k * XR + 736:k * XR + 768, :], zero16[0:32, :]), "st_z", reads=[B_c1], writes=[B_O2])

                def exchange(tt):
                    P.coll("pool", lambda e: e.collective_compute("AllGather", ALU.bypass, replica_groups=GQ,
                                                                  ins=[xin[tt].ap().opt()], outs=[o1[tt].ap().opt()]),
                           "cc1", reads=[B_xin[tt]], writes=[B_o1[tt]])
                    P.coll("pool", lambda e: e.collective_compute("AllGather", ALU.bypass, replica_groups=GP,
                                                                  ins=[o1[tt].ap().opt()],
                                                                  outs=[O2ap[(tt + 1) * BLK:(tt + 2) * BLK, :].opt()]),
                           "cc2", reads=[B_o1[tt]], writes=[B_O2])

                xT_v = xT.rearrange("(c p) t -> p c t", p=128)

                def load_x(t):
                    b = t % 2
                    t0 = t * TS
                    for hh in range(2):
                        P.dma("sp", DMA(xt[b][:, 4 * hh:4 * hh + 4, :], xT_v[:, 4 * hh:4 * hh + 4, t0:t0 + TS]),
                              "ld_x%d_%d" % (b, hh), writes=B_xt[b][4 * hh:4 * hh + 4])
                    P.dma("sp", DMA(cs[b][:, 0, :], cosT[:, t0:t0 + TS]), "ld_cs%d" % b, writes=[B_cs[b]])
                    P.dma("sp", DMA(cs[b][:, 1, :], sinT[:, t0:t0 + TS]), "ld_cs%d" % b, writes=[B_cs[b]])

                load_x(0)
                def zipg(g1, g2):
                    live = [g1, g2]
                    while live:
                        for g in list(live):
                            try:
                                next(g)
                                yield
                            except StopIteration:
                                live.remove(g)

                def prep(t):
                    b = t % 2
                    t0 = t * TS
                    nkb = 4 * t + 4
                    early = t < 10

                    def EV(out, in_):
                        return ("act", ACT(out, in_, AF.Copy)) if early else ("dve", CP(out, in_))

                    if t + 1 < NT:
                        load_x(t + 1)
                    for c in range(8):
                        if early and c % 2 == 1:
                            P.op("act", ACT(hT[b][:, c, :], xt[b][:, c, :], AF.Identity, bias=sh1c[:, c:c + 1], scale=sc1p[:, c:c + 1]),
                                 reads=[B_xt[b][c], B_mod], writes=[B_hT[b][c]])
                            yield
                            continue
                        fn = TSC(hT[b][:, c, :], xt[b][:, c, :], sc1p[:, c:c + 1], sh1c[:, c:c + 1], ALU.mult, ALU.add)
                        P.op("dve", fn, reads=[B_xt[b][c], B_mod], writes=[B_hT[b][c]])
                        if c % 2 == 1:
                            yield
                    yield

                    def fproj(g, M, bank):
                        ps, Bp = psM[bank], B_psM[bank]
                        for kc in range(8):
                            P.op("pe", MM(ps[0:M, :], wf[:, kc, g * 128:g * 128 + M], hT[b][:, kc, :], kc == 0, kc == 7),
                                 reads=[B_c1] + (B_hT[b] if kc == 7 else [B_hT[b][kc]]), writes=[Bp], signal=(kc == 7))
                        return ps, Bp

                    for half in range(2):
                        ps, Bp = psM[half], B_psM[half]
                        for bl in range(2):
                            blk = 2 * half + bl
                            for kc in range(8):
                                P.op("pe", MM(ps[:, bl * 256:bl * 256 + 193], hT[b][:, kc, blk * 128:(blk + 1) * 128],
                                              wt[:, kc, :], kc == 0, kc == 7),
                                     reads=[B_c1] + (B_hT[b] if (kc == 7 and bl == 1) else [B_hT[b][kc]]), writes=[Bp],
                                     signal=(kc == 7 and bl == 1))
                        yield
                        yield
                        psv = ps[:, :].rearrange("p (b c) -> p b c", c=256)
                        kb0 = 4 * t + 2 * half
                        P.op("dve", CP(Vaug[:, kb0:kb0 + 2, 0:64], psv[:, :, 0:64]), reads=[Bp], writes=[B_V[t]])
                        P.op("dve", CP(spc[:, 2 * half:2 * half + 2], psv[:, :, 192]), reads=[Bp], writes=[B_spc])
                        yield
                        P.op("dve", CP(v16[:, 2 * half:2 * half + 2, :], psv[:, :, 64:128]), reads=[Bp], writes=[B_v16])
                        P.op("dve", CP(gat[:, 2 * half:2 * half + 2, :], psv[:, :, 128:192]), reads=[Bp], writes=[B_gat])
                        yield

                    def chainA():
                        psA, BA = fproj(0, 97, 0)
                        P.op("act", ACT(spc[:], spc[:], AF.Exp, bias=negb[:, 0:1], scale=-1.0), reads=[B_spc, B_c1], writes=[B_spc])
                        P.op("act", ACT(spc[:], spc[:], AF.Ln, bias=1.0), reads=[B_spc], writes=[B_spc])
                        yield
                        yield
                        P.op(*EV(Qaug[b][0:64, :], psA[0:64, :]), reads=[BA], writes=[B_Q[b]])
                        P.op("dve", CP(fr2[64:97, :], psA[64:97, :]), reads=[BA], writes=[B_fr])
                        yield
                        psB, BB = fproj(1, 64, 0)
                        P.op("act", ACT(fr1[64:97, :], fr2[64:97, :], AF.Exp, bias=negb[64:97, 0:1], scale=-1.0),
                             reads=[B_fr, B_c1], writes=[B_fr])
                        P.op("act", ACT(fr1[64:97, :], fr1[64:97, :], AF.Ln, bias=1.0), reads=[B_fr], writes=[B_fr])
                        yield
                        yield
                        P.op(*EV(Kaug[0:64, t0:t0 + TS], psB[0:64, :]), reads=[BB], writes=[B_K[t]])
                        P.op("dve", lambda e: e.tensor_tensor_scan(out=fr2[64:97, :], data0=ones[64:97, :], data1=fr1[64:97, :],
                                                                   initial=carr[64:97, 0:1], op0=ALU.mult, op1=ALU.add),
                             reads=[B_fr, B_carr, B_const], writes=[B_fr])
                        yield
                        P.op("dve", TSC(fr1[64:97, :], fr2[64:97, :], carr[64:97, 0:1], -8.0, ALU.subtract, ALU.mult),
                             reads=[B_fr, B_carr], writes=[B_fr])
                        P.op("dve", CP(carr[64:97, 0:1], fr2[64:97, TS - 1:TS]), reads=[B_fr], writes=[B_carr])
                        P.op("dve", CP(fhi[64:97, :], fr1[64:97, :]), reads=[B_fr], writes=[B_fr])
                        yield
                        P.op("dve", TT(flo[64:97, :], fr1[64:97, :], fhi[64:97, :], ALU.subtract), reads=[B_fr], writes=[B_fr])
                        P.op("dve", CP(Qaug[b][64:65, :], fhi[64:65, :]), reads=[B_fr], writes=[B_Q[b]])
                        P.op("dve", CP(Qaug[b][96:97, :], flo[96:97, :]), reads=[B_fr], writes=[B_Q[b]])
                        yield
                        for bl in range(1, 5):
                            P.op("dve", TT(exc[:, bl:bl + 1], exc[:, bl - 1:bl], spc[:, bl - 1:bl], ALU.add),
                                 reads=[B_spc], writes=[B_spc])
                        yield
                        yield
                        psF, BF = psM[0], B_psM[0]
                        P.op("pe", MM(psF[:, 0:4], utri[:], spc[:], True, False), reads=[B_c1, B_spc], writes=[BF], signal=False)
                        P.op("pe", MM(psF[:, 0:4], ones[:, 0:128], exc[:, 0:4], False, True), reads=[B_c1, B_spc, B_const], writes=[BF])
                        P.op("pe", MM(psF[:, 8:9], ones[:, 0:128], exc[:, 4:5], True, True), reads=[B_spc, B_const], writes=[BF])
                        yield
                        yield
                        P.op("dve", TSC(NFcol[:, 4 * t:4 * t + 4], psF[:, 0:4], carc[:, 0:1], None, ALU.add),
                             reads=[BF, B_carc], writes=[B_NF])
                        P.op("dve", TSC(biasT[b][:, 0:nkb], NFcol[:, 0:nkb], carc[:, 0:1], None, ALU.subtract),
                             reads=[B_NF, B_carc], writes=[B_bias[b]])
                        P.op("dve", TT(carc[:], carc[:], psF[:, 8:9], ALU.add), reads=[BF, B_carc], writes=[B_carc])
                        yield

                    def chainB():
                        psC, BC = fproj(2, 128, 1)
                        P.op("act", ACT(gat2[:], gat[:], AF.Exp, scale=-1.0), reads=[B_gat], writes=[B_gat])
                        yield
                        yield
                        P.op(*EV(q16[:], psC[:]), reads=[BC], writes=[B_q16])
                        yield
                        psD, BD = fproj(3, 128, 1)
                        P.op("dve", TSC(gat2[:], gat2[:], 1.0, None, ALU.add), reads=[B_gat], writes=[B_gat])
                        P.op("dve", lambda e: e.reciprocal(out=gat2[:], in_=gat2[:]), reads=[B_gat], writes=[B_gat])
                        yield
                        yield
                        P.op(*EV(k16[:], psD[:]), reads=[BD], writes=[B_k16])
                        P.op("dve", TT(ro16[b][:, :, 64:128], gat[:], gat2[:], ALU.mult), reads=[B_gat], writes=[B_ro[b]])
                        yield
                        for (src16, Bsrc, dst16, Bdst) in ((q16, B_q16, qr16, B_qr), (k16, B_k16, kr16, B_kr)):
                            psP, BP = psM[1], B_psM[1]
                            P.op("pe", MM(psP[:], pm16[:], src16[:], True, True), reads=[B_c1, Bsrc], writes=[BP])
                            P.op("dve", TT(rt1[:], src16[:], cs[b][:, 0, :], ALU.mult), reads=[Bsrc, B_cs[b]], writes=[B_rt1])
                            yield
                            yield
                            P.op("dve", TT(rt2[:], psP[:], cs[b][:, 1, :], ALU.mult), reads=[BP, B_cs[b]], writes=[B_rt2])
                            yield
                            P.op("dve", TT(dst16[:], rt1[:], rt2[:], ALU.add), reads=[B_rt1, B_rt2], writes=[Bdst])
                            yield
                        P.op("dve", TT(qxi16[:], qr16[:], xi512[:], ALU.mult), reads=[B_qr, B_c1], writes=[B_qxi])
                        yield
                        for ci in range(4):
                            csl = slice(ci * 128, (ci + 1) * 128)
                            P.op("pe", TR(psT[:, 0:128], kr16[:, csl], ident16[:]), reads=[B_kr, B_const], writes=[B_psT])
                            psR, BR = psM[1], B_psM[1]
                            P.op("pe", MM(psR[:, 0:128], kr16[:, csl], qr16[:, csl], True, True), reads=[B_kr, B_qr], writes=[BR])
                            yield
                            P.op("dve", TSC(kz16[:], psT[:, 0:128], zeta[:, 0:1], None, ALU.mult), reads=[B_psT, B_c1], writes=[B_kz])
                            P.op("dve", TT(sTm16[:], psR[:, 0:128], innerT[:], ALU.mult), reads=[BR, B_c1], writes=[B_sTm])
                            yield
                            yield
                            P.op("pe", MM(psR[:, 128:192], sTm16[:], v16[:, ci, :], True, False), reads=[B_sTm, B_v16], writes=[BR],
                                 signal=False)
                            P.op("pe", MM(psR[:, 128:192], qxi16[:, csl], R16[:], False, True), reads=[B_sTm, B_v16, B_qxi, B_R16],
                                 writes=[BR], signal=False)
                            P.op("pe", MM(psR[:, 256:320], kz16[:], v16[:, ci, :], True, True), reads=[B_kz, B_v16, B_sTm, B_qxi, B_R16],
                                 writes=[BR])
                            yield
                            yield
                            P.op("dve", CP(ro16[b][:, ci, 0:64], psR[:, 128:192]), reads=[BR], writes=[B_ro[b]])
                            P.op("dve", STT(R32[:], R32[:], gch[:, 0:1], psR[:, 256:320], ALU.mult, ALU.add),
                                 reads=[BR, B_c1], writes=[B_R32])
                            P.op("dve", CP(R16[:], R32[:]), reads=[B_R32], writes=[B_R16])
                            yield
                        P.dma("sp", DMA(xin[t].ap()[256:768, :].rearrange("(b p) f -> p b f", p=128), ro16[b][:]),
                              "st_ro%d" % b, reads=[B_ro[b]], writes=[B_xin[t]])

                    yield from zipg(chainA(), chainB())

                def finalize(t):
                    b = t % 2
                    psOb, BOb = psO[b], B_psO[b]
                    P.op("dve", lambda e, psOb=psOb: e.reciprocal(out=rr[64:97, :], in_=psOb[64:97, :]), reads=[BOb], writes=[B_rr])
                    yield
                    P.op("dve", CP(rhi[64:97, :], rr[64:97, :]), reads=[B_rr], writes=[B_rr])
                    P.op("dve", TT(rlo[64:97, :], rr[64:97, :], rhi[64:97, :], ALU.subtract), reads=[B_rr], writes=[B_rr])
                    P.op("dve", CP(rhi[96:97, :], rlo[96:97, :]), reads=[B_rr], writes=[B_rr])
                    yield
                    yield
                    yield
                    psN, BN = psM[0], B_psM[0]
                    P.op("pe", MM(psN[0:64, :], sel16[64:97, :], rhi[64:97, :], True, True), reads=[B_c1, B_rr], writes=[BN])
                    yield
                    yield
                    P.op("dve", CP(bcs[:], psN[0:64, :]), reads=[BN], writes=[B_bcs])
                    P.op("dve", TT(OT16[b][:], psOb[0:64, :], bcs[:], ALU.mult), reads=[BOb, B_bcs], writes=[B_OT[b]])
                    yield
                    P.dma("sp", DMA(xin[t].ap()[0:256, :].rearrange("(d four) c -> d (four c)", four=4), OT16[b][:]),
                          "st_ot%d" % b, reads=[B_OT[b]], writes=[B_xin[t]])
                    exchange(t)

                def chain(*gens):
                    for g in gens:
                        for _ in g:
                            yield

                NHOPS = 90
                for _ in prep(0):
                    pass
                for t in range(NT):
                    b = t % 2
                    t0 = t * TS
                    nkb = 4 * t + 4
                    gens = []
                    if t >= 1:
                        gens.append(finalize(t - 1))
                    if t + 1 < NT:
                        gens.append(prep(t + 1))
                    gen = chain(*gens)
                    per = -(-NHOPS // nkb)
                    psOb, BOb = psO[b], B_psO[b]

                    def emit_S(kb):
                        si = kb % 3
                        pi = kb % NPT
                        j = kb - 4 * t
                        n0 = 0 if j < 0 else 128 * j
                        kcs = slice(kb * 128, (kb + 1) * 128)
                        tk = kb // 4
                        if j < 0:
                            P.op("pe", MM(psS[si][:, :], Kaug[0:97, kcs], Qaug[b][0:97, :], True, True),
                                 reads=[B_K[tk], B_c1, B_Q[b]], writes=[B_psS[si]])
                        else:
                            P.op("pe", MM(psS[si][:, n0:n0 + 128], Kaug[0:97, kcs], Qaug[b][0:97, n0:n0 + 128], True, False),
                                 reads=[B_K[tk], B_c1, B_Q[b]], writes=[B_psS[si]], signal=False)
                            P.op("pe", MM(psS[si][:, n0:n0 + 128], ident16[:], cmask16[:], False, True),
                                 reads=[B_c1, B_const, B_K[tk], B_Q[b]], writes=[B_psS[si]], signal=(n0 + 128 >= TS))
                            if n0 + 128 < TS:
                                P.op("pe", MM(psS[si][:, n0 + 128:TS], Kaug[0:97, kcs], Qaug[b][0:97, n0 + 128:TS], True, True),
                                     reads=[B_K[tk], B_c1, B_Q[b]], writes=[B_psS[si]])
                        P.op("act", ACT(PT[pi][:, n0:TS], psS[si][:, n0:TS], AF.Exp, bias=biasT[b][:, kb:kb + 1], scale=0.125),
                             reads=[B_psS[si], B_bias[b]], writes=[B_PT[pi]])

                    def emit_PV(kb):
                        pi = kb % NPT
                        j = kb - 4 * t
                        n0 = 0 if j < 0 else 128 * j
                        tk = kb // 4
                        P.op("pe", MM(psOb[:, n0:TS], Vaug[:, kb, :], PT[pi][:, n0:TS], kb == 0, kb == nkb - 1),
                             reads=[B_V[tk], B_c1, B_PT[pi]], writes=[BOb])

                    LA = 2
                    for step in range(nkb + LA):
                        if step < nkb:
                            emit_S(step)
                        if step - LA >= 0:
                            emit_PV(step - LA)
                        for _ in range(per):
                            next(gen, None)
                    for _ in gen:
                        pass
                    if big_casts and (t >= NT // 4 or NT - t <= len(big_casts)):
                        big_casts.pop(0)()
                while big_casts:
                    big_casts.pop(0)()
                for _ in finalize(NT - 1):
                    pass

                pass

            P.set_fence()
            B_ag = B_O2
            if stop in (1, 2):
                final.append(P.dma("sp", DMA(y[0:128, :], g1bc[:]), "st_dbg", reads=[B_mod, B_O2]))
                raise _Stop()

            PIDS = {}
            with ExitStack() as s2:
                QT = min(512, SLAB)
                NQ = SLAB // QT
                NBL = QT // 128
                lng = [sb(s2, "lng%d" % i, [128, D]) for i in range(4)]
                wo16 = sb(s2, "wo16", [128, 8, D], BF16)
                cw = sb(s2, "cw", [128, 2 * NCH * 3])
                cb = sb(s2, "cb", [128, 2 * NCH])
                hmask = sb(s2, "hmask", [128, 1])
                neghalf = sb(s2, "neghalf", [128, 4])
                foxT = sb(s2, "foxT", [128, 4, QT], BF16)
                retin = [sb(s2, "retin%d" % i, [128, 8, 128], BF16) for i in range(2)]
                oc = [sb(s2, "oc%d" % i, [128, 512]) for i in range(3)]
                rn16 = [sb(s2, "rn16_%d" % i, [128, 512], BF16) for i in range(2)]
                retT = sb(s2, "retT", [128, 4, QT], BF16)
                xsb = [sb(s2, "xsb%d" % i, [128, D]) for i in range(2)]
                zt = [sb(s2, "zt%d" % i, [128, D]) for i in range(2)]
                x1q = [sb(s2, "x1q%d" % i, [128, 4, D]) for i in range(2)]
                h2Tq = [sb(s2, "h2Tq%d" % i, [128, 8, QT], BF16) for i in range(2)]
                h2Th = sb(s2, "h2Th", [128, 8, HALO], BF16)
                st6 = [sb(s2, "st6_%d" % i, [128, 4, 6]) for i in range(3)]
                mv = [sb(s2, "mv%d" % i, [128, 4, 2]) for i in range(3)]
                rstd = [sb(s2, "rstd%d" % i, [128, 4]) for i in range(3)]
                NWU, NWD = 4, 4
                GW = 3
                wu = [sb(s2, "wu%d" % i, [128, 8, 2, GW * 128], BF16) for i in range(2)]
                wd = [sb(s2, "wd%d" % i, [128, 512], BF16) for i in range(NWD)]
                ub = [sb(s2, "ub%d" % i, [128, QT + 2]) for i in range(4)]
                yv = [sb(s2, "yv%d" % i, [128, QT]) for i in range(6)]
                uprev = sb(s2, "uprev", [128, 2 * NCH, 2])
                act16 = sb(s2, "act16", [128, NCH, QT], BF16)
                ot = [sb(s2, "ot%d" % i, [128, D]) for i in range(2)]

                pA = [pst(s2, "pA%d" % i, [128, 512]) for i in range(4)]
                pU = [pst(s2, "pU%d" % i, [128, 512]) for i in range(2)]
                pTf = pst(s2, "pTf", [128, 512])
                pTb = pst(s2, "pTb", [128, 1024], BF16)
                B_pA = [Buf(True) for _ in range(4)]
                B_pU = [Buf(True) for _ in range(2)]
                B_pTf, B_pTb = Buf(True), Buf(True)
                B_c2 = Buf()
                B_fox, B_retT = Buf(), Buf()
                B_oc = [Buf() for _ in range(3)]
                B_rn = [Buf(), Buf()]
                B_retin = [Buf(), Buf()]
                B_xsb = [Buf(), Buf()]
                B_zt = [Buf(), Buf()]
                B_st = [Buf() for _ in range(3)]
                B_x1q = [[Buf() for _ in range(4)] for _ in range(2)]
                B_h2Tq = [Buf(), Buf()]
                B_h2Th = Buf()
                B_wu = [Buf(), Buf()]
                B_wd = [Buf() for _ in range(NWD)]
                B_ub = [Buf() for _ in range(4)]
                B_yv = [Buf() for _ in range(6)]
                B_uprev = Buf()
                B_act = [Buf() for _ in range(NCH)]
                B_ot = [Buf(), Buf()]

                for i, src in enumerate((ln1g_d, ln1b_d, ln2g_d, ln2b_d)):
                    P.dma("sp", DMA(lng[i][:], src[0:1, :].broadcast_to([128, D])), "ld_c", writes=[B_c2])
                P.dma("sp", DMA(wo16[:], wout16.rearrange("(kc p) n -> p kc n", p=128)), "ld_c", reads=[B_w16], writes=[B_c2])
                P.dma("sp", DMA(cw[:], convw), "ld_c", writes=[B_c2])
                P.dma("sp", DMA(cb[:], convb), "ld_c", writes=[B_c2])
                P.dma("sp", DMA(hmask[:], hmask_d), "ld_c", writes=[B_c2])
                P.batch_done("ld_c", [B_c2])
                P.op("pool", MS(neghalf[:], -0.5), writes=[B_c2])

                TPS = SLAB // TS
                assert QT == TS and TPS >= 1

                def tile_blk(e, qidx, eng):
                    if eng not in PIDS:
                        PIDS[eng] = e.partition_id() * (TPS * BLK)
                    return O2ap[(qidx + 1) * BLK:(NT + 1) * BLK, :][bass.ds(PIDS[eng], BLK), :]
                wup_v = wup16.rearrange("(kc p) n -> p kc n", p=128)
                wdn_v = wdn16.rearrange("(i p) n -> p i n", p=128)

                def wait(n):
                    for _ in range(n):
                        yield

                def layer_norm(eng_obj_unused, src, np_, gi, dst, Bsrc, Bdst, k=0):
                    for hh in range(2):
                        P.op("dve", lambda e, hh=hh: e.bn_stats(out=st6[k][0:np_, hh, :], in_=src[0:np_, hh * 512:(hh + 1) * 512]),
                             reads=[Bsrc], writes=[B_st[k]])
                    P.op("dve", lambda e: e.bn_aggr(out=mv[k][0:np_, 0, :], in_=st6[k][0:np_, 0:2, :].rearrange("p a s -> p (a s)")),
                         reads=[B_st[k]], writes=[B_st[k]])
                    P.op("dve", TSC(rstd[k][0:np_, 0:1], mv[k][0:np_, 0, 1:2], LN_EPS, None, ALU.add), reads=[B_st[k]], writes=[B_st[k]])
                    yield from wait(4)
                    P.op("pool", TT(rstd[k][0:np_, 0:1], rstd[k][0:np_, 0:1], neghalf[0:np_, 0:1], ALU.pow), reads=[B_st[k], B_c2], writes=[B_st[k]])
                    yield from wait(2)
                    P.op("dve", TSC(rstd[k][0:np_, 1:2], mv[k][0:np_, 0, 0:1], rstd[k][0:np_, 0:1], -1.0, ALU.mult, ALU.mult),
                         reads=[B_st[k]], writes=[B_st[k]])
                    yield from wait(2)
                    P.op("act", ACT(src[0:np_, :], src[0:np_, :], AF.Identity, bias=rstd[k][0:np_, 1:2], scale=rstd[k][0:np_, 0:1]),
                         reads=[B_st[k]], writes=[Bsrc])
                    yield from wait(4)
                    P.op("dve", TT(src[0:np_, :], src[0:np_, :], lng[gi][0:np_, :], ALU.mult), reads=[B_c2], writes=[Bsrc])
                    yield from wait(4)
                    P.op("pool", TT(dst[0:np_, :], src[0:np_, :], lng[gi + 1][0:np_, :], ALU.add), reads=[Bsrc, B_c2], writes=[Bdst])
                    yield from wait(16)

                def token_block(loc0, np_, x1dst, Bx1, hcol0, ri, qidx, tok0, h2dst, Bh2):
                    def dyn_ret(e):
                        src = tile_blk(e, qidx, "sp").rearrange("(k x) f -> x k f", x=XR)
                        return e.dma_start(out=retin[ri][0:np_, :, :], in_=src[256 + tok0:256 + tok0 + np_, :, :])
                    P.dma("sp", dyn_ret, "ld_ret%d" % ri, reads=[B_ag], writes=[B_retin[ri]])
                    P.dma("sp", DMA(xsb[ri][0:np_, :], xs[loc0:loc0 + np_, :]), "ld_xs%d" % ri, writes=[B_xsb[ri]])
                    yield from wait(8)
                    oc_, rn_, zt_ = oc[ri], rn16[ri], zt[ri]
                    Boc, Brn, Bzt, Bst = B_oc[ri], B_rn[ri], B_zt[ri], B_st[ri]
                    P.op("pool", CP(oc_[0:np_, :].rearrange("p (h a d) -> p a h d", a=2, d=64),
                                    retin[ri][0:np_, :, 0:64].rearrange("p (a h) d -> p a h d", a=2)),
                         reads=[B_retin[ri]], writes=[Boc])
                    yield from wait(6)
                    for h in range(4):
                        P.op("dve", lambda e, h=h: e.bn_stats(out=st6[ri][0:np_, h, :], in_=oc_[0:np_, h * 128:(h + 1) * 128]),
                             reads=[Boc], writes=[Bst])
                    for h in range(4):
                        P.op("dve", lambda e, h=h: e.bn_aggr(out=mv[ri][0:np_, h, :], in_=st6[ri][0:np_, h, :]), reads=[Bst], writes=[Bst])
                    P.op("dve", TSC(rstd[ri][0:np_, :], mv[ri][0:np_, :, 1], GN_EPS, None, ALU.add), reads=[Bst], writes=[Bst])
                    yield from wait(4)
                    P.op("pool", TT(rstd[ri][0:np_, :], rstd[ri][0:np_, :], neghalf[0:np_, :], ALU.pow), reads=[Bst, B_c2], writes=[Bst])
                    yield from wait(2)
                    for h in range(4):
                        P.op("dve", TSC(oc_[0:np_, h * 128:(h + 1) * 128], oc_[0:np_, h * 128:(h + 1) * 128], mv[ri][0:np_, h, 0:1],
                                        rstd[ri][0:np_, h:h + 1], ALU.subtract, ALU.mult), reads=[Bst], writes=[Boc])
                    yield from wait(3)
                    P.op("pool", TT(rn_[0:np_, :].rearrange("p (h a d) -> p a h d", a=2, d=64),
                                    oc_[0:np_, :].rearrange("p (h a d) -> p a h d", a=2, d=64),
                                    retin[ri][0:np_, :, 64:128].rearrange("p (a h) d -> p a h d", a=2), ALU.mult),
                         reads=[Boc, B_retin[ri]], writes=[Brn])
                    yield from wait(14)
                    for h in range(4):
                        P.op("pe", TR(pTb[:, h * 128:h * 128 + np_], rn_[0:np_, h * 128:(h + 1) * 128], ident16[0:np_, 0:np_]),
                             reads=[Brn, B_const], writes=[B_pTb])
                    P.op("dve", CP(retT[:, :, hcol0:hcol0 + np_], pTb[:, 0:512].rearrange("p (h c) -> p h c", c=128)[:, :, 0:np_]),
                         reads=[B_pTb], writes=[B_retT])
                    yield from wait(8)
                    for n2 in range(2):
                        pa, Bpa = pA[n2], B_pA[n2]
                        for c in range(8):
                            lhs = foxT[:, c, hcol0:hcol0 + np_] if c < 4 else retT[:, c - 4, hcol0:hcol0 + np_]
                            P.op("pe", MM(pa[0:np_, :], lhs, wo16[:, c, n2 * 512:(n2 + 1) * 512], c == 0, c == 7),
                                 reads=[B_fox, B_retT, B_c2], writes=[Bpa], signal=(c == 7))
                        P.op("dve", TT(zt_[0:np_, n2 * 512:(n2 + 1) * 512], pa[0:np_, :], g1bc[0:np_, n2 * 512:(n2 + 1) * 512], ALU.mult),
                             reads=[Bpa, B_mod], writes=[Bzt])
                        yield
                    yield from wait(2)
                    P.op("dve", STT(zt_[0:np_, :], xsb[ri][0:np_, :], ALPHA, zt_[0:np_, :], ALU.mult, ALU.add),
                         reads=[B_xsb[ri]], writes=[Bzt])
                    yield from wait(4)
                    yield from layer_norm(None, zt_, np_, 0, x1dst, Bzt, Bx1, k=ri)
                    for half in range(2):
                        for cc in range(4):
                            c = 4 * half + cc
                            P.op("pe", TR(pTf[:, cc * 128:cc * 128 + np_], x1dst[0:np_, c * 128:(c + 1) * 128], ident[0:np_, 0:np_]),
                                 reads=[Bx1, B_const], writes=[B_pTf])
                        for cc in range(4):
                            c = 4 * half + cc
                            P.op("dve" if cc % 2 == 0 else "act",
                                 TSC(h2dst[:, c, hcol0:hcol0 + np_], pTf[:, cc * 128:cc * 128 + np_], sc2p[:, c:c + 1], sh2c[:, c:c + 1],
                                     ALU.mult, ALU.add) if cc % 2 == 0 else
                                 ACT(h2dst[:, c, hcol0:hcol0 + np_], pTf[:, cc * 128:cc * 128 + np_], AF.Identity,
                                     bias=sh2c[:, c:c + 1], scale=sc2p[:, c:c + 1]),
                                 reads=[B_pTf, B_mod], writes=[Bh2])
                        yield from wait(2)

                def load_fox(qidx, tok0, n):
                    for two in range(2):
                        def dyn_fox(e, two=two):
                            v = tile_blk(e, qidx, "act").rearrange("(k x) f -> k x f", x=XR)[4 * two:4 * two + 4, 0:256, :]
                            v = v.rearrange("c (d four) f -> d c (four f)", four=4)
                            return e.dma_start(out=foxT[64 * two:64 * two + 64, :, 0:n], in_=v[:, :, tok0:tok0 + n])
                        P.dma("act", dyn_fox, "ld_fox", reads=[B_ag], writes=[B_fox])

                up_banks = [(pU[0], B_pU[0]), (pU[1], B_pU[1]), (pA[2], B_pA[2]), (pA[3], B_pA[3])]
                upctr = [0]
                upq = [0]

                def up_chunk(i, part, ncols, wbuf, src, Bsrc):
                    ps, Bp = up_banks[upctr[0] % len(up_banks)]
                    upctr[0] += 1
                    g, o = i // GW, i % GW
                    w = wu[g % 2]
                    for kc in range(8):
                        P.op("pe", MM(ps[:, 0:ncols], w[:, kc, part, o * 128:(o + 1) * 128], src[:, kc, 0:ncols], kc == 0, kc == 7),
                             reads=[Bsrc, B_wu[g % 2]], writes=[Bp], signal=(kc == 7))
                    return ps, Bp

                NG = -(-NCH // GW)

                def load_wu_group(g):
                    w = wu[g % 2]
                    c0 = g * GW * 128
                    nc_ = min(GW * 128, DFF - c0)
                    for part in range(2):
                        P.dma("sp", DMA(w[:, :, part, 0:nc_], wup_v[:, :, part * DFF + c0:part * DFF + c0 + nc_]),
                              "ld_wu%d" % (g % 2), reads=[B_w16], writes=[B_wu[g % 2]])

                wu_loaded = set()

                def load_wu(i):
                    if (upq[0], i) in wu_loaded:
                        return
                    wu_loaded.add((upq[0], i))
                    if i == 0:
                        load_wu_group(0)
                    if i % GW == 0 and i // GW + 1 < NG:
                        load_wu_group(i // GW + 1)

                load_fox(-1, TS - HALO, HALO)
                for _ in token_block(0, HALO, x1q[0][:, 0, :], B_x1q[0][0], 0, 0, -1, TS - HALO, h2Th, B_h2Th):
                    pass

                def zip2(g1, g2):
                    live = [g1, g2]
                    while live:
                        for g in list(live):
                            try:
                                next(g)
                                yield
                            except StopIteration:
                                live.remove(g)

                def quarter_blocks(q):
                    pq = q % 2
                    load_fox(q, 0, QT)
                    gens_ = [token_block(HALO + q * QT + bl * 128, 128, x1q[pq][:, bl, :], B_x1q[pq][bl], bl * 128, bl % 2,
                                         q, bl * 128, h2Tq[pq], B_h2Tq[pq]) for bl in range(NBL)]
                    for k in range(0, NBL, 2):
                        if k + 1 < NBL:
                            yield from zip2(gens_[k], gens_[k + 1])
                        else:
                            yield from gens_[k]

                def ln2_hops(q):
                    x1_, Bx1_ = x1q[q % 2], B_x1q[q % 2]
                    for bl in range(NBL):
                        oi = bl % 2
                        for _ in layer_norm(None, x1_[:, bl, :], 128, 2, ot[oi], Bx1_[bl], B_ot[oi], k=2):
                            yield
                        r0 = q * QT + bl * 128
                        final.append(P.dma("sp", DMA(y[r0:r0 + 128, :], ot[oi][:]), "st_y%d" % oi, reads=[B_ot[oi]]))
                        yield

                for _ in quarter_blocks(0):
                    pass
                ln2_pending = iter(())
                for q in range(NQ):
                    upq[0] = q
                    pq = q % 2
                    x1 = x1q[pq]
                    B_x1 = B_x1q[pq]
                    gen = chain(ln2_pending, quarter_blocks(q + 1) if q + 1 < NQ else iter(()))
                    def pre_u(i, part):
                        ch = part * NCH + i
                        bi_ = 2 * (i % 2) + part
                        P.op("dve", CP(ub[bi_][:, 0:2], uprev[:, ch, :]), reads=[B_uprev], writes=[B_ub[bi_]])

                    def halo_u(i):
                        for part in range(2):
                            ps, Bp = up_chunk(i, part, HALO, None, h2Th, B_h2Th)
                            P.op("dve", TSC(uprev[:, part * NCH + i, :], ps[:, HALO - 2:HALO], hmask[:, 0:1], None, ALU.mult),
                                 reads=[Bp, B_c2], writes=[B_uprev])

                    if q == 0:
                        load_wu(0)
                        halo_u(0)
                    pre_u(0, 0)

                    def geglu(i):
                        ya, yb_ = yv[2 * (i % 3)], yv[2 * (i % 3) + 1]
                        P.op("act", ACT(ya[:], ya[:], AF.Gelu), reads=[], writes=[B_yv[2 * (i % 3)]])
                        P.op("pool", TT(act16[:, i, :], ya[:], yb_[:], ALU.mult),
                             reads=[B_yv[2 * (i % 3)], B_yv[2 * (i % 3) + 1]], writes=[B_act[i]])

                    for i in range(NCH):
                        load_wu(i)
                        if q == 0 and i + 1 < NCH:
                            halo_u(i + 1)
                        for part in range(2):
                            ch = part * NCH + i
                            ps, Bp = up_chunk(i, part, QT, None, h2Tq[pq], B_h2Tq[pq])
                            bi_ = 2 * (i % 2) + part
                            u = ub[bi_]
                            yi_ = 2 * (i % 3) + part
                            yy = yv[yi_]
                            P.op("act", ACT(u[:, 2:QT + 2], ps[:, 0:QT], AF.Copy), reads=[Bp], writes=[B_ub[bi_]])
                            P.op("act", ACT(yy[:], ps[:, 0:QT], AF.Identity, bias=cb[:, ch:ch + 1], scale=cw[:, 3 * ch + 2:3 * ch + 3]),
                                 reads=[Bp, B_c2], writes=[B_yv[yi_]])
                            if True:
                                if part == 0:
                                    pre_u(i, 1)
                                elif i + 1 < NCH:
                                    pre_u(i + 1, 0)
                            P.op("dve", STT(yy[:], u[:, 1:QT + 1], cw[:, 3 * ch + 1:3 * ch + 2], yy[:], ALU.mult, ALU.add),
                                 reads=[B_ub[bi_], B_c2], writes=[B_yv[yi_]])
                            P.op("dve", STT(yy[:], u[:, 0:QT], cw[:, 3 * ch:3 * ch + 1], yy[:], ALU.mult, ALU.add),
                                 reads=[B_ub[bi_], B_c2], writes=[B_yv[yi_]])
                            P.op("dve", CP(uprev[:, ch, :], u[:, QT:QT + 2]), reads=[B_ub[bi_]], writes=[B_uprev])
                            for _ in range(13):
                                next(gen, None)
                        if i >= 1:
                            geglu(i - 1)
                    geglu(NCH - 1)
                    for _ in gen:
                        pass
                    for n2 in range(2):
                        for i in range(NCH):
                            wi = (n2 * NCH + i) % NWD
                            P.dma("sp", DMA(wd[wi][:, :], wdn_v[:, i, n2 * 512:(n2 + 1) * 512]), "ld_wd%d" % wi,
                                  reads=[B_w16], writes=[B_wd[wi]])
                            for bl in range(NBL):
                                P.op("pe", MM(pA[bl][:, :], act16[:, i, bl * 128:(bl + 1) * 128], wd[wi][:, :],
                                              i == 0, i == NCH - 1),
                                     reads=[B_act[i], B_wd[wi]], writes=[B_pA[bl]], signal=(i == NCH - 1 or bl == NBL - 1))
                        for bl in range(NBL):
                            cs_ = slice(n2 * 512, (n2 + 1) * 512)
                            P.op("dve", TT(oc[2][:, :], pA[bl][:, :], g2bc[:, cs_], ALU.mult), reads=[B_pA[bl], B_mod], writes=[B_oc[2]])
                            P.op("dve", STT(x1[:, bl, cs_], x1[:, bl, cs_], ALPHA, oc[2][:, :], ALU.mult, ALU.add),
                                 reads=[B_oc[2]], writes=[B_x1[bl]])
                    ln2_pending = ln2_hops(q)
                for _ in ln2_pending:
                    pass

        except _Stop:
            pass
        with ExitStack() as fin:
            sems = {n: fin.enter_context(nc.semaphore(n)) for n in P.sem_names()}
            block = fin.enter_context(nc.Block())
            last = {}
            if limit is not None:
                P.set_fence()
                final = list(final) + list(P.fence)
            for h in final:
                if h is not None:
                    last[h[0]] = max(last.get(h[0], 0), h[1])

            @block.sync
            def _(e):
                P.replay("sp", e, sems, final_waits=list(last.items()))

            @block.tensor
            def _(e):
                P.replay("pe", e, sems)

            @block.scalar
            def _(e):
                P.replay("act", e, sems)

            @block.vector
            def _(e):
                P.replay("dve", e, sems)

            @block.gpsimd
            def _(e):
                P.replay("pool", e, sems)
    nc._oplog = P.log
    return nc


def _const_tables():
    f32 = np.float32
    pos = np.arange(S, dtype=f32)
    inv_freq = (f32(10000.0) ** (-np.arange(0, 128, 2, dtype=f32) / f32(128))).astype(f32)
    ang = (pos[:, None] * inv_freq[None, :]).astype(f32)
    ang = np.concatenate([ang, ang], axis=-1)
    cosT = np.ascontiguousarray(np.cos(ang).astype(f32).T)
    sin = np.sin(ang).astype(f32)
    sgn = np.concatenate([-np.ones(64, f32), np.ones(64, f32)])
    sinT = np.ascontiguousarray((sin * sgn[None, :]).T.astype(f32))
    ident = np.eye(128, dtype=f32)
    utri = np.triu(np.ones((128, 128), f32))
    kk = np.arange(128)
    cmask = np.where(kk[:, None] <= kk[None, :], f32(0.0), f32(-1.0e6)).astype(f32)
    pm = np.zeros((128, 128), f32)
    for i in range(128):
        pm[(i + 64) % 128, i] = 1.0
    return cosT, sinT, ident, utri, cmask, pm


def _ret_tables(h):
    f32 = np.float32
    lg = np.log1p(-np.exp2(f32(-5.0 - h))).astype(f32)
    idx = np.arange(128, dtype=f32)
    c = f32(128.0 ** -0.5)
    diff = idx[:, None] - idx[None, :]
    inner = np.where(diff >= 0, np.exp(np.maximum(diff, 0.0) * lg), 0.0).astype(f32)
    innerT = np.ascontiguousarray((inner * c).T.astype(f32))
    xi = (np.exp((idx + 1.0) * lg) * c).astype(f32)
    xi512 = np.ascontiguousarray(np.tile(xi[None, :], (128, 4)).astype(f32))
    zeta = np.exp((127.0 - idx) * lg).astype(f32).reshape(128, 1)
    gch = np.full((128, 1), np.exp(f32(128.0) * lg), f32)
    return innerT, xi512, zeta, gch


def _make_in_maps(inputs):
    f32 = np.float32
    x = np.asarray(inputs["x"], f32)[0]
    w_in = np.asarray(inputs["w_in"], f32)[0]
    b_f = np.asarray(inputs["b_f"], f32)[0]
    conv_w = np.asarray(inputs["conv_w"], f32)[0]
    conv_b = np.asarray(inputs["conv_b"], f32)[0]
    cosT, sinT, ident, utri, cmask, pm = _const_tables()
    xT = np.ascontiguousarray(x.T)
    c_col = np.ascontiguousarray(np.asarray(inputs["c"], f32)[0].reshape(8, 128).T)
    convw = np.ascontiguousarray(conv_w.T.reshape(2 * NCH, 128, 3).transpose(1, 0, 2).reshape(128, 2 * NCH * 3))
    convb = np.ascontiguousarray(conv_b.reshape(2 * NCH, 128).T)
    shared = {
        "xT": xT, "c_col": c_col,
        "w_ada": np.ascontiguousarray(np.asarray(inputs["w_ada"], f32)[0]),
        "b_ada": np.ascontiguousarray(np.asarray(inputs["b_ada"], f32)[0].reshape(1, -1)),
        "cosT": cosT, "sinT": sinT, "ident": ident, "utri": utri, "cmask": cmask, "pm": pm,
        "w_out": np.ascontiguousarray(np.asarray(inputs["w_out"], f32)[0]),
        "w_up": np.ascontiguousarray(np.asarray(inputs["w_up"], f32)[0]),
        "w_dn": np.ascontiguousarray(np.asarray(inputs["w_down"], f32)[0]),
        "convw": convw, "convb": convb,
        "ln1g": np.asarray(inputs["ln1_g"], f32).reshape(1, D), "ln1b": np.asarray(inputs["ln1_b"], f32).reshape(1, D),
        "ln2g": np.asarray(inputs["ln2_g"], f32).reshape(1, D), "ln2b": np.asarray(inputs["ln2_b"], f32).reshape(1, D),
    }
    FQ, FK, FV, FF, RQ, RK, RV, RG = 0, 512, 1024, 1536, 1544, 2056, 2568, 3080
    maps = []
    for j in range(NCORES):
        hr, half = j // 2, j % 2
        w1f = np.zeros((D, 512), f32)
        w1f[:, 0:64] = w_in[:, FQ + 64 * j:FQ + 64 * j + 64]
        w1f[:, 64] = w_in[:, FF + j]
        w1f[:, 96] = w_in[:, FF + j]
        w1f[:, 128:192] = w_in[:, FK + 64 * j:FK + 64 * j + 64]
        w1f[:, 256:384] = w_in[:, RQ + 128 * hr:RQ + 128 * hr + 128]
        w1f[:, 384:512] = w_in[:, RK + 128 * hr:RK + 128 * hr + 128]
        o = 128 * hr + 64 * half
        w1t = np.concatenate([w_in[:, FV + 64 * j:FV + 64 * j + 64], w_in[:, RV + o:RV + o + 64],
                              w_in[:, RG + o:RG + o + 64], w_in[:, FF + j:FF + j + 1]], axis=1)
        innerT, xi512, zeta, gch = _ret_tables(hr)
        xs = np.zeros((SL, D), f32)
        lo = j * SLAB - HALO
        if lo < 0:
            xs[HALO:] = x[0:SLAB]
        else:
            xs[:] = x[lo:lo + SL]
        m = dict(shared)
        m.update({
            "xs": xs, "w1f": w1f, "w1t": np.ascontiguousarray(w1t),
            "bfh": np.full((128, 1), b_f[j], f32),
            "innerT": innerT, "xi512": xi512, "zeta": zeta, "gch": gch,
            "hmask": np.full((128, 1), 0.0 if j == 0 else 1.0, f32),
        })
        maps.append(m)
    return maps


_NC_CACHE = {}


def kernel(**inputs):
    S_ = int(np.asarray(inputs["x"]).shape[1])
    if S_ != S:
        _cfg(S_)
    if S_ not in _NC_CACHE:
        _NC_CACHE[S_] = build_program()
    nc = _NC_CACHE[S_]
    in_maps = _make_in_maps(inputs)
    res = run_bass_kernel_spmd(nc, in_maps, core_ids=list(range(NCORES)))
    out = np.concatenate([np.asarray(res.results[j]["y"], np.float32) for j in range(NCORES)], axis=0)
    return out.reshape(1, S, D)
```

```python
from contextlib import ExitStack

import numpy as np
import concourse.bass as bass
import concourse.mybir as mybir
from concourse.bass_utils import run_bass_kernel_spmd

F32 = mybir.dt.float32
BF16 = mybir.dt.bfloat16
ALU = mybir.AluOpType
AF = mybir.ActivationFunctionType
AX = mybir.AxisListType

NCORES = 8
D = 1024
TS = 512
HALO = 32


def _cfg(S_):
    global S, NT, NKB, SLAB, SL
    S = S_
    NT = S // TS
    NKB = S // 128
    SLAB = S // NCORES
    SL = SLAB + HALO


_cfg(16384)
DFF = 2816
NCH = DFF // 128
ALPHA = 2.0 ** 0.25
LN_EPS = 1e-5
GN_EPS = 1e-6
ENGS = ["pe", "act", "dve", "pool", "sp"]


class Buf:
    __slots__ = ("w", "r", "excl")

    def __init__(self, excl=False):
        self.w = None
        self.r = {}
        self.excl = excl


class Prog:
    def __init__(self):
        self.q = {e: [] for e in ENGS}
        self.cnt = {e: 0 for e in ENGS}
        self.dcnt = {}
        self.waited = {e: {} for e in ENGS}
        self.fence = []
        self.nops = 0
        self.limit = None
        self.log = []

    def _skip(self):
        self.nops += 1
        import sys as _s
        self.log.append((self.nops, _s._getframe(2).f_lineno))
        return self.limit is not None and self.nops > self.limit

    def _deps(self, eng, reads, writes, extra):
        deps = list(extra) + list(self.fence)
        for b in reads:
            deps.append(b.w)
            if b.excl:
                deps.extend(b.r.items())
        for b in writes:
            deps.append(b.w)
            deps.extend(b.r.items())
        ws = []
        for d in deps:
            if d is None:
                continue
            s, v = d
            if eng == "pe" and s == "e_pe":
                continue
            if self.waited[eng].get(s, 0) >= v:
                continue
            self.waited[eng][s] = v
            ws.append((s, v))
        return ws

    def _commit(self, h, reads, writes):
        for b in reads:
            if b.excl:
                b.w = h
                b.r = {}
            elif b.r.get(h[0], 0) < h[1]:
                b.r[h[0]] = h[1]
        for b in writes:
            b.w = h
            b.r = {}

    def op(self, eng, fn, reads=(), writes=(), extra=(), signal=True):
        if self._skip():
            return None
        ws = self._deps(eng, reads, writes, extra)
        if signal:
            self.cnt[eng] += 1
            h = ("e_" + eng, self.cnt[eng])
            self.q[eng].append((ws, fn, ("e_" + eng, 1)))
            self._commit(h, reads, writes)
            return h
        self.q[eng].append((ws, fn, None))
        return None

    def dma(self, eng, fn, sem, reads=(), writes=(), extra=()):
        if self._skip():
            return None
        ws = self._deps(eng, reads, writes, extra)
        self.dcnt[sem] = self.dcnt.get(sem, 0) + 16
        h = (sem, self.dcnt[sem])
        self.q[eng].append((ws, fn, (sem, 16)))
        self._commit(h, reads, writes)
        return h

    def coll(self, eng, fn, sem, reads=(), writes=()):
        if self._skip():
            return None
        ws = self._deps(eng, reads, writes, ())
        self.dcnt[sem] = self.dcnt.get(sem, 0) + 1
        h = (sem, self.dcnt[sem])
        self.q[eng].append((ws, fn, (sem, 1)))
        self._commit(h, reads, writes)
        return h

    def batch_done(self, sem, bufs):
        if sem not in self.dcnt:
            return
        h = (sem, self.dcnt[sem])
        for b in bufs:
            b.w = h

    def set_fence(self):
        f = [("e_" + e, self.cnt[e]) for e in ENGS if self.cnt[e] > 0]
        f += [(s, v) for s, v in self.dcnt.items()]
        self.fence = f

    def sem_names(self):
        return ["e_" + e for e in ENGS] + sorted(self.dcnt.keys())

    def replay(self, eng, engine_obj, sems, final_waits=()):
        for ws, fn, inc in self.q[eng]:
            for s, v in ws[:-1]:
                engine_obj.wait_ge(sems[s], v)
            inst = fn(engine_obj)
            if ws:
                s, v = ws[-1]
                try:
                    inst.wait_op(sems[s], v, "sem-ge")
                except Exception:
                    raise
            if inc is not None:
                inst.then_inc(sems[inc[0]], inc[1])
        for s, v in final_waits:
            engine_obj.wait_ge(sems[s], v)


def TSC(out, in0, s1, s2, op0, op1=None):
    if op1 is None:
        return lambda e: e.tensor_scalar(out=out, in0=in0, scalar1=s1, scalar2=None, op0=op0)
    return lambda e: e.tensor_scalar(out=out, in0=in0, scalar1=s1, scalar2=s2, op0=op0, op1=op1)


def TT(out, in0, in1, op):
    return lambda e: e.tensor_tensor(out=out, in0=in0, in1=in1, op=op)


def STT(out, in0, scalar, in1, op0, op1):
    return lambda e: e.scalar_tensor_tensor(out=out, in0=in0, scalar=scalar, in1=in1, op0=op0, op1=op1)


def CP(out, in_):
    return lambda e: e.tensor_copy(out=out, in_=in_)


def ACT(out, in_, func, bias=None, scale=None):
    kw = {}
    if bias is not None:
        kw["bias"] = bias
    if scale is not None:
        kw["scale"] = scale
    return lambda e: e.activation(out=out, in_=in_, func=func, **kw)


def MM(out, lhsT, rhs, start, stop):
    return lambda e: e.matmul(out, lhsT=lhsT, rhs=rhs, start=start, stop=stop)


def TR(out, in_, ident):
    return lambda e: e.transpose(out, in_, ident)


def MS(ap, v):
    return lambda e: e.memset(ap, v)


def DMA(out, in_):
    return lambda e: e.dma_start(out=out, in_=in_)


class _Stop(Exception):
    pass


def build_program(stop=None, limit=None):
    nc = bass.Bass("TRN2", target_bir_lowering=False)
    P = Prog()
    P.limit = limit

    def din(name, shape):
        return nc.dram_tensor(name, shape, F32, kind="ExternalInput").ap()

    xT = din("xT", [D, S])
    xs = din("xs", [SL, D])
    c_col = din("c_col", [128, 8])
    w_ada = din("w_ada", [D, 6 * D])
    b_ada = din("b_ada", [1, 6 * D])
    w1f = din("w1f", [D, 512])
    w1t = din("w1t", [D, 193])
    bfh = din("bfh", [128, 1])
    cosT = din("cosT", [128, S])
    sinT = din("sinT", [128, S])
    innerT_d = din("innerT", [128, 128])
    xi512_d = din("xi512", [128, 512])
    zeta_d = din("zeta", [128, 1])
    gch_d = din("gch", [128, 1])
    ident_d = din("ident", [128, 128])
    utri_d = din("utri", [128, 128])
    cmask_d = din("cmask", [128, 128])
    pm_d = din("pm", [128, 128])
    hmask_d = din("hmask", [128, 1])
    w_out = din("w_out", [D, D])
    w_up = din("w_up", [D, 2 * DFF])
    w_dn = din("w_dn", [DFF, D])
    convw = din("convw", [128, 2 * NCH * 3])
    convb = din("convb", [128, 2 * NCH])
    ln1g_d = din("ln1g", [1, D])
    ln1b_d = din("ln1b", [1, D])
    ln2g_d = din("ln2g", [1, D])
    ln2b_d = din("ln2b", [1, D])
    y = nc.dram_tensor("y", [SLAB, D], F32, kind="ExternalOutput").ap()

    w1f16 = nc.dram_tensor("w1f16", [D, 512], BF16).ap()
    w1t16 = nc.dram_tensor("w1t16", [D, 193], BF16).ap()
    wout16 = nc.dram_tensor("wout16", [D, D], BF16).ap()
    wup16 = nc.dram_tensor("wup16", [D, 2 * DFF], BF16).ap()
    wdn16 = nc.dram_tensor("wdn16", [DFF, D], BF16).ap()
    XR = 768
    BLK = NCORES * XR
    xin = [nc.dram_tensor("xin%d" % t, [XR, 128], BF16) for t in range(NT)]
    o1 = [nc.dram_tensor("o1_%d" % t, [4 * XR, 128], BF16) for t in range(NT)]
    O2 = nc.dram_tensor("O2", [(NT + 1) * BLK, 128], BF16)
    O2ap = O2.ap()
    GQ = [[0, 2, 4, 6], [1, 3, 5, 7]]
    GP = [[0, 1], [2, 3], [4, 5], [6, 7]]

    with ExitStack() as top:
        def sb(es, name, shape, dt=F32):
            return es.enter_context(nc.sbuf_tensor("s_" + name, shape, dt))

        def pst(es, name, shape, dt=F32):
            return es.enter_context(nc.psum_tensor("p_" + name, shape, dt))

        g1bc = sb(top, "g1bc", [128, D])
        g2bc = sb(top, "g2bc", [128, D])
        sc1p = sb(top, "sc1p", [128, 8])
        sh1c = sb(top, "sh1c", [128, 8])
        sc2p = sb(top, "sc2p", [128, 8])
        sh2c = sb(top, "sh2c", [128, 8])
        ident = sb(top, "ident", [128, 128])
        ident16 = sb(top, "ident16", [128, 128], BF16)
        ones = sb(top, "ones", [128, 512])
        B_const = Buf()
        B_mod = Buf()

        P.dma("sp", DMA(ident[:], ident_d), "ld_c0", writes=[B_const])
        P.op("pool", MS(ones[:], 1.0), writes=[B_const])
        P.op("dve", CP(ident16[:], ident[:]), reads=[B_const], writes=[B_const])

        B_w16a = Buf()
        B_w16 = Buf()

        def cast_dma(dst, src, rows, cols, sem, Bw, defer=None):
            cw_ = max(c for c in range(1, 2049) if cols % c == 0)
            if cw_ != cols:
                d2 = dst.rearrange("r (k c) -> (r k) c", c=cw_)
                s2 = src.rearrange("r (k c) -> (r k) c", c=cw_)
                n = rows * (cols // cw_)
            else:
                d2, s2, n = dst, src, rows
            step = 2048
            for r0 in range(0, n, step):
                r1 = min(n, r0 + step)
                fn_ = (lambda a, b_: (lambda: P.dma("pool", DMA(d2[a:b_, :], s2[a:b_, :]), sem, writes=[Bw])))(r0, r1)
                if defer is None:
                    fn_()
                else:
                    defer.append(fn_)

        if stop != -2:
            cast_dma(w1f16, w1f, D, 512, "ld_w16a", B_w16a)
            cast_dma(w1t16, w1t, D, 193, "ld_w16a", B_w16a)

        final = []
        try:
            if stop in (-1, -2):
                final.append(P.dma("sp", DMA(y[0:128, 0:128], ident[:]), "st_dbg", reads=[B_const, B_w16a]))
                raise _Stop()
            with ExitStack() as s0:
                ccol = sb(s0, "ccol", [128, 8])
                sil = sb(s0, "sil", [128, 8])
                tmp8 = sb(s0, "tmp8", [128, 8])
                rep = sb(s0, "rep", [128, 8, 128])
                wa = [sb(s0, "wa%d" % i, [128, 8, 512]) for i in range(2)]
                bb = [sb(s0, "bb%d" % i, [128, 512]) for i in range(2)]
                modbc = sb(s0, "modbc", [128, D])
                dtmp = sb(s0, "dtmp", [128, 128])
                psm = [pst(s0, "psm%d" % i, [128, 512]) for i in range(2)]
                B_wa = [Buf(), Buf()]
                B_bb = [Buf(), Buf()]
                B_psm = [Buf(True), Buf(True)]
                B_s0 = Buf()
                B_modbc = Buf()
                B_dtmp = Buf()

                P.dma("sp", DMA(ccol[:], c_col), "ld_c0b", writes=[B_s0])
                P.op("act", ACT(tmp8[:], ccol[:], AF.Exp, scale=-1.0), reads=[B_s0], writes=[B_s0])
                P.op("dve", TSC(tmp8[:], tmp8[:], 1.0, None, ALU.add), reads=[B_s0], writes=[B_s0])
                P.op("dve", lambda e: e.reciprocal(out=tmp8[:], in_=tmp8[:]), reads=[B_s0], writes=[B_s0])
                P.op("dve", TT(sil[:], ccol[:], tmp8[:], ALU.mult), reads=[B_s0], writes=[B_s0])
                for kc in range(8):
                    P.op("dve", TSC(rep[:, kc, :], ones[:, 0:128], sil[:, kc:kc + 1], None, ALU.mult),
                         reads=[B_s0, B_const], writes=[B_s0])
                if stop == -3:
                    final.append(P.dma("sp", DMA(y[0:128, 0:128], rep[:, 3, :]), "st_dbg", reads=[B_s0]))
                    raise _Stop()
                w_ada_v = w_ada.rearrange("(kc p) n -> p kc n", p=128)
                it = 0
                for g in range(6):
                    for n2 in range(2):
                        bi = it % 2
                        it += 1
                        c0 = g * D + n2 * 512
                        P.dma("sp", DMA(wa[bi][:], w_ada_v[:, :, c0:c0 + 512]), "ld_wa%d" % bi, writes=[B_wa[bi]])
                        P.dma("sp", DMA(bb[bi][:], b_ada[0:1, c0:c0 + 512].broadcast_to([128, 512])), "ld_bb%d" % bi,
                              writes=[B_bb[bi]])
                        for kc in range(8):
                            P.op("pe", MM(psm[bi][:], rep[:, kc, :], wa[bi][:, kc, :], kc == 0, kc == 7),
                                 reads=[B_s0, B_wa[bi]], writes=[B_psm[bi]], signal=(kc == 7))
                        P.op("dve", TT(modbc[:, n2 * 512:(n2 + 1) * 512], psm[bi][:], bb[bi][:], ALU.add),
                             reads=[B_psm[bi], B_bb[bi]], writes=[B_modbc])
                    if stop == -4:
                        final.append(P.dma("sp", DMA(y[0:128, :], modbc[:]), "st_dbg", reads=[B_modbc]))
                        raise _Stop()
                    if g in (0, 1, 3, 4):
                        dst = {0: sh1c, 1: sc1p, 3: sh2c, 4: sc2p}[g]
                        for cc in range(8):
                            P.op("dve", TT(dtmp[:], modbc[:, cc * 128:(cc + 1) * 128], ident[:], ALU.mult),
                                 reads=[B_modbc, B_const], writes=[B_dtmp])
                            P.op("dve", lambda e, cc=cc, dst=dst: e.reduce_sum(out=dst[:, cc:cc + 1], in_=dtmp[:], axis=AX.X),
                                 reads=[B_dtmp], writes=[B_mod])
                        if g in (1, 4):
                            P.op("dve", TSC(dst[:], dst[:], 1.0, None, ALU.add), reads=[B_mod], writes=[B_mod])
                        if stop == -5:
                            final.append(P.dma("sp", DMA(y[0:128, 0:8], dst[:]), "st_dbg", reads=[B_mod]))
                            raise _Stop()
                    else:
                        dst = g1bc if g == 2 else g2bc
                        P.op("dve", CP(dst[:], modbc[:]), reads=[B_modbc], writes=[B_mod])
            if stop == 0:
                final.append(P.dma("sp", DMA(y[0:128, :], g1bc[:]), "st_dbg", reads=[B_mod]))
                raise _Stop()
            P.set_fence()

            big_casts = []
            cast_dma(wout16, w_out, D, D, "ld_w16b", B_w16, big_casts)
            cast_dma(wup16, w_up, D, 2 * DFF, "ld_w16b", B_w16, big_casts)
            cast_dma(wdn16, w_dn, DFF, D, "ld_w16b", B_w16, big_casts)

            with ExitStack() as s1:
                xt = [sb(s1, "xt%d" % i, [128, 8, TS]) for i in range(2)]
                hT = [sb(s1, "hT%d" % i, [128, 8, TS], BF16) for i in range(2)]
                wf = sb(s1, "wf", [128, 8, 512], BF16)
                wt = sb(s1, "wt", [128, 8, 193], BF16)
                Kaug = sb(s1, "Kaug", [128, S], BF16)
                Vaug = sb(s1, "Vaug", [128, NKB, 128], BF16)
                Qaug = [sb(s1, "Qaug%d" % i, [128, TS], BF16) for i in range(2)]
                NPT = 4
                PT = [sb(s1, "PT%d" % i, [128, TS], BF16) for i in range(NPT)]
                NFcol = sb(s1, "NFcol", [128, NKB])
                biasT = [sb(s1, "biasT%d" % i, [128, NKB]) for i in range(2)]
                carc = sb(s1, "carc", [128, 1])
                carr = sb(s1, "carr", [128, 1])
                negb = sb(s1, "negb", [128, 1])
                fr1 = sb(s1, "fr1", [128, TS])
                fr2 = sb(s1, "fr2", [128, TS])
                fhi = sb(s1, "fhi", [128, TS], BF16)
                flo = sb(s1, "flo", [128, TS], BF16)
                spc = sb(s1, "spc", [128, 4])
                exc = sb(s1, "exc", [128, 5])
                cs = [sb(s1, "cs%d" % i, [128, 2, TS]) for i in range(2)]
                q16 = sb(s1, "q16", [128, TS], BF16)
                k16 = sb(s1, "k16", [128, TS], BF16)
                rt1 = sb(s1, "rt1", [128, TS])
                rt2 = sb(s1, "rt2", [128, TS])
                qr16 = sb(s1, "qr16", [128, TS], BF16)
                kr16 = sb(s1, "kr16", [128, TS], BF16)
                qxi16 = sb(s1, "qxi16", [128, TS], BF16)
                v16 = sb(s1, "v16", [128, 4, 64], BF16)
                gat = sb(s1, "gat", [128, 4, 64])
                gat2 = sb(s1, "gat2", [128, 4, 64])
                kz16 = sb(s1, "kz16", [128, 128], BF16)
                sTm16 = sb(s1, "sTm16", [128, 128], BF16)
                R32 = sb(s1, "R32", [128, 64])
                R16 = sb(s1, "R16", [128, 64], BF16)
                ro16 = [sb(s1, "ro16_%d" % i, [128, 4, 128], BF16) for i in range(2)]
                OT16 = [sb(s1, "OT16_%d" % i, [64, TS], BF16) for i in range(2)]
                rr = sb(s1, "rr", [128, TS])
                rhi = sb(s1, "rhi", [128, TS], BF16)
                rlo = sb(s1, "rlo", [128, TS], BF16)
                bcs = sb(s1, "bcs", [64, TS])
                sel16 = sb(s1, "sel16", [128, 64], BF16)
                utri = sb(s1, "utri", [128, 128])
                cm32 = sb(s1, "cm32", [128, 128])
                pm32 = sb(s1, "pm32", [128, 128])
                cmask16 = sb(s1, "cmask16", [128, 128], BF16)
                pm16 = sb(s1, "pm16", [128, 128], BF16)
                innerT = sb(s1, "innerT", [128, 128])
                xi512 = sb(s1, "xi512", [128, 512])
                zeta = sb(s1, "zeta", [128, 1])
                gch = sb(s1, "gch", [128, 1])
                zero16 = sb(s1, "zero16", [128, 128], BF16)

                psS = [pst(s1, "psS%d" % i, [128, 512]) for i in range(3)]
                psO = [pst(s1, "psO%d" % i, [128, 512]) for i in range(2)]
                psM = [pst(s1, "psM%d" % i, [128, 512]) for i in range(2)]
                psT = pst(s1, "psT", [128, 1024], BF16)
                B_psS = [Buf(True) for _ in range(3)]
                B_psO = [Buf(True) for _ in range(2)]
                B_psM = [Buf(True) for _ in range(2)]
                B_psT = Buf(True)
                mctr = [0]

                def misc_bank():
                    i = mctr[0] % 2
                    mctr[0] += 1
                    return psM[i], B_psM[i]

                B_c1 = Buf()
                B_xt = [[Buf() for _ in range(8)] for _ in range(2)]
                B_hT = [[Buf() for _ in range(8)] for _ in range(2)]
                B_K = [Buf() for _ in range(NT)]
                B_V = [Buf() for _ in range(NT)]
                B_Q = [Buf(), Buf()]
                B_PT = [Buf() for _ in range(NPT)]
                B_NF = Buf()
                B_bias = [Buf(), Buf()]
                B_carc = Buf()
                B_carr = Buf()
                B_fr = Buf()
                B_spc = Buf()
                B_cs = [Buf(), Buf()]
                B_q16, B_k16, B_rt1, B_rt2 = Buf(), Buf(), Buf(), Buf()
                B_qr, B_kr, B_qxi = Buf(), Buf(), Buf()
                B_v16, B_gat, B_kz, B_sTm, B_R32, B_R16 = Buf(), Buf(), Buf(), Buf(), Buf(), Buf()
                B_ro = [Buf(), Buf()]
                B_OT = [Buf(), Buf()]
                B_rr = Buf()
                B_bcs = Buf()
                B_xin = [Buf() for _ in range(NT)]
                B_o1 = [Buf() for _ in range(NT)]
                B_O2 = Buf()

                P.dma("sp", DMA(utri[:], utri_d), "ld_c", writes=[B_c1])
                P.dma("sp", DMA(cm32[:], cmask_d), "ld_c", writes=[B_c1])
                P.dma("sp", DMA(pm32[:], pm_d), "ld_c", writes=[B_c1])
                P.dma("sp", DMA(innerT[:], innerT_d), "ld_c", writes=[B_c1])
                P.dma("sp", DMA(xi512[:], xi512_d), "ld_c", writes=[B_c1])
                P.dma("sp", DMA(zeta[:], zeta_d), "ld_c", writes=[B_c1])
                P.dma("sp", DMA(gch[:], gch_d), "ld_c", writes=[B_c1])
                P.dma("sp", DMA(negb[:], bfh), "ld_c", writes=[B_c1])
                P.dma("sp", DMA(wf[:], w1f16.rearrange("(kc p) n -> p kc n", p=128)), "ld_c", reads=[B_w16a], writes=[B_c1])
                P.dma("sp", DMA(wt[:], w1t16.rearrange("(kc p) n -> p kc n", p=128)), "ld_c", reads=[B_w16a], writes=[B_c1])
                P.batch_done("ld_c", [B_c1])
                P.op("dve", TSC(negb[:], negb[:], -1.0, None, ALU.mult), reads=[B_c1], writes=[B_c1])
                P.op("dve", CP(cmask16[:], cm32[:]), reads=[B_c1], writes=[B_c1])
                P.op("dve", CP(pm16[:], pm32[:]), reads=[B_c1], writes=[B_c1])
                P.op("pool", MS(sel16[:], 0.0), writes=[B_c1])
                P.op("pool", MS(sel16[64:65, :], 1.0), writes=[B_c1])
                P.op("pool", MS(sel16[96:97, :], 1.0), writes=[B_c1])
                P.op("pool", MS(zero16[:], 0.0), writes=[B_c1])
                P.op("pool", MS(Kaug[64:97, :], 0.0), writes=[B_c1])
                P.op("pool", MS(Kaug[64:65, :], 1.0), writes=[B_c1])
                P.op("pool", MS(Kaug[96:97, :], 1.0), writes=[B_c1])
                P.op("pool", MS(Vaug[:, :, 64:128], 1.0), writes=[B_c1])
                P.op("pool", MS(Qaug[0][:], 0.0), writes=[B_Q[0]])
                P.op("pool", MS(Qaug[1][:], 0.0), writes=[B_Q[1]])
                P.op("dve", MS(carc[:], 0.0), writes=[B_carc])
                P.op("dve", MS(carr[:], 0.0), writes=[B_carr])
                P.op("dve", MS(R32[:], 0.0), writes=[B_R32])
                P.op("dve", MS(R16[:], 0.0), writes=[B_R16])
                P.op("dve", MS(exc[:], 0.0), writes=[B_spc])
                for k in range(NCORES):
                    P.dma("sp", DMA(O2ap[k * XR:k * XR + 128, :], zero16[:, :]), "st_z", reads=[B_c1], writes=[B_O2])
                    P.dma("sp", DMA(O2ap[k * XR + 128:k * XR + 256, :], zero16[:, :]), "st_z", reads=[B_c1], writes=[B_O2])
                    P.dma("sp", DMA(O2ap[k * XR + 736:k * XR + 768, :], zero16[0:32, :]), "st_z", reads=[B_c1], writes=[B_O2])

                def exchange(tt):
                    P.coll("pool", lambda e: e.collective_compute("AllGather", ALU.bypass, replica_groups=GQ,
                                                                  ins=[xin[tt].ap().opt()], outs=[o1[tt].ap().opt()]),
                           "cc1", reads=[B_xin[tt]], writes=[B_o1[tt]])
                    P.coll("pool", lambda e: e.collective_compute("AllGather", ALU.bypass, replica_groups=GP,
                                                                  ins=[o1[tt].ap().opt()],
                                                                  outs=[O2ap[(tt + 1) * BLK:(tt + 2) * BLK, :].opt()]),
                           "cc2", reads=[B_o1[tt]], writes=[B_O2])

                xT_v = xT.rearrange("(c p) t -> p c t", p=128)

                def load_x(t):
                    b = t % 2
                    t0 = t * TS
                    for hh in range(2):
                        P.dma("sp", DMA(xt[b][:, 4 * hh:4 * hh + 4, :], xT_v[:, 4 * hh:4 * hh + 4, t0:t0 + TS]),
                              "ld_x%d_%d" % (b, hh), writes=B_xt[b][4 * hh:4 * hh + 4])
                    P.dma("sp", DMA(cs[b][:, 0, :], cosT[:, t0:t0 + TS]), "ld_cs%d" % b, writes=[B_cs[b]])
                    P.dma("sp", DMA(cs[b][:, 1, :], sinT[:, t0:t0 + TS]), "ld_cs%d" % b, writes=[B_cs[b]])

                load_x(0)
                def zipg(g1, g2):
                    live = [g1, g2]
                    while live:
                        for g in list(live):
                            try:
                                next(g)
                                yield
                            except StopIteration:
                                live.remove(g)

                def prep(t):
                    b = t % 2
                    t0 = t * TS
                    nkb = 4 * t + 4
                    early = t < 10

                    def EV(out, in_):
                        return ("act", ACT(out, in_, AF.Copy)) if early else ("dve", CP(out, in_))

                    if t + 1 < NT:
                        load_x(t + 1)
                    for c in range(8):
                        if early and c % 2 == 1:
                            P.op("act", ACT(hT[b][:, c, :], xt[b][:, c, :], AF.Identity, bias=sh1c[:, c:c + 1], scale=sc1p[:, c:c + 1]),
                                 reads=[B_xt[b][c], B_mod], writes=[B_hT[b][c]])
                            yield
                            continue
                        fn = TSC(hT[b][:, c, :], xt[b][:, c, :], sc1p[:, c:c + 1], sh1c[:, c:c + 1], ALU.mult, ALU.add)
                        P.op("dve", fn, reads=[B_xt[b][c], B_mod], writes=[B_hT[b][c]])
                        if c % 2 == 1:
                            yield
                    yield

                    def fproj(g, M, bank):
                        ps, Bp = psM[bank], B_psM[bank]
                        for kc in range(8):
                            P.op("pe", MM(ps[0:M, :], wf[:, kc, g * 128:g * 128 + M], hT[b][:, kc, :], kc == 0, kc == 7),
                                 reads=[B_c1] + (B_hT[b] if kc == 7 else [B_hT[b][kc]]), writes=[Bp], signal=(kc == 7))
                        return ps, Bp

                    for half in range(2):
                        ps, Bp = psM[half], B_psM[half]
                        for bl in range(2):
                            blk = 2 * half + bl
                            for kc in range(8):
                                P.op("pe", MM(ps[:, bl * 256:bl * 256 + 193], hT[b][:, kc, blk * 128:(blk + 1) * 128],
                                              wt[:, kc, :], kc == 0, kc == 7),
                                     reads=[B_c1] + (B_hT[b] if (kc == 7 and bl == 1) else [B_hT[b][kc]]), writes=[Bp],
                                     signal=(kc == 7 and bl == 1))
                        yield
                        yield
                        psv = ps[:, :].rearrange("p (b c) -> p b c", c=256)
                        kb0 = 4 * t + 2 * half
                        P.op("dve", CP(Vaug[:, kb0:kb0 + 2, 0:64], psv[:, :, 0:64]), reads=[Bp], writes=[B_V[t]])
                        P.op("dve", CP(spc[:, 2 * half:2 * half + 2], psv[:, :, 192]), reads=[Bp], writes=[B_spc])
                        yield
                        P.op("dve", CP(v16[:, 2 * half:2 * half + 2, :], psv[:, :, 64:128]), reads=[Bp], writes=[B_v16])
                        P.op("dve", CP(gat[:, 2 * half:2 * half + 2, :], psv[:, :, 128:192]), reads=[Bp], writes=[B_gat])
                        yield

                    def chainA():
                        psA, BA = fproj(0, 97, 0)
                        P.op("act", ACT(spc[:], spc[:], AF.Exp, bias=negb[:, 0:1], scale=-1.0), reads=[B_spc, B_c1], writes=[B_spc])
                        P.op("act", ACT(spc[:], spc[:], AF.Ln, bias=1.0), reads=[B_spc], writes=[B_spc])
                        yield
                        yield
                        P.op(*EV(Qaug[b][0:64, :], psA[0:64, :]), reads=[BA], writes=[B_Q[b]])
                        P.op("dve", CP(fr2[64:97, :], psA[64:97, :]), reads=[BA], writes=[B_fr])
                        yield
                        psB, BB = fproj(1, 64, 0)
                        P.op("act", ACT(fr1[64:97, :], fr2[64:97, :], AF.Exp, bias=negb[64:97, 0:1], scale=-1.0),
                             reads=[B_fr, B_c1], writes=[B_fr])
                        P.op("act", ACT(fr1[64:97, :], fr1[64:97, :], AF.Ln, bias=1.0), reads=[B_fr], writes=[B_fr])
                        yield
                        yield
                        P.op(*EV(Kaug[0:64, t0:t0 + TS], psB[0:64, :]), reads=[BB], writes=[B_K[t]])
                        P.op("dve", lambda e: e.tensor_tensor_scan(out=fr2[64:97, :], data0=ones[64:97, :], data1=fr1[64:97, :],
                                                                   initial=carr[64:97, 0:1], op0=ALU.mult, op1=ALU.add),
                             reads=[B_fr, B_carr, B_const], writes=[B_fr])
                        yield
                        P.op("dve", TSC(fr1[64:97, :], fr2[64:97, :], carr[64:97, 0:1], -8.0, ALU.subtract, ALU.mult),
                             reads=[B_fr, B_carr], writes=[B_fr])
                        P.op("dve", CP(carr[64:97, 0:1], fr2[64:97, TS - 1:TS]), reads=[B_fr], writes=[B_carr])
                        P.op("dve", CP(fhi[64:97, :], fr1[64:97, :]), reads=[B_fr], writes=[B_fr])
                        yield
                        P.op("dve", TT(flo[64:97, :], fr1[64:97, :], fhi[64:97, :], ALU.subtract), reads=[B_fr], writes=[B_fr])
                        P.op("dve", CP(Qaug[b][64:65, :], fhi[64:65, :]), reads=[B_fr], writes=[B_Q[b]])
                        P.op("dve", CP(Qaug[b][96:97, :], flo[96:97, :]), reads=[B_fr], writes=[B_Q[b]])
                        yield
                        for bl in range(1, 5):
                            P.op("dve", TT(exc[:, bl:bl + 1], exc[:, bl - 1:bl], spc[:, bl - 1:bl], ALU.add),
                                 reads=[B_spc], writes=[B_spc])
                        yield
                        yield
                        psF, BF = psM[0], B_psM[0]
                        P.op("pe", MM(psF[:, 0:4], utri[:], spc[:], True, False), reads=[B_c1, B_spc], writes=[BF], signal=False)
                        P.op("pe", MM(psF[:, 0:4], ones[:, 0:128], exc[:, 0:4], False, True), reads=[B_c1, B_spc, B_const], writes=[BF])
                        P.op("pe", MM(psF[:, 8:9], ones[:, 0:128], exc[:, 4:5], True, True), reads=[B_spc, B_const], writes=[BF])
                        yield
                        yield
                        P.op("dve", TSC(NFcol[:, 4 * t:4 * t + 4], psF[:, 0:4], carc[:, 0:1], None, ALU.add),
                             reads=[BF, B_carc], writes=[B_NF])
                        P.op("dve", TSC(biasT[b][:, 0:nkb], NFcol[:, 0:nkb], carc[:, 0:1], None, ALU.subtract),
                             reads=[B_NF, B_carc], writes=[B_bias[b]])
                        P.op("dve", TT(carc[:], carc[:], psF[:, 8:9], ALU.add), reads=[BF, B_carc], writes=[B_carc])
                        yield

                    def chainB():
                        psC, BC = fproj(2, 128, 1)
                        P.op("act", ACT(gat2[:], gat[:], AF.Exp, scale=-1.0), reads=[B_gat], writes=[B_gat])
                        yield
                        yield
                        P.op(*EV(q16[:], psC[:]), reads=[BC], writes=[B_q16])
                        yield
                        psD, BD = fproj(3, 128, 1)
                        P.op("dve", TSC(gat2[:], gat2[:], 1.0, None, ALU.add), reads=[B_gat], writes=[B_gat])
                        P.op("dve", lambda e: e.reciprocal(out=gat2[:], in_=gat2[:]), reads=[B_gat], writes=[B_gat])
                        yield
                        yield
                        P.op(*EV(k16[:], psD[:]), reads=[BD], writes=[B_k16])
                        P.op("dve", TT(ro16[b][:, :, 64:128], gat[:], gat2[:], ALU.mult), reads=[B_gat], writes=[B_ro[b]])
                        yield
                        for (src16, Bsrc, dst16, Bdst) in ((q16, B_q16, qr16, B_qr), (k16, B_k16, kr16, B_kr)):
                            psP, BP = psM[1], B_psM[1]
                            P.op("pe", MM(psP[:], pm16[:], src16[:], True, True), reads=[B_c1, Bsrc], writes=[BP])
                            P.op("dve", TT(rt1[:], src16[:], cs[b][:, 0, :], ALU.mult), reads=[Bsrc, B_cs[b]], writes=[B_rt1])
                            yield
                            yield
                            P.op("dve", TT(rt2[:], psP[:], cs[b][:, 1, :], ALU.mult), reads=[BP, B_cs[b]], writes=[B_rt2])
                            yield
                            P.op("dve", TT(dst16[:], rt1[:], rt2[:], ALU.add), reads=[B_rt1, B_rt2], writes=[Bdst])
                            yield
                        P.op("dve", TT(qxi16[:], qr16[:], xi512[:], ALU.mult), reads=[B_qr, B_c1], writes=[B_qxi])
                        yield
                        for ci in range(4):
                            csl = slice(ci * 128, (ci + 1) * 128)
                            P.op("pe", TR(psT[:, 0:128], kr16[:, csl], ident16[:]), reads=[B_kr, B_const], writes=[B_psT])
                            psR, BR = psM[1], B_psM[1]
                            P.op("pe", MM(psR[:, 0:128], kr16[:, csl], qr16[:, csl], True, True), reads=[B_kr, B_qr], writes=[BR])
                            yield
                            P.op("dve", TSC(kz16[:], psT[:, 0:128], zeta[:, 0:1], None, ALU.mult), reads=[B_psT, B_c1], writes=[B_kz])
                            P.op("dve", TT(sTm16[:], psR[:, 0:128], innerT[:], ALU.mult), reads=[BR, B_c1], writes=[B_sTm])
                            yield
                            yield
                            P.op("pe", MM(psR[:, 128:192], sTm16[:], v16[:, ci, :], True, False), reads=[B_sTm, B_v16], writes=[BR],
                                 signal=False)
                            P.op("pe", MM(psR[:, 128:192], qxi16[:, csl], R16[:], False, True), reads=[B_sTm, B_v16, B_qxi, B_R16],
                                 writes=[BR], signal=False)
                            P.op("pe", MM(psR[:, 256:320], kz16[:], v16[:, ci, :], True, True), reads=[B_kz, B_v16, B_sTm, B_qxi, B_R16],
                                 writes=[BR])
                            yield
                            yield
                            P.op("dve", CP(ro16[b][:, ci, 0:64], psR[:, 128:192]), reads=[BR], writes=[B_ro[b]])
                            P.op("dve", STT(R32[:], R32[:], gch[:, 0:1], psR[:, 256:320], ALU.mult, ALU.add),
                                 reads=[BR, B_c1], writes=[B_R32])
                            P.op("dve", CP(R16[:], R32[:]), reads=[B_R32], writes=[B_R16])
                            yield
                        P.dma("sp", DMA(xin[t].ap()[256:768, :].rearrange("(b p) f -> p b f", p=128), ro16[b][:]),
                              "st_ro%d" % b, reads=[B_ro[b]], writes=[B_xin[t]])

                    yield from zipg(chainA(), chainB())

                def finalize(t):
                    b = t % 2
                    psOb, BOb = psO[b], B_psO[b]
                    P.op("dve", lambda e, psOb=psOb: e.reciprocal(out=rr[64:97, :], in_=psOb[64:97, :]), reads=[BOb], writes=[B_rr])
                    yield
                    P.op("dve", CP(rhi[64:97, :], rr[64:97, :]), reads=[B_rr], writes=[B_rr])
                    P.op("dve", TT(rlo[64:97, :], rr[64:97, :], rhi[64:97, :], ALU.subtract), reads=[B_rr], writes=[B_rr])
                    P.op("dve", CP(rhi[96:97, :], rlo[96:97, :]), reads=[B_rr], writes=[B_rr])
                    yield
                    yield
                    yield
                    psN, BN = psM[0], B_psM[0]
                    P.op("pe", MM(psN[0:64, :], sel16[64:97, :], rhi[64:97, :], True, True), reads=[B_c1, B_rr], writes=[BN])
                    yield
                    yield
                    P.op("dve", CP(bcs[:], psN[0:64, :]), reads=[BN], writes=[B_bcs])
                    P.op("dve", TT(OT16[b][:], psOb[0:64, :], bcs[:], ALU.mult), reads=[BOb, B_bcs], writes=[B_OT[b]])
                    yield
                    P.dma("sp", DMA(xin[t].ap()[0:256, :].rearrange("(d four) c -> d (four c)", four=4), OT16[b][:]),
                          "st_ot%d" % b, reads=[B_OT[b]], writes=[B_xin[t]])
                    exchange(t)

                def chain(*gens):
                    for g in gens:
                        for _ in g:
                            yield

                NHOPS = 90
                for _ in prep(0):
                    pass
                for t in range(NT):
                    b = t % 2
                    t0 = t * TS
                    nkb = 4 * t + 4
                    gens = []
                    if t >= 1:
                        gens.append(finalize(t - 1))
                    if t + 1 < NT:
                        gens.append(prep(t + 1))
                    gen = chain(*gens)
                    per = -(-NHOPS // nkb)
                    psOb, BOb = psO[b], B_psO[b]

                    def emit_S(kb):
                        si = kb % 3
                        pi = kb % NPT
                        j = kb - 4 * t
                        n0 = 0 if j < 0 else 128 * j
                        kcs = slice(kb * 128, (kb + 1) * 128)
                        tk = kb // 4
                        if j < 0:
                            P.op("pe", MM(psS[si][:, :], Kaug[0:97, kcs], Qaug[b][0:97, :], True, True),
                                 reads=[B_K[tk], B_c1, B_Q[b]], writes=[B_psS[si]])
                        else:
                            P.op("pe", MM(psS[si][:, n0:n0 + 128], Kaug[0:97, kcs], Qaug[b][0:97, n0:n0 + 128], True, False),
                                 reads=[B_K[tk], B_c1, B_Q[b]], writes=[B_psS[si]], signal=False)
                            P.op("pe", MM(psS[si][:, n0:n0 + 128], ident16[:], cmask16[:], False, True),
                                 reads=[B_c1, B_const, B_K[tk], B_Q[b]], writes=[B_psS[si]], signal=(n0 + 128 >= TS))
                            if n0 + 128 < TS:
                                P.op("pe", MM(psS[si][:, n0 + 128:TS], Kaug[0:97, kcs], Qaug[b][0:97, n0 + 128:TS], True, True),
                                     reads=[B_K[tk], B_c1, B_Q[b]], writes=[B_psS[si]])
                        P.op("act", ACT(PT[pi][:, n0:TS], psS[si][:, n0:TS], AF.Exp, bias=biasT[b][:, kb:kb + 1], scale=0.125),
                             reads=[B_psS[si], B_bias[b]], writes=[B_PT[pi]])

                    def emit_PV(kb):
                        pi = kb % NPT
                        j = kb - 4 * t
                        n0 = 0 if j < 0 else 128 * j
                        tk = kb // 4
                        P.op("pe", MM(psOb[:, n0:TS], Vaug[:, kb, :], PT[pi][:, n0:TS], kb == 0, kb == nkb - 1),
                             reads=[B_V[tk], B_c1, B_PT[pi]], writes=[BOb])

                    LA = 2
                    for step in range(nkb + LA):
                        if step < nkb:
                            emit_S(step)
                        if step - LA >= 0:
                            emit_PV(step - LA)
                        for _ in range(per):
                            next(gen, None)
                    for _ in gen:
                        pass
                    if big_casts and (t >= NT // 4 or NT - t <= len(big_casts)):
                        big_casts.pop(0)()
                while big_casts:
                    big_casts.pop(0)()
                for _ in finalize(NT - 1):
                    pass

                pass

            P.set_fence()
            B_ag = B_O2
            if stop in (1, 2):
                final.append(P.dma("sp", DMA(y[0:128, :], g1bc[:]), "st_dbg", reads=[B_mod, B_O2]))
                raise _Stop()

            PIDS = {}
            with ExitStack() as s2:
                QT = min(512, SLAB)
                NQ = SLAB // QT
                NBL = QT // 128
                lng = [sb(s2, "lng%d" % i, [128, D]) for i in range(4)]
                wo16 = sb(s2, "wo16", [128, 8, D], BF16)
                cw = sb(s2, "cw", [128, 2 * NCH * 3])
                cb = sb(s2, "cb", [128, 2 * NCH])
                hmask = sb(s2, "hmask", [128, 1])
                neghalf = sb(s2, "neghalf", [128, 4])
                foxT = sb(s2, "foxT", [128, 4, QT], BF16)
                retin = [sb(s2, "retin%d" % i, [128, 8, 128], BF16) for i in range(2)]
                oc = [sb(s2, "oc%d" % i, [128, 512]) for i in range(3)]
                rn16 = [sb(s2, "rn16_%d" % i, [128, 512], BF16) for i in range(2)]
                retT = sb(s2, "retT", [128, 4, QT], BF16)
                xsb = [sb(s2, "xsb%d" % i, [128, D]) for i in range(2)]
                zt = [sb(s2, "zt%d" % i, [128, D]) for i in range(2)]
                x1q = [sb(s2, "x1q%d" % i, [128, 4, D]) for i in range(2)]
                h2Tq = [sb(s2, "h2Tq%d" % i, [128, 8, QT], BF16) for i in range(2)]
                h2Th = sb(s2, "h2Th", [128, 8, HALO], BF16)
                st6 = [sb(s2, "st6_%d" % i, [128, 4, 6]) for i in range(3)]
                mv = [sb(s2, "mv%d" % i, [128, 4, 2]) for i in range(3)]
                rstd = [sb(s2, "rstd%d" % i, [128, 4]) for i in range(3)]
                NWU, NWD = 4, 4
                GW = 3
                wu = [sb(s2, "wu%d" % i, [128, 8, 2, GW * 128], BF16) for i in range(2)]
                wd = [sb(s2, "wd%d" % i, [128, 512], BF16) for i in range(NWD)]
                ub = [sb(s2, "ub%d" % i, [128, QT + 2]) for i in range(4)]
                yv = [sb(s2, "yv%d" % i, [128, QT]) for i in range(6)]
                uprev = sb(s2, "uprev", [128, 2 * NCH, 2])
                act16 = sb(s2, "act16", [128, NCH, QT], BF16)
                ot = [sb(s2, "ot%d" % i, [128, D]) for i in range(2)]

                pA = [pst(s2, "pA%d" % i, [128, 512]) for i in range(4)]
                pU = [pst(s2, "pU%d" % i, [128, 512]) for i in range(2)]
                pTf = pst(s2, "pTf", [128, 512])
                pTb = pst(s2, "pTb", [128, 1024], BF16)
                B_pA = [Buf(True) for _ in range(4)]
                B_pU = [Buf(True) for _ in range(2)]
                B_pTf, B_pTb = Buf(True), Buf(True)
                B_c2 = Buf()
                B_fox, B_retT = Buf(), Buf()
                B_oc = [Buf() for _ in range(3)]
                B_rn = [Buf(), Buf()]
                B_retin = [Buf(), Buf()]
                B_xsb = [Buf(), Buf()]
                B_zt = [Buf(), Buf()]
                B_st = [Buf() for _ in range(3)]
                B_x1q = [[Buf() for _ in range(4)] for _ in range(2)]
                B_h2Tq = [Buf(), Buf()]
                B_h2Th = Buf()
                B_wu = [Buf(), Buf()]
                B_wd = [Buf() for _ in range(NWD)]
                B_ub = [Buf() for _ in range(4)]
                B_yv = [Buf() for _ in range(6)]
                B_uprev = Buf()
                B_act = [Buf() for _ in range(NCH)]
                B_ot = [Buf(), Buf()]

                for i, src in enumerate((ln1g_d, ln1b_d, ln2g_d, ln2b_d)):
                    P.dma("sp", DMA(lng[i][:], src[0:1, :].broadcast_to([128, D])), "ld_c", writes=[B_c2])
                P.dma("sp", DMA(wo16[:], wout16.rearrange("(kc p) n -> p kc n", p=128)), "ld_c", reads=[B_w16], writes=[B_c2])
                P.dma("sp", DMA(cw[:], convw), "ld_c", writes=[B_c2])
                P.dma("sp", DMA(cb[:], convb), "ld_c", writes=[B_c2])
                P.dma("sp", DMA(hmask[:], hmask_d), "ld_c", writes=[B_c2])
                P.batch_done("ld_c", [B_c2])
                P.op("pool", MS(neghalf[:], -0.5), writes=[B_c2])

                TPS = SLAB // TS
                assert QT == TS and TPS >= 1

                def tile_blk(e, qidx, eng):
                    if eng not in PIDS:
                        PIDS[eng] = e.partition_id() * (TPS * BLK)
                    return O2ap[(qidx + 1) * BLK:(NT + 1) * BLK, :][bass.ds(PIDS[eng], BLK), :]
                wup_v = wup16.rearrange("(kc p) n -> p kc n", p=128)
                wdn_v = wdn16.rearrange("(i p) n -> p i n", p=128)

                def wait(n):
                    for _ in range(n):
                        yield

                def layer_norm(eng_obj_unused, src, np_, gi, dst, Bsrc, Bdst, k=0):
                    for hh in range(2):
                        P.op("dve", lambda e, hh=hh: e.bn_stats(out=st6[k][0:np_, hh, :], in_=src[0:np_, hh * 512:(hh + 1) * 512]),
                             reads=[Bsrc], writes=[B_st[k]])
                    P.op("dve", lambda e: e.bn_aggr(out=mv[k][0:np_, 0, :], in_=st6[k][0:np_, 0:2, :].rearrange("p a s -> p (a s)")),
                         reads=[B_st[k]], writes=[B_st[k]])
                    P.op("dve", TSC(rstd[k][0:np_, 0:1], mv[k][0:np_, 0, 1:2], LN_EPS, None, ALU.add), reads=[B_st[k]], writes=[B_st[k]])
                    yield from wait(4)
                    P.op("pool", TT(rstd[k][0:np_, 0:1], rstd[k][0:np_, 0:1], neghalf[0:np_, 0:1], ALU.pow), reads=[B_st[k], B_c2], writes=[B_st[k]])
                    yield from wait(2)
                    P.op("dve", TSC(rstd[k][0:np_, 1:2], mv[k][0:np_, 0, 0:1], rstd[k][0:np_, 0:1], -1.0, ALU.mult, ALU.mult),
                         reads=[B_st[k]], writes=[B_st[k]])
                    yield from wait(2)
                    P.op("act", ACT(src[0:np_, :], src[0:np_, :], AF.Identity, bias=rstd[k][0:np_, 1:2], scale=rstd[k][0:np_, 0:1]),
                         reads=[B_st[k]], writes=[Bsrc])
                    yield from wait(4)
                    P.op("dve", TT(src[0:np_, :], src[0:np_, :], lng[gi][0:np_, :], ALU.mult), reads=[B_c2], writes=[Bsrc])
                    yield from wait(4)
                    P.op("pool", TT(dst[0:np_, :], src[0:np_, :], lng[gi + 1][0:np_, :], ALU.add), reads=[Bsrc, B_c2], writes=[Bdst])
                    yield from wait(16)

                def token_block(loc0, np_, x1dst, Bx1, hcol0, ri, qidx, tok0, h2dst, Bh2):
                    def dyn_ret(e):
                        src = tile_blk(e, qidx, "sp").rearrange("(k x) f -> x k f", x=XR)
                        return e.dma_start(out=retin[ri][0:np_, :, :], in_=src[256 + tok0:256 + tok0 + np_, :, :])
                    P.dma("sp", dyn_ret, "ld_ret%d" % ri, reads=[B_ag], writes=[B_retin[ri]])
                    P.dma("sp", DMA(xsb[ri][0:np_, :], xs[loc0:loc0 + np_, :]), "ld_xs%d" % ri, writes=[B_xsb[ri]])
                    yield from wait(8)
                    oc_, rn_, zt_ = oc[ri], rn16[ri], zt[ri]
                    Boc, Brn, Bzt, Bst = B_oc[ri], B_rn[ri], B_zt[ri], B_st[ri]
                    P.op("pool", CP(oc_[0:np_, :].rearrange("p (h a d) -> p a h d", a=2, d=64),
                                    retin[ri][0:np_, :, 0:64].rearrange("p (a h) d -> p a h d", a=2)),
                         reads=[B_retin[ri]], writes=[Boc])
                    yield from wait(6)
                    for h in range(4):
                        P.op("dve", lambda e, h=h: e.bn_stats(out=st6[ri][0:np_, h, :], in_=oc_[0:np_, h * 128:(h + 1) * 128]),
                             reads=[Boc], writes=[Bst])
                    for h in range(4):
                        P.op("dve", lambda e, h=h: e.bn_aggr(out=mv[ri][0:np_, h, :], in_=st6[ri][0:np_, h, :]), reads=[Bst], writes=[Bst])
                    P.op("dve", TSC(rstd[ri][0:np_, :], mv[ri][0:np_, :, 1], GN_EPS, None, ALU.add), reads=[Bst], writes=[Bst])
                    yield from wait(4)
                    P.op("pool", TT(rstd[ri][0:np_, :], rstd[ri][0:np_, :], neghalf[0:np_, :], ALU.pow), reads=[Bst, B_c2], writes=[Bst])
                    yield from wait(2)
                    for h in range(4):
                        P.op("dve", TSC(oc_[0:np_, h * 128:(h + 1) * 128], oc_[0:np_, h * 128:(h + 1) * 128], mv[ri][0:np_, h, 0:1],
                                        rstd[ri][0:np_, h:h + 1], ALU.subtract, ALU.mult), reads=[Bst], writes=[Boc])
                    yield from wait(3)
                    P.op("pool", TT(rn_[0:np_, :].rearrange("p (h a d) -> p a h d", a=2, d=64),
                                    oc_[0:np_, :].rearrange("p (h a d) -> p a h d", a=2, d=64),
                                    retin[ri][0:np_, :, 64:128].rearrange("p (a h) d -> p a h d", a=2), ALU.mult),
                         reads=[Boc, B_retin[ri]], writes=[Brn])
                    yield from wait(14)
                    for h in range(4):
                        P.op("pe", TR(pTb[:, h * 128:h * 128 + np_], rn_[0:np_, h * 128:(h + 1) * 128], ident16[0:np_, 0:np_]),
                             reads=[Brn, B_const], writes=[B_pTb])
                    P.op("dve", CP(retT[:, :, hcol0:hcol0 + np_], pTb[:, 0:512].rearrange("p (h c) -> p h c", c=128)[:, :, 0:np_]),
                         reads=[B_pTb], writes=[B_retT])
                    yield from wait(8)
                    for n2 in range(2):
                        pa, Bpa = pA[n2], B_pA[n2]
                        for c in range(8):
                            lhs = foxT[:, c, hcol0:hcol0 + np_] if c < 4 else retT[:, c - 4, hcol0:hcol0 + np_]
                            P.op("pe", MM(pa[0:np_, :], lhs, wo16[:, c, n2 * 512:(n2 + 1) * 512], c == 0, c == 7),
                                 reads=[B_fox, B_retT, B_c2], writes=[Bpa], signal=(c == 7))
                        P.op("dve", TT(zt_[0:np_, n2 * 512:(n2 + 1) * 512], pa[0:np_, :], g1bc[0:np_, n2 * 512:(n2 + 1) * 512], ALU.mult),
                             reads=[Bpa, B_mod], writes=[Bzt])
                        yield
                    yield from wait(2)
                    P.op("dve", STT(zt_[0:np_, :], xsb[ri][0:np_, :], ALPHA, zt_[0:np_, :], ALU.mult, ALU.add),
                         reads=[B_xsb[ri]], writes=[Bzt])
                    yield from wait(4)
                    yield from layer_norm(None, zt_, np_, 0, x1dst, Bzt, Bx1, k=ri)
                    for half in range(2):
                        for cc in range(4):
                            c = 4 * half + cc
                            P.op("pe", TR(pTf[:, cc * 128:cc * 128 + np_], x1dst[0:np_, c * 128:(c + 1) * 128], ident[0:np_, 0:np_]),
                                 reads=[Bx1, B_const], writes=[B_pTf])
                        for cc in range(4):
                            c = 4 * half + cc
                            P.op("dve" if cc % 2 == 0 else "act",
                                 TSC(h2dst[:, c, hcol0:hcol0 + np_], pTf[:, cc * 128:cc * 128 + np_], sc2p[:, c:c + 1], sh2c[:, c:c + 1],
                                     ALU.mult, ALU.add) if cc % 2 == 0 else
                                 ACT(h2dst[:, c, hcol0:hcol0 + np_], pTf[:, cc * 128:cc * 128 + np_], AF.Identity,
                                     bias=sh2c[:, c:c + 1], scale=sc2p[:, c:c + 1]),
                                 reads=[B_pTf, B_mod], writes=[Bh2])
                        yield from wait(2)

                def load_fox(qidx, tok0, n):
                    for two in range(2):
                        def dyn_fox(e, two=two):
                            v = tile_blk(e, qidx, "act").rearrange("(k x) f -> k x f", x=XR)[4 * two:4 * two + 4, 0:256, :]
                            v = v.rearrange("c (d four) f -> d c (four f)", four=4)
                            return e.dma_start(out=foxT[64 * two:64 * two + 64, :, 0:n], in_=v[:, :, tok0:tok0 + n])
                        P.dma("act", dyn_fox, "ld_fox", reads=[B_ag], writes=[B_fox])

                up_banks = [(pU[0], B_pU[0]), (pU[1], B_pU[1]), (pA[2], B_pA[2]), (pA[3], B_pA[3])]
                upctr = [0]
                upq = [0]

                def up_chunk(i, part, ncols, wbuf, src, Bsrc):
                    ps, Bp = up_banks[upctr[0] % len(up_banks)]
                    upctr[0] += 1
                    g, o = i // GW, i % GW
                    w = wu[g % 2]
                    for kc in range(8):
                        P.op("pe", MM(ps[:, 0:ncols], w[:, kc, part, o * 128:(o + 1) * 128], src[:, kc, 0:ncols], kc == 0, kc == 7),
                             reads=[Bsrc, B_wu[g % 2]], writes=[Bp], signal=(kc == 7))
                    return ps, Bp

                NG = -(-NCH // GW)

                def load_wu_group(g):
                    w = wu[g % 2]
                    c0 = g * GW * 128
                    nc_ = min(GW * 128, DFF - c0)
                    for part in range(2):
                        P.dma("sp", DMA(w[:, :, part, 0:nc_], wup_v[:, :, part * DFF + c0:part * DFF + c0 + nc_]),
                              "ld_wu%d" % (g % 2), reads=[B_w16], writes=[B_wu[g % 2]])

                wu_loaded = set()

                def load_wu(i):
                    if (upq[0], i) in wu_loaded:
                        return
                    wu_loaded.add((upq[0], i))
                    if i == 0:
                        load_wu_group(0)
                    if i % GW == 0 and i // GW + 1 < NG:
                        load_wu_group(i // GW + 1)

                load_fox(-1, TS - HALO, HALO)
                for _ in token_block(0, HALO, x1q[0][:, 0, :], B_x1q[0][0], 0, 0, -1, TS - HALO, h2Th, B_h2Th):
                    pass

                def zip2(g1, g2):
                    live = [g1, g2]
                    while live:
                        for g in list(live):
                            try:
                                next(g)
                                yield
                            except StopIteration:
                                live.remove(g)

                def quarter_blocks(q):
                    pq = q % 2
                    load_fox(q, 0, QT)
                    gens_ = [token_block(HALO + q * QT + bl * 128, 128, x1q[pq][:, bl, :], B_x1q[pq][bl], bl * 128, bl % 2,
                                         q, bl * 128, h2Tq[pq], B_h2Tq[pq]) for bl in range(NBL)]
                    for k in range(0, NBL, 2):
                        if k + 1 < NBL:
                            yield from zip2(gens_[k], gens_[k + 1])
                        else:
                            yield from gens_[k]

                def ln2_hops(q):
                    x1_, Bx1_ = x1q[q % 2], B_x1q[q % 2]
                    for bl in range(NBL):
                        oi = bl % 2
                        for _ in layer_norm(None, x1_[:, bl, :], 128, 2, ot[oi], Bx1_[bl], B_ot[oi], k=2):
                            yield
                        r0 = q * QT + bl * 128
                        final.append(P.dma("sp", DMA(y[r0:r0 + 128, :], ot[oi][:]), "st_y%d" % oi, reads=[B_ot[oi]]))
                        yield

                for _ in quarter_blocks(0):
                    pass
                ln2_pending = iter(())
                for q in range(NQ):
                    upq[0] = q
                    pq = q % 2
                    x1 = x1q[pq]
                    B_x1 = B_x1q[pq]
                    gen = chain(ln2_pending, quarter_blocks(q + 1) if q + 1 < NQ else iter(()))
                    def pre_u(i, part):
                        ch = part * NCH + i
                        bi_ = 2 * (i % 2) + part
                        P.op("dve", CP(ub[bi_][:, 0:2], uprev[:, ch, :]), reads=[B_uprev], writes=[B_ub[bi_]])

                    def halo_u(i):
                        for part in range(2):
                            ps, Bp = up_chunk(i, part, HALO, None, h2Th, B_h2Th)
                            P.op("dve", TSC(uprev[:, part * NCH + i, :], ps[:, HALO - 2:HALO], hmask[:, 0:1], None, ALU.mult),
                                 reads=[Bp, B_c2], writes=[B_uprev])

                    if q == 0:
                        load_wu(0)
                        halo_u(0)
                    pre_u(0, 0)

                    def geglu(i):
                        ya, yb_ = yv[2 * (i % 3)], yv[2 * (i % 3) + 1]
                        P.op("act", ACT(ya[:], ya[:], AF.Gelu), reads=[], writes=[B_yv[2 * (i % 3)]])
                        P.op("pool", TT(act16[:, i, :], ya[:], yb_[:], ALU.mult),
                             reads=[B_yv[2 * (i % 3)], B_yv[2 * (i % 3) + 1]], writes=[B_act[i]])

                    for i in range(NCH):
                        load_wu(i)
                        if q == 0 and i + 1 < NCH:
                            halo_u(i + 1)
                        for part in range(2):
                            ch = part * NCH + i
                            ps, Bp = up_chunk(i, part, QT, None, h2Tq[pq], B_h2Tq[pq])
                            bi_ = 2 * (i % 2) + part
                            u = ub[bi_]
                            yi_ = 2 * (i % 3) + part
                            yy = yv[yi_]
                            P.op("act", ACT(u[:, 2:QT + 2], ps[:, 0:QT], AF.Copy), reads=[Bp], writes=[B_ub[bi_]])
                            P.op("act", ACT(yy[:], ps[:, 0:QT], AF.Identity, bias=cb[:, ch:ch + 1], scale=cw[:, 3 * ch + 2:3 * ch + 3]),
                                 reads=[Bp, B_c2], writes=[B_yv[yi_]])
                            if True:
                                if part == 0:
                                    pre_u(i, 1)
                                elif i + 1 < NCH:
                                    pre_u(i + 1, 0)
                            P.op("dve", STT(yy[:], u[:, 1:QT + 1], cw[:, 3 * ch + 1:3 * ch + 2], yy[:], ALU.mult, ALU.add),
                                 reads=[B_ub[bi_], B_c2], writes=[B_yv[yi_]])
                            P.op("dve", STT(yy[:], u[:, 0:QT], cw[:, 3 * ch:3 * ch + 1], yy[:], ALU.mult, ALU.add),
                                 reads=[B_ub[bi_], B_c2], writes=[B_yv[yi_]])
                            P.op("dve", CP(uprev[:, ch, :], u[:, QT:QT + 2]), reads=[B_ub[bi_]], writes=[B_uprev])
                            for _ in range(13):
                                next(gen, None)
                        if i >= 1:
                            geglu(i - 1)
                    geglu(NCH - 1)
                    for _ in gen:
                        pass
                    for n2 in range(2):
                        for i in range(NCH):
                            wi = (n2 * NCH + i) % NWD
                            P.dma("sp", DMA(wd[wi][:, :], wdn_v[:, i, n2 * 512:(n2 + 1) * 512]), "ld_wd%d" % wi,
                                  reads=[B_w16], writes=[B_wd[wi]])
                            for bl in range(NBL):
                                P.op("pe", MM(pA[bl][:, :], act16[:, i, bl * 128:(bl + 1) * 128], wd[wi][:, :],
                                              i == 0, i == NCH - 1),
                                     reads=[B_act[i], B_wd[wi]], writes=[B_pA[bl]], signal=(i == NCH - 1 or bl == NBL - 1))
                        for bl in range(NBL):
                            cs_ = slice(n2 * 512, (n2 + 1) * 512)
                            P.op("dve", TT(oc[2][:, :], pA[bl][:, :], g2bc[:, cs_], ALU.mult), reads=[B_pA[bl], B_mod], writes=[B_oc[2]])
                            P.op("dve", STT(x1[:, bl, cs_], x1[:, bl, cs_], ALPHA, oc[2][:, :], ALU.mult, ALU.add),
                                 reads=[B_oc[2]], writes=[B_x1[bl]])
                    ln2_pending = ln2_hops(q)
                for _ in ln2_pending:
                    pass

        except _Stop:
            pass
        with ExitStack() as fin:
            sems = {n: fin.enter_context(nc.semaphore(n)) for n in P.sem_names()}
            block = fin.enter_context(nc.Block())
            last = {}
            if limit is not None:
                P.set_fence()
                final = list(final) + list(P.fence)
            for h in final:
                if h is not None:
                    last[h[0]] = max(last.get(h[0], 0), h[1])

            @block.sync
            def _(e):
                P.replay("sp", e, sems, final_waits=list(last.items()))

            @block.tensor
            def _(e):
                P.replay("pe", e, sems)

            @block.scalar
            def _(e):
                P.replay("act", e, sems)

            @block.vector
            def _(e):
                P.replay("dve", e, sems)

            @block.gpsimd
            def _(e):
                P.replay("pool", e, sems)
    nc._oplog = P.log
    return nc


def _const_tables():
    f32 = np.float32
    pos = np.arange(S, dtype=f32)
    inv_freq = (f32(10000.0) ** (-np.arange(0, 128, 2, dtype=f32) / f32(128))).astype(f32)
    ang = (pos[:, None] * inv_freq[None, :]).astype(f32)
    ang = np.concatenate([ang, ang], axis=-1)
    cosT = np.ascontiguousarray(np.cos(ang).astype(f32).T)
    sin = np.sin(ang).astype(f32)
    sgn = np.concatenate([-np.ones(64, f32), np.ones(64, f32)])
    sinT = np.ascontiguousarray((sin * sgn[None, :]).T.astype(f32))
    ident = np.eye(128, dtype=f32)
    utri = np.triu(np.ones((128, 128), f32))
    kk = np.arange(128)
    cmask = np.where(kk[:, None] <= kk[None, :], f32(0.0), f32(-1.0e6)).astype(f32)
    pm = np.zeros((128, 128), f32)
    for i in range(128):
        pm[(i + 64) % 128, i] = 1.0
    return cosT, sinT, ident, utri, cmask, pm


def _ret_tables(h):
    f32 = np.float32
    lg = np.log1p(-np.exp2(f32(-5.0 - h))).astype(f32)
    idx = np.arange(128, dtype=f32)
    c = f32(128.0 ** -0.5)
    diff = idx[:, None] - idx[None, :]
    inner = np.where(diff >= 0, np.exp(np.maximum(diff, 0.0) * lg), 0.0).astype(f32)
    innerT = np.ascontiguousarray((inner * c).T.astype(f32))
    xi = (np.exp((idx + 1.0) * lg) * c).astype(f32)
    xi512 = np.ascontiguousarray(np.tile(xi[None, :], (128, 4)).astype(f32))
    zeta = np.exp((127.0 - idx) * lg).astype(f32).reshape(128, 1)
    gch = np.full((128, 1), np.exp(f32(128.0) * lg), f32)
    return innerT, xi512, zeta, gch


def _make_in_maps(inputs):
    f32 = np.float32
    x = np.asarray(inputs["x"], f32)[0]
    w_in = np.asarray(inputs["w_in"], f32)[0]
    b_f = np.asarray(inputs["b_f"], f32)[0]
    conv_w = np.asarray(inputs["conv_w"], f32)[0]
    conv_b = np.asarray(inputs["conv_b"], f32)[0]
    cosT, sinT, ident, utri, cmask, pm = _const_tables()
    xT = np.ascontiguousarray(x.T)
    c_col = np.ascontiguousarray(np.asarray(inputs["c"], f32)[0].reshape(8, 128).T)
    convw = np.ascontiguousarray(conv_w.T.reshape(2 * NCH, 128, 3).transpose(1, 0, 2).reshape(128, 2 * NCH * 3))
    convb = np.ascontiguousarray(conv_b.reshape(2 * NCH, 128).T)
    shared = {
        "xT": xT, "c_col": c_col,
        "w_ada": np.ascontiguousarray(np.asarray(inputs["w_ada"], f32)[0]),
        "b_ada": np.ascontiguousarray(np.asarray(inputs["b_ada"], f32)[0].reshape(1, -1)),
        "cosT": cosT, "sinT": sinT, "ident": ident, "utri": utri, "cmask": cmask, "pm": pm,
        "w_out": np.ascontiguousarray(np.asarray(inputs["w_out"], f32)[0]),
        "w_up": np.ascontiguousarray(np.asarray(inputs["w_up"], f32)[0]),
        "w_dn": np.ascontiguousarray(np.asarray(inputs["w_down"], f32)[0]),
        "convw": convw, "convb": convb,
        "ln1g": np.asarray(inputs["ln1_g"], f32).reshape(1, D), "ln1b": np.asarray(inputs["ln1_b"], f32).reshape(1, D),
        "ln2g": np.asarray(inputs["ln2_g"], f32).reshape(1, D), "ln2b": np.asarray(inputs["ln2_b"], f32).reshape(1, D),
    }
    FQ, FK, FV, FF, RQ, RK, RV, RG = 0, 512, 1024, 1536, 1544, 2056, 2568, 3080
    maps = []
    for j in range(NCORES):
        hr, half = j // 2, j % 2
        w1f = np.zeros((D, 512), f32)
        w1f[:, 0:64] = w_in[:, FQ + 64 * j:FQ + 64 * j + 64]
        w1f[:, 64] = w_in[:, FF + j]
        w1f[:, 96] = w_in[:, FF + j]
        w1f[:, 128:192] = w_in[:, FK + 64 * j:FK + 64 * j + 64]
        w1f[:, 256:384] = w_in[:, RQ + 128 * hr:RQ + 128 * hr + 128]
        w1f[:, 384:512] = w_in[:, RK + 128 * hr:RK + 128 * hr + 128]
        o = 128 * hr + 64 * half
        w1t = np.concatenate([w_in[:, FV + 64 * j:FV + 64 * j + 64], w_in[:, RV + o:RV + o + 64],
                              w_in[:, RG + o:RG + o + 64], w_in[:, FF + j:FF + j + 1]], axis=1)
        innerT, xi512, zeta, gch = _ret_tables(hr)
        xs = np.zeros((SL, D), f32)
        lo = j * SLAB - HALO
        if lo < 0:
            xs[HALO:] = x[0:SLAB]
        else:
            xs[:] = x[lo:lo + SL]
        m = dict(shared)
        m.update({
            "xs": xs, "w1f": w1f, "w1t": np.ascontiguousarray(w1t),
            "bfh": np.full((128, 1), b_f[j], f32),
            "innerT": innerT, "xi512": xi512, "zeta": zeta, "gch": gch,
            "hmask": np.full((128, 1), 0.0 if j == 0 else 1.0, f32),
        })
        maps.append(m)
    return maps


_NC_CACHE = {}


def kernel(**inputs):
    S_ = int(np.asarray(inputs["x"]).shape[1])
    if S_ != S:
        _cfg(S_)
    if S_ not in _NC_CACHE:
        _NC_CACHE[S_] = build_program()
    nc = _NC_CACHE[S_]
    in_maps = _make_in_maps(inputs)
    res = run_bass_kernel_spmd(nc, in_maps, core_ids=list(range(NCORES)))
    out = np.concatenate([np.asarray(res.results[j]["y"], np.float32) for j in range(NCORES)], axis=0)
    return out.reshape(1, S, D)
```

```python
from contextlib import ExitStack

import numpy as np
import concourse.bass as bass
import concourse.mybir as mybir
from concourse.bass_utils import run_bass_kernel_spmd

F32 = mybir.dt.float32
BF16 = mybir.dt.bfloat16
ALU = mybir.AluOpType
AF = mybir.ActivationFunctionType
AX = mybir.AxisListType

NCORES = 8
D = 1024
TS = 512
HALO = 32


def _cfg(S_):
    global S, NT, NKB, SLAB, SL
    S = S_
    NT = S // TS
    NKB = S // 128
    SLAB = S // NCORES
    SL = SLAB + HALO


_cfg(16384)
DFF = 2816
NCH = DFF // 128
ALPHA = 2.0 ** 0.25
LN_EPS = 1e-5
GN_EPS = 1e-6
ENGS = ["pe", "act", "dve", "pool", "sp"]


class Buf:
    __slots__ = ("w", "r", "excl")

    def __init__(self, excl=False):
        self.w = None
        self.r = {}
        self.excl = excl


class Prog:
    def __init__(self):
        self.q = {e: [] for e in ENGS}
        self.cnt = {e: 0 for e in ENGS}
        self.dcnt = {}
        self.waited = {e: {} for e in ENGS}
        self.fence = []
        self.nops = 0
        self.limit = None
        self.log = []

    def _skip(self):
        self.nops += 1
        import sys as _s
        self.log.append((self.nops, _s._getframe(2).f_lineno))
        return self.limit is not None and self.nops > self.limit

    def _deps(self, eng, reads, writes, extra):
        deps = list(extra) + list(self.fence)
        for b in reads:
            deps.append(b.w)
            if b.excl:
                deps.extend(b.r.items())
        for b in writes:
            deps.append(b.w)
            deps.extend(b.r.items())
        need = {}
        for d in deps:
            if d is None:
                continue
            s, v = d
            if eng == "pe" and s == "e_pe":
                continue
            if s == "e_" + eng and self.cnt[eng] - v >= 3:
                continue
            if need.get(s, 0) < v:
                need[s] = v
        ws = []
        for s, v in need.items():
            if self.waited[eng].get(s, 0) >= v:
                continue
            self.waited[eng][s] = v
            ws.append((s, v))
        return ws

    def _commit(self, h, reads, writes):
        for b in reads:
            if b.excl:
                b.w = h
                b.r = {}
            elif b.r.get(h[0], 0) < h[1]:
                b.r[h[0]] = h[1]
        for b in writes:
            b.w = h
            b.r = {}

    def op(self, eng, fn, reads=(), writes=(), extra=(), signal=True):
        if self._skip():
            return None
        ws = self._deps(eng, reads, writes, extra)
        if signal:
            self.cnt[eng] += 1
            h = ("e_" + eng, self.cnt[eng])
            self.q[eng].append((ws, fn, ("e_" + eng, 1)))
            self._commit(h, reads, writes)
            return h
        self.q[eng].append((ws, fn, None))
        return None

    def dma(self, eng, fn, sem, reads=(), writes=(), extra=()):
        if self._skip():
            return None
        ws = self._deps(eng, reads, writes, extra)
        self.dcnt[sem] = self.dcnt.get(sem, 0) + 16
        h = (sem, self.dcnt[sem])
        self.q[eng].append((ws, fn, (sem, 16)))
        self._commit(h, reads, writes)
        return h

    def coll(self, eng, fn, sem, reads=(), writes=()):
        if self._skip():
            return None
        ws = self._deps(eng, reads, writes, ())
        self.dcnt[sem] = self.dcnt.get(sem, 0) + 1
        h = (sem, self.dcnt[sem])
        self.q[eng].append((ws, fn, (sem, 1)))
        self._commit(h, reads, writes)
        return h

    def batch_done(self, sem, bufs):
        if sem not in self.dcnt:
            return
        h = (sem, self.dcnt[sem])
        for b in bufs:
            b.w = h

    def set_fence(self):
        f = [("e_" + e, self.cnt[e]) for e in ENGS if self.cnt[e] > 0]
        f += [(s, v) for s, v in self.dcnt.items()]
        self.fence = f

    def sem_names(self):
        return ["e_" + e for e in ENGS] + sorted(self.dcnt.keys())

    def replay(self, eng, engine_obj, sems, final_waits=()):
        for ws, fn, inc in self.q[eng]:
            for s, v in ws[:-1]:
                engine_obj.wait_ge(sems[s], v)
            inst = fn(engine_obj)
            if ws:
                s, v = ws[-1]
                try:
                    inst.wait_op(sems[s], v, "sem-ge")
                except Exception:
                    raise
            if inc is not None:
                inst.then_inc(sems[inc[0]], inc[1])
        for s, v in final_waits:
            engine_obj.wait_ge(sems[s], v)


def TSC(out, in0, s1, s2, op0, op1=None):
    if op1 is None:
        return lambda e: e.tensor_scalar(out=out, in0=in0, scalar1=s1, scalar2=None, op0=op0)
    return lambda e: e.tensor_scalar(out=out, in0=in0, scalar1=s1, scalar2=s2, op0=op0, op1=op1)


def TT(out, in0, in1, op):
    return lambda e: e.tensor_tensor(out=out, in0=in0, in1=in1, op=op)


def STT(out, in0, scalar, in1, op0, op1):
    return lambda e: e.scalar_tensor_tensor(out=out, in0=in0, scalar=scalar, in1=in1, op0=op0, op1=op1)


def CP(out, in_):
    return lambda e: e.tensor_copy(out=out, in_=in_)


def ACT(out, in_, func, bias=None, scale=None):
    kw = {}
    if bias is not None:
        kw["bias"] = bias
    if scale is not None:
        kw["scale"] = scale
    return lambda e: e.activation(out=out, in_=in_, func=func, **kw)


def MM(out, lhsT, rhs, start, stop):
    return lambda e: e.matmul(out, lhsT=lhsT, rhs=rhs, start=start, stop=stop)


def TR(out, in_, ident):
    return lambda e: e.transpose(out, in_, ident)


def MS(ap, v):
    return lambda e: e.memset(ap, v)


def DMA(out, in_):
    return lambda e: e.dma_start(out=out, in_=in_)


class _Stop(Exception):
    pass


def build_program(stop=None, limit=None):
    nc = bass.Bass("TRN2", target_bir_lowering=False)
    P = Prog()
    P.limit = limit

    def din(name, shape):
        return nc.dram_tensor(name, shape, F32, kind="ExternalInput").ap()

    xT = din("xT", [D, S])
    xs = din("xs", [SL, D])
    c_col = din("c_col", [128, 8])
    w_ada = din("w_ada", [D, 6 * D])
    b_ada = din("b_ada", [1, 6 * D])
    w1f = din("w1f", [D, 512])
    w1t = din("w1t", [D, 193])
    bfh = din("bfh", [128, 1])
    cosT = din("cosT", [128, S])
    sinT = din("sinT", [128, S])
    innerT_d = din("innerT", [128, 128])
    xi512_d = din("xi512", [128, 512])
    zeta_d = din("zeta", [128, 1])
    gch_d = din("gch", [128, 1])
    ident_d = din("ident", [128, 128])
    utri_d = din("utri", [128, 128])
    cmask_d = din("cmask", [128, 128])
    pm_d = din("pm", [128, 128])
    hmask_d = din("hmask", [128, 1])
    w_out = din("w_out", [D, D])
    w_up = din("w_up", [D, 2 * DFF])
    w_dn = din("w_dn", [DFF, D])
    convw = din("convw", [128, 2 * NCH * 3])
    convb = din("convb", [128, 2 * NCH])
    ln1g_d = din("ln1g", [1, D])
    ln1b_d = din("ln1b", [1, D])
    ln2g_d = din("ln2g", [1, D])
    ln2b_d = din("ln2b", [1, D])
    y = nc.dram_tensor("y", [SLAB, D], F32, kind="ExternalOutput").ap()

    w1f16 = nc.dram_tensor("w1f16", [D, 512], BF16).ap()
    w1t16 = nc.dram_tensor("w1t16", [D, 193], BF16).ap()
    wout16 = nc.dram_tensor("wout16", [D, D], BF16).ap()
    wup16 = nc.dram_tensor("wup16", [D, 2 * DFF], BF16).ap()
    wdn16 = nc.dram_tensor("wdn16", [DFF, D], BF16).ap()
    XR = 768
    BLK = NCORES * XR
    xin = [nc.dram_tensor("xin%d" % t, [XR, 128], BF16) for t in range(NT)]
    o1 = [nc.dram_tensor("o1_%d" % t, [4 * XR, 128], BF16) for t in range(NT)]
    O2 = nc.dram_tensor("O2", [(NT + 1) * BLK, 128], BF16)
    O2ap = O2.ap()
    GQ = [[0, 2, 4, 6], [1, 3, 5, 7]]
    GP = [[0, 1], [2, 3], [4, 5], [6, 7]]

    with ExitStack() as top:
        def sb(es, name, shape, dt=F32):
            return es.enter_context(nc.sbuf_tensor("s_" + name, shape, dt))

        def pst(es, name, shape, dt=F32):
            return es.enter_context(nc.psum_tensor("p_" + name, shape, dt))

        g1bc = sb(top, "g1bc", [128, D])
        g2bc = sb(top, "g2bc", [128, D])
        sc1p = sb(top, "sc1p", [128, 8])
        sh1c = sb(top, "sh1c", [128, 8])
        sc2p = sb(top, "sc2p", [128, 8])
        sh2c = sb(top, "sh2c", [128, 8])
        ident = sb(top, "ident", [128, 128])
        ident16 = sb(top, "ident16", [128, 128], BF16)
        ones = sb(top, "ones", [128, 512])
        B_const = Buf()
        B_mod = Buf()

        P.dma("sp", DMA(ident[:], ident_d), "ld_c0", writes=[B_const])
        P.op("pool", MS(ones[:], 1.0), writes=[B_const])
        P.op("dve", CP(ident16[:], ident[:]), reads=[B_const], writes=[B_const])

        B_w16a = Buf()
        B_w16 = Buf()

        def cast_dma(dst, src, rows, cols, sem, Bw, defer=None):
            cw_ = max(c for c in range(1, 2049) if cols % c == 0)
            if cw_ != cols:
                d2 = dst.rearrange("r (k c) -> (r k) c", c=cw_)
                s2 = src.rearrange("r (k c) -> (r k) c", c=cw_)
                n = rows * (cols // cw_)
            else:
                d2, s2, n = dst, src, rows
            step = 2048
            for r0 in range(0, n, step):
                r1 = min(n, r0 + step)
                fn_ = (lambda a, b_: (lambda: P.dma("pool", DMA(d2[a:b_, :], s2[a:b_, :]), sem, writes=[Bw])))(r0, r1)
                if defer is None:
                    fn_()
                else:
                    defer.append(fn_)

        if stop != -2:
            cast_dma(w1f16, w1f, D, 512, "ld_w16a", B_w16a)
            cast_dma(w1t16, w1t, D, 193, "ld_w16a", B_w16a)

        final = []
        try:
            if stop in (-1, -2):
                final.append(P.dma("sp", DMA(y[0:128, 0:128], ident[:]), "st_dbg", reads=[B_const, B_w16a]))
                raise _Stop()
            with ExitStack() as s0:
                ccol = sb(s0, "ccol", [128, 8])
                sil = sb(s0, "sil", [128, 8])
                tmp8 = sb(s0, "tmp8", [128, 8])
                rep = sb(s0, "rep", [128, 8, 128])
                wa = [sb(s0, "wa%d" % i, [128, 8, 512]) for i in range(2)]
                bb = [sb(s0, "bb%d" % i, [128, 512]) for i in range(2)]
                modbc = sb(s0, "modbc", [128, D])
                dtmp = sb(s0, "dtmp", [128, 128])
                psm = [pst(s0, "psm%d" % i, [128, 512]) for i in range(2)]
                B_wa = [Buf(), Buf()]
                B_bb = [Buf(), Buf()]
                B_psm = [Buf(True), Buf(True)]
                B_s0 = Buf()
                B_modbc = Buf()
                B_dtmp = Buf()

                P.dma("sp", DMA(ccol[:], c_col), "ld_c0b", writes=[B_s0])
                P.op("act", ACT(tmp8[:], ccol[:], AF.Exp, scale=-1.0), reads=[B_s0], writes=[B_s0])
                P.op("dve", TSC(tmp8[:], tmp8[:], 1.0, None, ALU.add), reads=[B_s0], writes=[B_s0])
                P.op("dve", lambda e: e.reciprocal(out=tmp8[:], in_=tmp8[:]), reads=[B_s0], writes=[B_s0])
                P.op("dve", TT(sil[:], ccol[:], tmp8[:], ALU.mult), reads=[B_s0], writes=[B_s0])
                for kc in range(8):
                    P.op("dve", TSC(rep[:, kc, :], ones[:, 0:128], sil[:, kc:kc + 1], None, ALU.mult),
                         reads=[B_s0, B_const], writes=[B_s0])
                if stop == -3:
                    final.append(P.dma("sp", DMA(y[0:128, 0:128], rep[:, 3, :]), "st_dbg", reads=[B_s0]))
                    raise _Stop()
                w_ada_v = w_ada.rearrange("(kc p) n -> p kc n", p=128)
                it = 0
                for g in range(6):
                    for n2 in range(2):
                        bi = it % 2
                        it += 1
                        c0 = g * D + n2 * 512
                        P.dma("sp", DMA(wa[bi][:], w_ada_v[:, :, c0:c0 + 512]), "ld_wa%d" % bi, writes=[B_wa[bi]])
                        P.dma("sp", DMA(bb[bi][:], b_ada[0:1, c0:c0 + 512].broadcast_to([128, 512])), "ld_bb%d" % bi,
                              writes=[B_bb[bi]])
                        for kc in range(8):
                            P.op("pe", MM(psm[bi][:], rep[:, kc, :], wa[bi][:, kc, :], kc == 0, kc == 7),
                                 reads=[B_s0, B_wa[bi]], writes=[B_psm[bi]], signal=(kc == 7))
                        P.op("dve", TT(modbc[:, n2 * 512:(n2 + 1) * 512], psm[bi][:], bb[bi][:], ALU.add),
                             reads=[B_psm[bi], B_bb[bi]], writes=[B_modbc])
                    if stop == -4:
                        final.append(P.dma("sp", DMA(y[0:128, :], modbc[:]), "st_dbg", reads=[B_modbc]))
                        raise _Stop()
                    if g in (0, 1, 3, 4):
                        dst = {0: sh1c, 1: sc1p, 3: sh2c, 4: sc2p}[g]
                        for cc in range(8):
                            P.op("dve", TT(dtmp[:], modbc[:, cc * 128:(cc + 1) * 128], ident[:], ALU.mult),
                                 reads=[B_modbc, B_const], writes=[B_dtmp])
                            P.op("dve", lambda e, cc=cc, dst=dst: e.reduce_sum(out=dst[:, cc:cc + 1], in_=dtmp[:], axis=AX.X),
                                 reads=[B_dtmp], writes=[B_mod])
                        if g in (1, 4):
                            P.op("dve", TSC(dst[:], dst[:], 1.0, None, ALU.add), reads=[B_mod], writes=[B_mod])
                        if stop == -5:
                            final.append(P.dma("sp", DMA(y[0:128, 0:8], dst[:]), "st_dbg", reads=[B_mod]))
                            raise _Stop()
                    else:
                        dst = g1bc if g == 2 else g2bc
                        P.op("dve", CP(dst[:], modbc[:]), reads=[B_modbc], writes=[B_mod])
            if stop == 0:
                final.append(P.dma("sp", DMA(y[0:128, :], g1bc[:]), "st_dbg", reads=[B_mod]))
                raise _Stop()
            P.set_fence()

            big_casts = []
            cast_dma(wout16, w_out, D, D, "ld_w16b", B_w16, big_casts)
            cast_dma(wup16, w_up, D, 2 * DFF, "ld_w16b", B_w16, big_casts)
            cast_dma(wdn16, w_dn, DFF, D, "ld_w16b", B_w16, big_casts)

            with ExitStack() as s1:
                xt = [sb(s1, "xt%d" % i, [128, 8, TS]) for i in range(2)]
                hT = [sb(s1, "hT%d" % i, [128, 8, TS], BF16) for i in range(2)]
                wf = sb(s1, "wf", [128, 8, 512], BF16)
                wt = sb(s1, "wt", [128, 8, 193], BF16)
                Kaug = sb(s1, "Kaug", [128, S], BF16)
                Vaug = sb(s1, "Vaug", [128, NKB, 128], BF16)
                Qaug = [sb(s1, "Qaug%d" % i, [128, TS], BF16) for i in range(2)]
                NPT = 4
                PT = [sb(s1, "PT%d" % i, [128, TS], BF16) for i in range(NPT)]
                NFcol = sb(s1, "NFcol", [128, NKB])
                biasT = [sb(s1, "biasT%d" % i, [128, NKB]) for i in range(2)]
                carc = sb(s1, "carc", [128, 1])
                carr = sb(s1, "carr", [128, 1])
                negb = sb(s1, "negb", [128, 1])
                fr1 = sb(s1, "fr1", [128, TS])
                fr2 = sb(s1, "fr2", [128, TS])
                fhi = sb(s1, "fhi", [128, TS], BF16)
                flo = sb(s1, "flo", [128, TS], BF16)
                spc = sb(s1, "spc", [128, 4])
                exc = sb(s1, "exc", [128, 5])
                cs = [sb(s1, "cs%d" % i, [128, 2, TS]) for i in range(2)]
                q16 = sb(s1, "q16", [128, TS], BF16)
                k16 = sb(s1, "k16", [128, TS], BF16)
                rt1 = sb(s1, "rt1", [128, TS])
                rt2 = sb(s1, "rt2", [128, TS])
                qr16 = sb(s1, "qr16", [128, TS], BF16)
                kr16 = sb(s1, "kr16", [128, TS], BF16)
                qxi16 = sb(s1, "qxi16", [128, TS], BF16)
                v16 = sb(s1, "v16", [128, 4, 64], BF16)
                gat = sb(s1, "gat", [128, 4, 64])
                gat2 = sb(s1, "gat2", [128, 4, 64])
                kz16 = sb(s1, "kz16", [128, 128], BF16)
                sTm16 = sb(s1, "sTm16", [128, 128], BF16)
                R32 = sb(s1, "R32", [128, 64])
                R16 = sb(s1, "R16", [128, 64], BF16)
                ro16 = [sb(s1, "ro16_%d" % i, [128, 4, 128], BF16) for i in range(2)]
                OT16 = [sb(s1, "OT16_%d" % i, [64, TS], BF16) for i in range(2)]
                rr = sb(s1, "rr", [128, TS])
                rhi = sb(s1, "rhi", [128, TS], BF16)
                rlo = sb(s1, "rlo", [128, TS], BF16)
                bcs = sb(s1, "bcs", [64, TS])
                sel16 = sb(s1, "sel16", [128, 64], BF16)
                utri = sb(s1, "utri", [128, 128])
                cm32 = sb(s1, "cm32", [128, 128])
                pm32 = sb(s1, "pm32", [128, 128])
                cmask16 = sb(s1, "cmask16", [128, 128], BF16)
                pm16 = sb(s1, "pm16", [128, 128], BF16)
                innerT = sb(s1, "innerT", [128, 128])
                xi512 = sb(s1, "xi512", [128, 512])
                zeta = sb(s1, "zeta", [128, 1])
                gch = sb(s1, "gch", [128, 1])
                zero16 = sb(s1, "zero16", [128, 128], BF16)

                psS = [pst(s1, "psS%d" % i, [128, 512]) for i in range(3)]
                psO = [pst(s1, "psO%d" % i, [128, 512]) for i in range(2)]
                psM = [pst(s1, "psM%d" % i, [128, 512]) for i in range(2)]
                psT = pst(s1, "psT", [128, 1024], BF16)
                B_psS = [Buf(True) for _ in range(3)]
                B_psO = [Buf(True) for _ in range(2)]
                B_psM = [Buf(True) for _ in range(2)]
                B_psT = Buf(True)
                mctr = [0]

                def misc_bank():
                    i = mctr[0] % 2
                    mctr[0] += 1
                    return psM[i], B_psM[i]

                B_c1 = Buf()
                B_xt = [[Buf() for _ in range(8)] for _ in range(2)]
                B_hT = [[Buf() for _ in range(8)] for _ in range(2)]
                B_K = [Buf() for _ in range(NT)]
                B_V = [Buf() for _ in range(NT)]
                B_Q = [Buf(), Buf()]
                B_PT = [Buf() for _ in range(NPT)]
                B_NF = Buf()
                B_bias = [Buf(), Buf()]
                B_carc = Buf()
                B_carr = Buf()
                B_fr = Buf()
                B_spc = Buf()
                B_cs = [Buf(), Buf()]
                B_q16, B_k16, B_rt1, B_rt2 = Buf(), Buf(), Buf(), Buf()
                B_qr, B_kr, B_qxi = Buf(), Buf(), Buf()
                B_v16, B_gat, B_kz, B_sTm, B_R32, B_R16 = Buf(), Buf(), Buf(), Buf(), Buf(), Buf()
                B_ro = [Buf(), Buf()]
                B_OT = [Buf(), Buf()]
                B_rr = Buf()
                B_bcs = Buf()
                B_xin = [Buf() for _ in range(NT)]
                B_o1 = [Buf() for _ in range(NT)]
                B_O2 = Buf()

                P.dma("sp", DMA(utri[:], utri_d), "ld_c", writes=[B_c1])
                P.dma("sp", DMA(cm32[:], cmask_d), "ld_c", writes=[B_c1])
                P.dma("sp", DMA(pm32[:], pm_d), "ld_c", writes=[B_c1])
                P.dma("sp", DMA(innerT[:], innerT_d), "ld_c", writes=[B_c1])
                P.dma("sp", DMA(xi512[:], xi512_d), "ld_c", writes=[B_c1])
                P.dma("sp", DMA(zeta[:], zeta_d), "ld_c", writes=[B_c1])
                P.dma("sp", DMA(gch[:], gch_d), "ld_c", writes=[B_c1])
                P.dma("sp", DMA(negb[:], bfh), "ld_c", writes=[B_c1])
                P.dma("sp", DMA(wf[:], w1f16.rearrange("(kc p) n -> p kc n", p=128)), "ld_c", reads=[B_w16a], writes=[B_c1])
                P.dma("sp", DMA(wt[:], w1t16.rearrange("(kc p) n -> p kc n", p=128)), "ld_c", reads=[B_w16a], writes=[B_c1])
                P.batch_done("ld_c", [B_c1])
                P.op("dve", TSC(negb[:], negb[:], -1.0, None, ALU.mult), reads=[B_c1], writes=[B_c1])
                P.op("dve", CP(cmask16[:], cm32[:]), reads=[B_c1], writes=[B_c1])
                P.op("dve", CP(pm16[:], pm32[:]), reads=[B_c1], writes=[B_c1])
                P.op("pool", MS(sel16[:], 0.0), writes=[B_c1])
                P.op("pool", MS(sel16[64:65, :], 1.0), writes=[B_c1])
                P.op("pool", MS(sel16[96:97, :], 1.0), writes=[B_c1])
                P.op("pool", MS(zero16[:], 0.0), writes=[B_c1])
                P.op("pool", MS(Kaug[64:97, :], 0.0), writes=[B_c1])
                P.op("pool", MS(Kaug[64:65, :], 1.0), writes=[B_c1])
                P.op("pool", MS(Kaug[96:97, :], 1.0), writes=[B_c1])
                P.op("pool", MS(Vaug[:, :, 64:128], 1.0), writes=[B_c1])
                P.op("pool", MS(Qaug[0][:], 0.0), writes=[B_Q[0]])
                P.op("pool", MS(Qaug[1][:], 0.0), writes=[B_Q[1]])
                P.op("dve", MS(carc[:], 0.0), writes=[B_carc])
                P.op("dve", MS(carr[:], 0.0), writes=[B_carr])
                P.op("dve", MS(R32[:], 0.0), writes=[B_R32])
                P.op("dve", MS(R16[:], 0.0), writes=[B_R16])
                P.op("dve", MS(exc[:], 0.0), writes=[B_spc])
                for k in range(NCORES):
                    P.dma("sp", DMA(O2ap[k * XR:k * XR + 128, :], zero16[:, :]), "st_z", reads=[B_c1], writes=[B_O2])
                    P.dma("sp", DMA(O2ap[k * XR + 128:k * XR + 256, :], zero16[:, :]), "st_z", reads=[B_c1], writes=[B_O2])
                    P.dma("sp", DMA(O2ap[k * XR + 736:k * XR + 768, :], zero16[0:32, :]), "st_z", reads=[B_c1], writes=[B_O2])

                def exchange(tt):
                    P.coll("pool", lambda e: e.collective_compute("AllGather", ALU.bypass, replica_groups=GQ,
                                                                  ins=[xin[tt].ap().opt()], outs=[o1[tt].ap().opt()]),
                           "cc1", reads=[B_xin[tt]], writes=[B_o1[tt]])
                    P.coll("pool", lambda e: e.collective_compute("AllGather", ALU.bypass, replica_groups=GP,
                                                                  ins=[o1[tt].ap().opt()],
                                                                  outs=[O2ap[(tt + 1) * BLK:(tt + 2) * BLK, :].opt()]),
                           "cc2", reads=[B_o1[tt]], writes=[B_O2])

                xT_v = xT.rearrange("(c p) t -> p c t", p=128)

                def load_x(t):
                    b = t % 2
                    t0 = t * TS
                    for hh in range(2):
                        P.dma("sp", DMA(xt[b][:, 4 * hh:4 * hh + 4, :], xT_v[:, 4 * hh:4 * hh + 4, t0:t0 + TS]),
                              "ld_x%d_%d" % (b, hh), writes=B_xt[b][4 * hh:4 * hh + 4])
                    P.dma("sp", DMA(cs[b][:, 0, :], cosT[:, t0:t0 + TS]), "ld_cs%d" % b, writes=[B_cs[b]])
                    P.dma("sp", DMA(cs[b][:, 1, :], sinT[:, t0:t0 + TS]), "ld_cs%d" % b, writes=[B_cs[b]])

                load_x(0)
                def zipg(g1, g2):
                    live = [g1, g2]
                    while live:
                        for g in list(live):
                            try:
                                next(g)
                                yield
                            except StopIteration:
                                live.remove(g)

                def prep(t):
                    b = t % 2
                    t0 = t * TS
                    nkb = 4 * t + 4
                    early = t < 10

                    def EV(out, in_):
                        return ("act", ACT(out, in_, AF.Copy)) if early else ("dve", CP(out, in_))

                    if t + 1 < NT:
                        load_x(t + 1)
                    for c in range(8):
                        if early and c % 2 == 1:
                            P.op("act", ACT(hT[b][:, c, :], xt[b][:, c, :], AF.Identity, bias=sh1c[:, c:c + 1], scale=sc1p[:, c:c + 1]),
                                 reads=[B_xt[b][c], B_mod], writes=[B_hT[b][c]])
                            yield
                            continue
                        fn = TSC(hT[b][:, c, :], xt[b][:, c, :], sc1p[:, c:c + 1], sh1c[:, c:c + 1], ALU.mult, ALU.add)
                        P.op("dve", fn, reads=[B_xt[b][c], B_mod], writes=[B_hT[b][c]])
                        if c % 2 == 1:
                            yield
                    yield

                    def fproj(g, M, bank):
                        ps, Bp = psM[bank], B_psM[bank]
                        for kc in range(8):
                            P.op("pe", MM(ps[0:M, :], wf[:, kc, g * 128:g * 128 + M], hT[b][:, kc, :], kc == 0, kc == 7),
                                 reads=[B_c1] + (B_hT[b] if kc == 7 else [B_hT[b][kc]]), writes=[Bp], signal=(kc == 7))
                        return ps, Bp

                    for half in range(2):
                        ps, Bp = psM[half], B_psM[half]
                        for bl in range(2):
                            blk = 2 * half + bl
                            for kc in range(8):
                                P.op("pe", MM(ps[:, bl * 256:bl * 256 + 193], hT[b][:, kc, blk * 128:(blk + 1) * 128],
                                              wt[:, kc, :], kc == 0, kc == 7),
                                     reads=[B_c1] + (B_hT[b] if (kc == 7 and bl == 1) else [B_hT[b][kc]]), writes=[Bp],
                                     signal=(kc == 7 and bl == 1))
                        yield
                        yield
                        psv = ps[:, :].rearrange("p (b c) -> p b c", c=256)
                        kb0 = 4 * t + 2 * half
                        P.op("dve", CP(Vaug[:, kb0:kb0 + 2, 0:64], psv[:, :, 0:64]), reads=[Bp], writes=[B_V[t]])
                        P.op("dve", CP(spc[:, 2 * half:2 * half + 2], psv[:, :, 192]), reads=[Bp], writes=[B_spc])
                        yield
                        P.op("dve", CP(v16[:, 2 * half:2 * half + 2, :], psv[:, :, 64:128]), reads=[Bp], writes=[B_v16])
                        P.op("dve", CP(gat[:, 2 * half:2 * half + 2, :], psv[:, :, 128:192]), reads=[Bp], writes=[B_gat])
                        yield

                    def chainA():
                        psA, BA = fproj(0, 97, 0)
                        P.op("act", ACT(spc[:], spc[:], AF.Exp, bias=negb[:, 0:1], scale=-1.0), reads=[B_spc, B_c1], writes=[B_spc])
                        P.op("act", ACT(spc[:], spc[:], AF.Ln, bias=1.0), reads=[B_spc], writes=[B_spc])
                        yield
                        yield
                        P.op(*EV(Qaug[b][0:64, :], psA[0:64, :]), reads=[BA], writes=[B_Q[b]])
                        P.op("dve", CP(fr2[64:97, :], psA[64:97, :]), reads=[BA], writes=[B_fr])
                        yield
                        psB, BB = fproj(1, 64, 0)
                        P.op("act", ACT(fr1[64:97, :], fr2[64:97, :], AF.Exp, bias=negb[64:97, 0:1], scale=-1.0),
                             reads=[B_fr, B_c1], writes=[B_fr])
                        P.op("act", ACT(fr1[64:97, :], fr1[64:97, :], AF.Ln, bias=1.0), reads=[B_fr], writes=[B_fr])
                        yield
                        yield
                        P.op(*EV(Kaug[0:64, t0:t0 + TS], psB[0:64, :]), reads=[BB], writes=[B_K[t]])
                        P.op("dve", lambda e: e.tensor_tensor_scan(out=fr2[64:97, :], data0=ones[64:97, :], data1=fr1[64:97, :],
                                                                   initial=carr[64:97, 0:1], op0=ALU.mult, op1=ALU.add),
                             reads=[B_fr, B_carr, B_const], writes=[B_fr])
                        yield
                        P.op("dve", TSC(fr1[64:97, :], fr2[64:97, :], carr[64:97, 0:1], -8.0, ALU.subtract, ALU.mult),
                             reads=[B_fr, B_carr], writes=[B_fr])
                        P.op("dve", CP(carr[64:97, 0:1], fr2[64:97, TS - 1:TS]), reads=[B_fr], writes=[B_carr])
                        P.op("dve", CP(fhi[64:97, :], fr1[64:97, :]), reads=[B_fr], writes=[B_fr])
                        yield
                        P.op("dve", TT(flo[64:97, :], fr1[64:97, :], fhi[64:97, :], ALU.subtract), reads=[B_fr], writes=[B_fr])
                        P.op("dve", CP(Qaug[b][64:65, :], fhi[64:65, :]), reads=[B_fr], writes=[B_Q[b]])
                        P.op("dve", CP(Qaug[b][96:97, :], flo[96:97, :]), reads=[B_fr], writes=[B_Q[b]])
                        yield
                        for bl in range(1, 5):
                            P.op("dve", TT(exc[:, bl:bl + 1], exc[:, bl - 1:bl], spc[:, bl - 1:bl], ALU.add),
                                 reads=[B_spc], writes=[B_spc])
                        yield
                        yield
                        psF, BF = psM[0], B_psM[0]
                        P.op("pe", MM(psF[:, 0:4], utri[:], spc[:], True, False), reads=[B_c1, B_spc], writes=[BF], signal=False)
                        P.op("pe", MM(psF[:, 0:4], ones[:, 0:128], exc[:, 0:4], False, True), reads=[B_c1, B_spc, B_const], writes=[BF])
                        P.op("pe", MM(psF[:, 8:9], ones[:, 0:128], exc[:, 4:5], True, True), reads=[B_spc, B_const], writes=[BF])
                        yield
                        yield
                        P.op("dve", TSC(NFcol[:, 4 * t:4 * t + 4], psF[:, 0:4], carc[:, 0:1], None, ALU.add),
                             reads=[BF, B_carc], writes=[B_NF])
                        P.op("dve", TSC(biasT[b][:, 0:nkb], NFcol[:, 0:nkb], carc[:, 0:1], None, ALU.subtract),
                             reads=[B_NF, B_carc], writes=[B_bias[b]])
                        P.op("dve", TT(carc[:], carc[:], psF[:, 8:9], ALU.add), reads=[BF, B_carc], writes=[B_carc])
                        yield

                    def chainB():
                        psC, BC = fproj(2, 128, 1)
                        P.op("act", ACT(gat2[:], gat[:], AF.Exp, scale=-1.0), reads=[B_gat], writes=[B_gat])
                        yield
                        yield
                        P.op(*EV(q16[:], psC[:]), reads=[BC], writes=[B_q16])
                        yield
                        psD, BD = fproj(3, 128, 1)
                        P.op("dve", TSC(gat2[:], gat2[:], 1.0, None, ALU.add), reads=[B_gat], writes=[B_gat])
                        P.op("dve", lambda e: e.reciprocal(out=gat2[:], in_=gat2[:]), reads=[B_gat], writes=[B_gat])
                        yield
                        yield
                        P.op(*EV(k16[:], psD[:]), reads=[BD], writes=[B_k16])
                        P.op("dve", TT(ro16[b][:, :, 64:128], gat[:], gat2[:], ALU.mult), reads=[B_gat], writes=[B_ro[b]])
                        yield
                        for (src16, Bsrc, dst16, Bdst) in ((q16, B_q16, qr16, B_qr), (k16, B_k16, kr16, B_kr)):
                            psP, BP = psM[1], B_psM[1]
                            P.op("pe", MM(psP[:], pm16[:], src16[:], True, True), reads=[B_c1, Bsrc], writes=[BP])
                            P.op("dve", TT(rt1[:], src16[:], cs[b][:, 0, :], ALU.mult), reads=[Bsrc, B_cs[b]], writes=[B_rt1])
                            yield
                            yield
                            P.op("dve", TT(rt2[:], psP[:], cs[b][:, 1, :], ALU.mult), reads=[BP, B_cs[b]], writes=[B_rt2])
                            yield
                            P.op("dve", TT(dst16[:], rt1[:], rt2[:], ALU.add), reads=[B_rt1, B_rt2], writes=[Bdst])
                            yield
                        P.op("dve", TT(qxi16[:], qr16[:], xi512[:], ALU.mult), reads=[B_qr, B_c1], writes=[B_qxi])
                        yield
                        for ci in range(4):
                            csl = slice(ci * 128, (ci + 1) * 128)
                            P.op("pe", TR(psT[:, 0:128], kr16[:, csl], ident16[:]), reads=[B_kr, B_const], writes=[B_psT])
                            psR, BR = psM[1], B_psM[1]
                            P.op("pe", MM(psR[:, 0:128], kr16[:, csl], qr16[:, csl], True, True), reads=[B_kr, B_qr], writes=[BR])
                            yield
                            P.op("dve", TSC(kz16[:], psT[:, 0:128], zeta[:, 0:1], None, ALU.mult), reads=[B_psT, B_c1], writes=[B_kz])
                            P.op("dve", TT(sTm16[:], psR[:, 0:128], innerT[:], ALU.mult), reads=[BR, B_c1], writes=[B_sTm])
                            yield
                            yield
                            P.op("pe", MM(psR[:, 128:192], sTm16[:], v16[:, ci, :], True, False), reads=[B_sTm, B_v16], writes=[BR],
                                 signal=False)
                            P.op("pe", MM(psR[:, 128:192], qxi16[:, csl], R16[:], False, True), reads=[B_sTm, B_v16, B_qxi, B_R16],
                                 writes=[BR], signal=False)
                            P.op("pe", MM(psR[:, 256:320], kz16[:], v16[:, ci, :], True, True), reads=[B_kz, B_v16, B_sTm, B_qxi, B_R16],
                                 writes=[BR])
                            yield
                            yield
                            P.op("dve", CP(ro16[b][:, ci, 0:64], psR[:, 128:192]), reads=[BR], writes=[B_ro[b]])
                            P.op("dve", STT(R32[:], R32[:], gch[:, 0:1], psR[:, 256:320], ALU.mult, ALU.add),
                                 reads=[BR, B_c1], writes=[B_R32])
                            P.op("dve", CP(R16[:], R32[:]), reads=[B_R32], writes=[B_R16])
                            yield
                        P.dma("sp", DMA(xin[t].ap()[256:768, :].rearrange("(b p) f -> p b f", p=128), ro16[b][:]),
                              "st_ro%d" % b, reads=[B_ro[b]], writes=[B_xin[t]])

                    yield from zipg(chainA(), chainB())

                def finalize(t):
                    b = t % 2
                    psOb, BOb = psO[b], B_psO[b]
                    P.op("dve", lambda e, psOb=psOb: e.reciprocal(out=rr[64:97, :], in_=psOb[64:97, :]), reads=[BOb], writes=[B_rr])
                    yield
                    P.op("dve", CP(rhi[64:97, :], rr[64:97, :]), reads=[B_rr], writes=[B_rr])
                    P.op("dve", TT(rlo[64:97, :], rr[64:97, :], rhi[64:97, :], ALU.subtract), reads=[B_rr], writes=[B_rr])
                    P.op("dve", CP(rhi[96:97, :], rlo[96:97, :]), reads=[B_rr], writes=[B_rr])
                    yield
                    yield
                    yield
                    psN, BN = psM[0], B_psM[0]
                    P.op("pe", MM(psN[0:64, :], sel16[64:97, :], rhi[64:97, :], True, True), reads=[B_c1, B_rr], writes=[BN])
                    yield
                    yield
                    P.op("dve", CP(bcs[:], psN[0:64, :]), reads=[BN], writes=[B_bcs])
                    P.op("dve", TT(OT16[b][:], psOb[0:64, :], bcs[:], ALU.mult), reads=[BOb, B_bcs], writes=[B_OT[b]])
                    yield
                    P.dma("sp", DMA(xin[t].ap()[0:256, :].rearrange("(d four) c -> d (four c)", four=4), OT16[b][:]),
                          "st_ot%d" % b, reads=[B_OT[b]], writes=[B_xin[t]])
                    exchange(t)

                def chain(*gens):
                    for g in gens:
                        for _ in g:
                            yield

                NHOPS = 90
                for _ in prep(0):
                    pass
                for t in range(NT):
                    b = t % 2
                    t0 = t * TS
                    nkb = 4 * t + 4
                    gens = []
                    if t >= 1:
                        gens.append(finalize(t - 1))
                    if t + 1 < NT:
                        gens.append(prep(t + 1))
                    gen = chain(*gens)
                    per = -(-NHOPS // nkb)
                    psOb, BOb = psO[b], B_psO[b]

                    def emit_S(kb):
                        si = kb % 3
                        pi = kb % NPT
                        j = kb - 4 * t
                        n0 = 0 if j < 0 else 128 * j
                        kcs = slice(kb * 128, (kb + 1) * 128)
                        tk = kb // 4
                        if j < 0:
                            P.op("pe", MM(psS[si][:, :], Kaug[0:97, kcs], Qaug[b][0:97, :], True, True),
                                 reads=[B_K[tk], B_c1, B_Q[b]], writes=[B_psS[si]])
                        else:
                            P.op("pe", MM(psS[si][:, n0:n0 + 128], Kaug[0:97, kcs], Qaug[b][0:97, n0:n0 + 128], True, False),
                                 reads=[B_K[tk], B_c1, B_Q[b]], writes=[B_psS[si]], signal=False)
                            P.op("pe", MM(psS[si][:, n0:n0 + 128], ident16[:], cmask16[:], False, True),
                                 reads=[B_c1, B_const, B_K[tk], B_Q[b]], writes=[B_psS[si]], signal=(n0 + 128 >= TS))
                            if n0 + 128 < TS:
                                P.op("pe", MM(psS[si][:, n0 + 128:TS], Kaug[0:97, kcs], Qaug[b][0:97, n0 + 128:TS], True, True),
                                     reads=[B_K[tk], B_c1, B_Q[b]], writes=[B_psS[si]])
                        P.op("act", ACT(PT[pi][:, n0:TS], psS[si][:, n0:TS], AF.Exp, bias=biasT[b][:, kb:kb + 1], scale=0.125),
                             reads=[B_psS[si], B_bias[b]], writes=[B_PT[pi]])

                    def emit_PV(kb):
                        pi = kb % NPT
                        j = kb - 4 * t
                        n0 = 0 if j < 0 else 128 * j
                        tk = kb // 4
                        P.op("pe", MM(psOb[:, n0:TS], Vaug[:, kb, :], PT[pi][:, n0:TS], kb == 0, kb == nkb - 1),
                             reads=[B_V[tk], B_c1, B_PT[pi]], writes=[BOb])

                    LA = 2
                    for step in range(nkb + LA):
                        if step < nkb:
                            emit_S(step)
                        if step - LA >= 0:
                            emit_PV(step - LA)
                        for _ in range(per):
                            next(gen, None)
                    for _ in gen:
                        pass
                    if big_casts and (t >= NT // 4 or NT - t <= len(big_casts)):
                        big_casts.pop(0)()
                while big_casts:
                    big_casts.pop(0)()
                for _ in finalize(NT - 1):
                    pass

                pass

            P.set_fence()
            B_ag = B_O2
            if stop in (1, 2):
                final.append(P.dma("sp", DMA(y[0:128, :], g1bc[:]), "st_dbg", reads=[B_mod, B_O2]))
                raise _Stop()

            PIDS = {}
            with ExitStack() as s2:
                QT = min(512, SLAB)
                NQ = SLAB // QT
                NBL = QT // 128
                lng = [sb(s2, "lng%d" % i, [128, D]) for i in range(4)]
                wo16 = sb(s2, "wo16", [128, 8, D], BF16)
                cw = sb(s2, "cw", [128, 2 * NCH * 3])
                cb = sb(s2, "cb", [128, 2 * NCH])
                hmask = sb(s2, "hmask", [128, 1])
                neghalf = sb(s2, "neghalf", [128, 4])
                foxT = sb(s2, "foxT", [128, 4, QT], BF16)
                retin = [sb(s2, "retin%d" % i, [128, 8, 128], BF16) for i in range(2)]
                oc = [sb(s2, "oc%d" % i, [128, 512]) for i in range(3)]
                rn16 = [sb(s2, "rn16_%d" % i, [128, 512], BF16) for i in range(2)]
                retT = sb(s2, "retT", [128, 4, QT], BF16)
                xsb = [sb(s2, "xsb%d" % i, [128, D]) for i in range(2)]
                zt = [sb(s2, "zt%d" % i, [128, D]) for i in range(2)]
                x1q = [sb(s2, "x1q%d" % i, [128, 4, D]) for i in range(2)]
                h2Tq = [sb(s2, "h2Tq%d" % i, [128, 8, QT], BF16) for i in range(2)]
                h2Th = sb(s2, "h2Th", [128, 8, HALO], BF16)
                st6 = [sb(s2, "st6_%d" % i, [128, 4, 6]) for i in range(3)]
                mv = [sb(s2, "mv%d" % i, [128, 4, 2]) for i in range(3)]
                rstd = [sb(s2, "rstd%d" % i, [128, 4]) for i in range(3)]
                NWU, NWD = 4, 4
                GW = 3
                wu = [sb(s2, "wu%d" % i, [128, 8, 2, GW * 128], BF16) for i in range(2)]
                wd = [sb(s2, "wd%d" % i, [128, 512], BF16) for i in range(NWD)]
                ub = [sb(s2, "ub%d" % i, [128, QT + 2]) for i in range(4)]
                yv = [sb(s2, "yv%d" % i, [128, QT]) for i in range(6)]
                uprev = sb(s2, "uprev", [128, 2 * NCH, 2])
                act16 = sb(s2, "act16", [128, NCH, QT], BF16)
                ot = [sb(s2, "ot%d" % i, [128, D]) for i in range(2)]

                pA = [pst(s2, "pA%d" % i, [128, 512]) for i in range(4)]
                pU = [pst(s2, "pU%d" % i, [128, 512]) for i in range(2)]
                pTf = pst(s2, "pTf", [128, 512])
                pTb = pst(s2, "pTb", [128, 1024], BF16)
                B_pA = [Buf(True) for _ in range(4)]
                B_pU = [Buf(True) for _ in range(2)]
                B_pTf, B_pTb = Buf(True), Buf(True)
                B_c2 = Buf()
                B_fox, B_retT = Buf(), Buf()
                B_oc = [Buf() for _ in range(3)]
                B_rn = [Buf(), Buf()]
                B_retin = [Buf(), Buf()]
                B_xsb = [Buf(), Buf()]
                B_zt = [Buf(), Buf()]
                B_st = [Buf() for _ in range(3)]
                B_x1q = [[Buf() for _ in range(4)] for _ in range(2)]
                B_h2Tq = [Buf(), Buf()]
                B_h2Th = Buf()
                B_wu = [Buf(), Buf()]
                B_wd = [Buf() for _ in range(NWD)]
                B_ub = [Buf() for _ in range(4)]
                B_yv = [Buf() for _ in range(6)]
                B_uprev = Buf()
                B_act = [Buf() for _ in range(NCH)]
                B_ot = [Buf(), Buf()]

                for i, src in enumerate((ln1g_d, ln1b_d, ln2g_d, ln2b_d)):
                    P.dma("sp", DMA(lng[i][:], src[0:1, :].broadcast_to([128, D])), "ld_c", writes=[B_c2])
                P.dma("sp", DMA(wo16[:], wout16.rearrange("(kc p) n -> p kc n", p=128)), "ld_c", reads=[B_w16], writes=[B_c2])
                P.dma("sp", DMA(cw[:], convw), "ld_c", writes=[B_c2])
                P.dma("sp", DMA(cb[:], convb), "ld_c", writes=[B_c2])
                P.dma("sp", DMA(hmask[:], hmask_d), "ld_c", writes=[B_c2])
                P.batch_done("ld_c", [B_c2])
                P.op("pool", MS(neghalf[:], -0.5), writes=[B_c2])

                TPS = SLAB // TS
                assert QT == TS and TPS >= 1

                def tile_blk(e, qidx, eng):
                    if eng not in PIDS:
                        PIDS[eng] = e.partition_id() * (TPS * BLK)
                    return O2ap[(qidx + 1) * BLK:(NT + 1) * BLK, :][bass.ds(PIDS[eng], BLK), :]
                wup_v = wup16.rearrange("(kc p) n -> p kc n", p=128)
                wdn_v = wdn16.rearrange("(i p) n -> p i n", p=128)

                def wait(n):
                    for _ in range(n):
                        yield

                def layer_norm(eng_obj_unused, src, np_, gi, dst, Bsrc, Bdst, k=0):
                    for hh in range(2):
                        P.op("dve", lambda e, hh=hh: e.bn_stats(out=st6[k][0:np_, hh, :], in_=src[0:np_, hh * 512:(hh + 1) * 512]),
                             reads=[Bsrc], writes=[B_st[k]])
                    P.op("dve", lambda e: e.bn_aggr(out=mv[k][0:np_, 0, :], in_=st6[k][0:np_, 0:2, :].rearrange("p a s -> p (a s)")),
                         reads=[B_st[k]], writes=[B_st[k]])
                    P.op("dve", TSC(rstd[k][0:np_, 0:1], mv[k][0:np_, 0, 1:2], LN_EPS, None, ALU.add), reads=[B_st[k]], writes=[B_st[k]])
                    yield from wait(4)
                    P.op("pool", TT(rstd[k][0:np_, 0:1], rstd[k][0:np_, 0:1], neghalf[0:np_, 0:1], ALU.pow), reads=[B_st[k], B_c2], writes=[B_st[k]])
                    yield from wait(2)
                    P.op("dve", TSC(rstd[k][0:np_, 1:2], mv[k][0:np_, 0, 0:1], rstd[k][0:np_, 0:1], -1.0, ALU.mult, ALU.mult),
                         reads=[B_st[k]], writes=[B_st[k]])
                    yield from wait(2)
                    P.op("act", ACT(src[0:np_, :], src[0:np_, :], AF.Identity, bias=rstd[k][0:np_, 1:2], scale=rstd[k][0:np_, 0:1]),
                         reads=[B_st[k]], writes=[Bsrc])
                    yield from wait(4)
                    P.op("dve", TT(src[0:np_, :], src[0:np_, :], lng[gi][0:np_, :], ALU.mult), reads=[B_c2], writes=[Bsrc])
                    yield from wait(4)
                    P.op("pool", TT(dst[0:np_, :], src[0:np_, :], lng[gi + 1][0:np_, :], ALU.add), reads=[Bsrc, B_c2], writes=[Bdst])
                    yield from wait(16)

                def token_block(loc0, np_, x1dst, Bx1, hcol0, ri, qidx, tok0, h2dst, Bh2):
                    def dyn_ret(e):
                        src = tile_blk(e, qidx, "sp").rearrange("(k x) f -> x k f", x=XR)
                        return e.dma_start(out=retin[ri][0:np_, :, :], in_=src[256 + tok0:256 + tok0 + np_, :, :])
                    P.dma("sp", dyn_ret, "ld_ret%d" % ri, reads=[B_ag], writes=[B_retin[ri]])
                    P.dma("sp", DMA(xsb[ri][0:np_, :], xs[loc0:loc0 + np_, :]), "ld_xs%d" % ri, writes=[B_xsb[ri]])
                    yield from wait(8)
                    oc_, rn_, zt_ = oc[ri], rn16[ri], zt[ri]
                    Boc, Brn, Bzt, Bst = B_oc[ri], B_rn[ri], B_zt[ri], B_st[ri]
                    P.op("pool", CP(oc_[0:np_, :].rearrange("p (h a d) -> p a h d", a=2, d=64),
                                    retin[ri][0:np_, :, 0:64].rearrange("p (a h) d -> p a h d", a=2)),
                         reads=[B_retin[ri]], writes=[Boc])
                    yield from wait(6)
                    for h in range(4):
                        P.op("dve", lambda e, h=h: e.bn_stats(out=st6[ri][0:np_, h, :], in_=oc_[0:np_, h * 128:(h + 1) * 128]),
                             reads=[Boc], writes=[Bst])
                    for h in range(4):
                        P.op("dve", lambda e, h=h: e.bn_aggr(out=mv[ri][0:np_, h, :], in_=st6[ri][0:np_, h, :]), reads=[Bst], writes=[Bst])
                    P.op("dve", TSC(rstd[ri][0:np_, :], mv[ri][0:np_, :, 1], GN_EPS, None, ALU.add), reads=[Bst], writes=[Bst])
                    yield from wait(4)
                    P.op("pool", TT(rstd[ri][0:np_, :], rstd[ri][0:np_, :], neghalf[0:np_, :], ALU.pow), reads=[Bst, B_c2], writes=[Bst])
                    yield from wait(2)
                    for h in range(4):
                        P.op("dve", TSC(oc_[0:np_, h * 128:(h + 1) * 128], oc_[0:np_, h * 128:(h + 1) * 128], mv[ri][0:np_, h, 0:1],
                                        rstd[ri][0:np_, h:h + 1], ALU.subtract, ALU.mult), reads=[Bst], writes=[Boc])
                    yield from wait(3)
                    P.op("pool", TT(rn_[0:np_, :].rearrange("p (h a d) -> p a h d", a=2, d=64),
                                    oc_[0:np_, :].rearrange("p (h a d) -> p a h d", a=2, d=64),
                                    retin[ri][0:np_, :, 64:128].rearrange("p (a h) d -> p a h d", a=2), ALU.mult),
                         reads=[Boc, B_retin[ri]], writes=[Brn])
                    yield from wait(14)
                    for h in range(4):
                        P.op("pe", TR(pTb[:, h * 128:h * 128 + np_], rn_[0:np_, h * 128:(h + 1) * 128], ident16[0:np_, 0:np_]),
                             reads=[Brn, B_const], writes=[B_pTb])
                    P.op("dve", CP(retT[:, :, hcol0:hcol0 + np_], pTb[:, 0:512].rearrange("p (h c) -> p h c", c=128)[:, :, 0:np_]),
                         reads=[B_pTb], writes=[B_retT])
                    yield from wait(8)
                    for n2 in range(2):
                        pa, Bpa = pA[n2], B_pA[n2]
                        for c in range(8):
                            lhs = foxT[:, c, hcol0:hcol0 + np_] if c < 4 else retT[:, c - 4, hcol0:hcol0 + np_]
                            P.op("pe", MM(pa[0:np_, :], lhs, wo16[:, c, n2 * 512:(n2 + 1) * 512], c == 0, c == 7),
                                 reads=[B_fox, B_retT, B_c2], writes=[Bpa], signal=(c == 7))
                        P.op("dve", TT(zt_[0:np_, n2 * 512:(n2 + 1) * 512], pa[0:np_, :], g1bc[0:np_, n2 * 512:(n2 + 1) * 512], ALU.mult),
                             reads=[Bpa, B_mod], writes=[Bzt])
                        yield
                    yield from wait(2)
                    P.op("dve", STT(zt_[0:np_, :], xsb[ri][0:np_, :], ALPHA, zt_[0:np_, :], ALU.mult, ALU.add),
                         reads=[B_xsb[ri]], writes=[Bzt])
                    yield from wait(4)
                    yield from layer_norm(None, zt_, np_, 0, x1dst, Bzt, Bx1, k=ri)
                    for half in range(2):
                        for cc in range(4):
                            c = 4 * half + cc
                            P.op("pe", TR(pTf[:, cc * 128:cc * 128 + np_], x1dst[0:np_, c * 128:(c + 1) * 128], ident[0:np_, 0:np_]),
                                 reads=[Bx1, B_const], writes=[B_pTf])
                        for cc in range(4):
                            c = 4 * half + cc
                            P.op("dve" if cc % 2 == 0 else "act",
                                 TSC(h2dst[:, c, hcol0:hcol0 + np_], pTf[:, cc * 128:cc * 128 + np_], sc2p[:, c:c + 1], sh2c[:, c:c + 1],
                                     ALU.mult, ALU.add) if cc % 2 == 0 else
                                 ACT(h2dst[:, c, hcol0:hcol0 + np_], pTf[:, cc * 128:cc * 128 + np_], AF.Identity,
                                     bias=sh2c[:, c:c + 1], scale=sc2p[:, c:c + 1]),
                                 reads=[B_pTf, B_mod], writes=[Bh2])
                        yield from wait(2)

                def load_fox(qidx, tok0, n):
                    for two in range(2):
                        def dyn_fox(e, two=two):
                            v = tile_blk(e, qidx, "act").rearrange("(k x) f -> k x f", x=XR)[4 * two:4 * two + 4, 0:256, :]
                            v = v.rearrange("c (d four) f -> d c (four f)", four=4)
                            return e.dma_start(out=foxT[64 * two:64 * two + 64, :, 0:n], in_=v[:, :, tok0:tok0 + n])
                        P.dma("act", dyn_fox, "ld_fox", reads=[B_ag], writes=[B_fox])

                up_banks = [(pU[0], B_pU[0]), (pU[1], B_pU[1]), (pA[2], B_pA[2]), (pA[3], B_pA[3])]
                upctr = [0]
                upq = [0]

                def up_chunk(i, part, ncols, wbuf, src, Bsrc):
                    ps, Bp = up_banks[upctr[0] % len(up_banks)]
                    upctr[0] += 1
                    g, o = i // GW, i % GW
                    w = wu[g % 2]
                    for kc in range(8):
                        P.op("pe", MM(ps[:, 0:ncols], w[:, kc, part, o * 128:(o + 1) * 128], src[:, kc, 0:ncols], kc == 0, kc == 7),
                             reads=[Bsrc, B_wu[g % 2]], writes=[Bp], signal=(kc == 7))
                    return ps, Bp

                NG = -(-NCH // GW)

                def load_wu_group(g):
                    w = wu[g % 2]
                    c0 = g * GW * 128
                    nc_ = min(GW * 128, DFF - c0)
                    for part in range(2):
                        P.dma("sp", DMA(w[:, :, part, 0:nc_], wup_v[:, :, part * DFF + c0:part * DFF + c0 + nc_]),
                              "ld_wu%d" % (g % 2), reads=[B_w16], writes=[B_wu[g % 2]])

                wu_loaded = set()

                def load_wu(i):
                    if (upq[0], i) in wu_loaded:
                        return
                    wu_loaded.add((upq[0], i))
                    if i == 0:
                        load_wu_group(0)
                    if i % GW == 0 and i // GW + 1 < NG:
                        load_wu_group(i // GW + 1)

                load_fox(-1, TS - HALO, HALO)
                for _ in token_block(0, HALO, x1q[0][:, 0, :], B_x1q[0][0], 0, 0, -1, TS - HALO, h2Th, B_h2Th):
                    pass

                def zip2(g1, g2):
                    live = [g1, g2]
                    while live:
                        for g in list(live):
                            try:
                                next(g)
                                yield
                            except StopIteration:
                                live.remove(g)

                def quarter_blocks(q):
                    pq = q % 2
                    load_fox(q, 0, QT)
                    gens_ = [token_block(HALO + q * QT + bl * 128, 128, x1q[pq][:, bl, :], B_x1q[pq][bl], bl * 128, bl % 2,
                                         q, bl * 128, h2Tq[pq], B_h2Tq[pq]) for bl in range(NBL)]
                    for k in range(0, NBL, 2):
                        if k + 1 < NBL:
                            yield from zip2(gens_[k], gens_[k + 1])
                        else:
                            yield from gens_[k]

                def ln2_hops(q):
                    x1_, Bx1_ = x1q[q % 2], B_x1q[q % 2]
                    for bl in range(NBL):
                        oi = bl % 2
                        for _ in layer_norm(None, x1_[:, bl, :], 128, 2, ot[oi], Bx1_[bl], B_ot[oi], k=2):
                            yield
                        r0 = q * QT + bl * 128
                        final.append(P.dma("sp", DMA(y[r0:r0 + 128, :], ot[oi][:]), "st_y%d" % oi, reads=[B_ot[oi]]))
                        yield

                for _ in quarter_blocks(0):
                    pass
                ln2_pending = iter(())
                for q in range(NQ):
                    upq[0] = q
                    pq = q % 2
                    x1 = x1q[pq]
                    B_x1 = B_x1q[pq]
                    gen = chain(ln2_pending, quarter_blocks(q + 1) if q + 1 < NQ else iter(()))
                    def pre_u(i, part):
                        ch = part * NCH + i
                        bi_ = 2 * (i % 2) + part
                        P.op("dve", CP(ub[bi_][:, 0:2], uprev[:, ch, :]), reads=[B_uprev], writes=[B_ub[bi_]])

                    def halo_u(i):
                        for part in range(2):
                            ps, Bp = up_chunk(i, part, HALO, None, h2Th, B_h2Th)
                            P.op("dve", TSC(uprev[:, part * NCH + i, :], ps[:, HALO - 2:HALO], hmask[:, 0:1], None, ALU.mult),
                                 reads=[Bp, B_c2], writes=[B_uprev])

                    if q == 0:
                        load_wu(0)
                        halo_u(0)
                    pre_u(0, 0)

                    def geglu(i):
                        ya, yb_ = yv[2 * (i % 3)], yv[2 * (i % 3) + 1]
                        P.op("act", ACT(ya[:], ya[:], AF.Gelu), reads=[], writes=[B_yv[2 * (i % 3)]])
                        P.op("pool", TT(act16[:, i, :], ya[:], yb_[:], ALU.mult),
                             reads=[B_yv[2 * (i % 3)], B_yv[2 * (i % 3) + 1]], writes=[B_act[i]])

                    for i in range(NCH):
                        load_wu(i)
                        if q == 0 and i + 1 < NCH:
                            halo_u(i + 1)
                        for part in range(2):
                            ch = part * NCH + i
                            ps, Bp = up_chunk(i, part, QT, None, h2Tq[pq], B_h2Tq[pq])
                            bi_ = 2 * (i % 2) + part
                            u = ub[bi_]
                            yi_ = 2 * (i % 3) + part
                            yy = yv[yi_]
                            P.op("act", ACT(u[:, 2:QT + 2], ps[:, 0:QT], AF.Copy), reads=[Bp], writes=[B_ub[bi_]])
                            P.op("act", ACT(yy[:], ps[:, 0:QT], AF.Identity, bias=cb[:, ch:ch + 1], scale=cw[:, 3 * ch + 2:3 * ch + 3]),
                                 reads=[Bp, B_c2], writes=[B_yv[yi_]])
                            if True:
                                if part == 0:
                                    pre_u(i, 1)
                                elif i + 1 < NCH:
                                    pre_u(i + 1, 0)
                            P.op("dve", STT(yy[:], u[:, 1:QT + 1], cw[:, 3 * ch + 1:3 * ch + 2], yy[:], ALU.mult, ALU.add),
                                 reads=[B_ub[bi_], B_c2], writes=[B_yv[yi_]])
                            P.op("dve", STT(yy[:], u[:, 0:QT], cw[:, 3 * ch:3 * ch + 1], yy[:], ALU.mult, ALU.add),
                                 reads=[B_ub[bi_], B_c2], writes=[B_yv[yi_]])
                            P.op("dve", CP(uprev[:, ch, :], u[:, QT:QT + 2]), reads=[B_ub[bi_]], writes=[B_uprev])
                            for _ in range(13):
                                next(gen, None)
                        if i >= 1:
                            geglu(i - 1)
                    geglu(NCH - 1)
                    for _ in gen:
                        pass
                    for n2 in range(2):
                        for i in range(NCH):
                            wi = (n2 * NCH + i) % NWD
                            P.dma("sp", DMA(wd[wi][:, :], wdn_v[:, i, n2 * 512:(n2 + 1) * 512]), "ld_wd%d" % wi,
                                  reads=[B_w16], writes=[B_wd[wi]])
                            for bl in range(NBL):
                                P.op("pe", MM(pA[bl][:, :], act16[:, i, bl * 128:(bl + 1) * 128], wd[wi][:, :],
                                              i == 0, i == NCH - 1),
                                     reads=[B_act[i], B_wd[wi]], writes=[B_pA[bl]], signal=(i == NCH - 1 or bl == NBL - 1))
                        for bl in range(NBL):
                            cs_ = slice(n2 * 512, (n2 + 1) * 512)
                            P.op("dve", TT(oc[2][:, :], pA[bl][:, :], g2bc[:, cs_], ALU.mult), reads=[B_pA[bl], B_mod], writes=[B_oc[2]])
                            P.op("dve", STT(x1[:, bl, cs_], x1[:, bl, cs_], ALPHA, oc[2][:, :], ALU.mult, ALU.add),
                                 reads=[B_oc[2]], writes=[B_x1[bl]])
                    ln2_pending = ln2_hops(q)
                for _ in ln2_pending:
                    pass

        except _Stop:
            pass
        with ExitStack() as fin:
            sems = {n: fin.enter_context(nc.semaphore(n)) for n in P.sem_names()}
            block = fin.enter_context(nc.Block())
            last = {}
            if limit is not None:
                P.set_fence()
                final = list(final) + list(P.fence)
            for h in final:
                if h is not None:
                    last[h[0]] = max(last.get(h[0], 0), h[1])

            @block.sync
            def _(e):
                P.replay("sp", e, sems, final_waits=list(last.items()))

            @block.tensor
            def _(e):
                P.replay("pe", e, sems)

            @block.scalar
            def _(e):
                P.replay("act", e, sems)

            @block.vector
            def _(e):
                P.replay("dve", e, sems)

            @block.gpsimd
            def _(e):
                P.replay("pool", e, sems)
    nc._oplog = P.log
    return nc


def _const_tables():
    f32 = np.float32
    pos = np.arange(S, dtype=f32)
    inv_freq = (f32(10000.0) ** (-np.arange(0, 128, 2, dtype=f32) / f32(128))).astype(f32)
    ang = (pos[:, None] * inv_freq[None, :]).astype(f32)
    ang = np.concatenate([ang, ang], axis=-1)
    cosT = np.ascontiguousarray(np.cos(ang).astype(f32).T)
    sin = np.sin(ang).astype(f32)
    sgn = np.concatenate([-np.ones(64, f32), np.ones(64, f32)])
    sinT = np.ascontiguousarray((sin * sgn[None, :]).T.astype(f32))
    ident = np.eye(128, dtype=f32)
    utri = np.triu(np.ones((128, 128), f32))
    kk = np.arange(128)
    cmask = np.where(kk[:, None] <= kk[None, :], f32(0.0), f32(-1.0e6)).astype(f32)
    pm = np.zeros((128, 128), f32)
    for i in range(128):
        pm[(i + 64) % 128, i] = 1.0
    return cosT, sinT, ident, utri, cmask, pm


def _ret_tables(h):
    f32 = np.float32
    lg = np.log1p(-np.exp2(f32(-5.0 - h))).astype(f32)
    idx = np.arange(128, dtype=f32)
    c = f32(128.0 ** -0.5)
    diff = idx[:, None] - idx[None, :]
    inner = np.where(diff >= 0, np.exp(np.maximum(diff, 0.0) * lg), 0.0).astype(f32)
    innerT = np.ascontiguousarray((inner * c).T.astype(f32))
    xi = (np.exp((idx + 1.0) * lg) * c).astype(f32)
    xi512 = np.ascontiguousarray(np.tile(xi[None, :], (128, 4)).astype(f32))
    zeta = np.exp((127.0 - idx) * lg).astype(f32).reshape(128, 1)
    gch = np.full((128, 1), np.exp(f32(128.0) * lg), f32)
    return innerT, xi512, zeta, gch


def _make_in_maps(inputs):
    f32 = np.float32
    x = np.asarray(inputs["x"], f32)[0]
    w_in = np.asarray(inputs["w_in"], f32)[0]
    b_f = np.asarray(inputs["b_f"], f32)[0]
    conv_w = np.asarray(inputs["conv_w"], f32)[0]
    conv_b = np.asarray(inputs["conv_b"], f32)[0]
    cosT, sinT, ident, utri, cmask, pm = _const_tables()
    xT = np.ascontiguousarray(x.T)
    c_col = np.ascontiguousarray(np.asarray(inputs["c"], f32)[0].reshape(8, 128).T)
    convw = np.ascontiguousarray(conv_w.T.reshape(2 * NCH, 128, 3).transpose(1, 0, 2).reshape(128, 2 * NCH * 3))
    convb = np.ascontiguousarray(conv_b.reshape(2 * NCH, 128).T)
    shared = {
        "xT": xT, "c_col": c_col,
        "w_ada": np.ascontiguousarray(np.asarray(inputs["w_ada"], f32)[0]),
        "b_ada": np.ascontiguousarray(np.asarray(inputs["b_ada"], f32)[0].reshape(1, -1)),
        "cosT": cosT, "sinT": sinT, "ident": ident, "utri": utri, "cmask": cmask, "pm": pm,
        "w_out": np.ascontiguousarray(np.asarray(inputs["w_out"], f32)[0]),
        "w_up": np.ascontiguousarray(np.asarray(inputs["w_up"], f32)[0]),
        "w_dn": np.ascontiguousarray(np.asarray(inputs["w_down"], f32)[0]),
        "convw": convw, "convb": convb,
        "ln1g": np.asarray(inputs["ln1_g"], f32).reshape(1, D), "ln1b": np.asarray(inputs["ln1_b"], f32).reshape(1, D),
        "ln2g": np.asarray(inputs["ln2_g"], f32).reshape(1, D), "ln2b": np.asarray(inputs["ln2_b"], f32).reshape(1, D),
    }
    FQ, FK, FV, FF, RQ, RK, RV, RG = 0, 512, 1024, 1536, 1544, 2056, 2568, 3080
    maps = []
    for j in range(NCORES):
        hr, half = j // 2, j % 2
        w1f = np.zeros((D, 512), f32)
        w1f[:, 0:64] = w_in[:, FQ + 64 * j:FQ + 64 * j + 64]
        w1f[:, 64] = w_in[:, FF + j]
        w1f[:, 96] = w_in[:, FF + j]
        w1f[:, 128:192] = w_in[:, FK + 64 * j:FK + 64 * j + 64]
        w1f[:, 256:384] = w_in[:, RQ + 128 * hr:RQ + 128 * hr + 128]
        w1f[:, 384:512] = w_in[:, RK + 128 * hr:RK + 128 * hr + 128]
        o = 128 * hr + 64 * half
        w1t = np.concatenate([w_in[:, FV + 64 * j:FV + 64 * j + 64], w_in[:, RV + o:RV + o + 64],
                              w_in[:, RG + o:RG + o + 64], w_in[:, FF + j:FF + j + 1]], axis=1)
        innerT, xi512, zeta, gch = _ret_tables(hr)
        xs = np.zeros((SL, D), f32)
        lo = j * SLAB - HALO
        if lo < 0:
            xs[HALO:] = x[0:SLAB]
        else:
            xs[:] = x[lo:lo + SL]
        m = dict(shared)
        m.update({
            "xs": xs, "w1f": w1f, "w1t": np.ascontiguousarray(w1t),
            "bfh": np.full((128, 1), b_f[j], f32),
            "innerT": innerT, "xi512": xi512, "zeta": zeta, "gch": gch,
            "hmask": np.full((128, 1), 0.0 if j == 0 else 1.0, f32),
        })
        maps.append(m)
    return maps


_NC_CACHE = {}


def kernel(**inputs):
    S_ = int(np.asarray(inputs["x"]).shape[1])
    if S_ != S:
        _cfg(S_)
    if S_ not in _NC_CACHE:
        _NC_CACHE[S_] = build_program()
    nc = _NC_CACHE[S_]
    in_maps = _make_in_maps(inputs)
    res = run_bass_kernel_spmd(nc, in_maps, core_ids=list(range(NCORES)))
    out = np.concatenate([np.asarray(res.results[j]["y"], np.float32) for j in range(NCORES)], axis=0)
    return out.reshape(1, S, D)
```
